# Optimizing a Trainium2 kernel written in Bass

```python
import jax
import jax.numpy as jnp
from jax import lax
import numpy as np

D_MODEL = 1024
BATCH = 16
SEQ = 256
DEPTH = 4
DEC_BATCH = 8
DEC_SEQ = 1024
PAST_LEN = 256

GRID_W = 64
N_MIXERS = 4
N_DN = (DEPTH + 3) // 4
N_NA = (DEPTH + 2) // 4
N_MLA = (DEPTH + 1) // 4
N_RW = DEPTH // 4
D_FF = -(-(8 * D_MODEL) // (3 * 256)) * 256
DN_HEADS = 8
DN_DIM = D_MODEL // DN_HEADS
DN_CONV = 3
DN_CHUNK = 64
NA_HEADS = 16
NA_DIM = D_MODEL // NA_HEADS
NA_WIN_ROWS = 8
NA_WIN_COLS = 16
NA_QCOLS = 16
NA_KCOLS = NA_QCOLS + NA_WIN_COLS
MLA_HEADS = 16
MLA_Q_LORA = 384
MLA_KV_LORA = 256
MLA_NOPE = 64
MLA_ROPE = 32
MLA_V = 64
RW_HEAD = 64
RW_HEADS = D_MODEL // RW_HEAD
RW_DECAY_LORA = 64
RW_A_LORA = 64
RW_GATE_LORA = 128
RW_LN_EPS = 64e-5
ROPE_THETA = 10000.0
Q_BLOCK = 128
NORM_EPS = 1e-6
NEG_INF = -1e30
F32 = jnp.float32

kernel_name = 'hybrid_diffusion_trunk_step'


def rmsnorm(x, g):
    xf = x.astype(F32)
    y = xf * lax.rsqrt(jnp.mean(xf * xf, -1, keepdims=True) + NORM_EPS)
    return (y * g.astype(F32)).astype(x.dtype)


def l2norm(x):
    xf = x.astype(F32)
    return (xf * lax.rsqrt(jnp.sum(xf * xf, -1, keepdims=True) + NORM_EPS)).astype(x.dtype)


def modulate(h, shift, scale):
    return h * (1 + scale) + shift


def swiglu(h, w_in, w_out):
    gate, up = jnp.split(h @ w_in, 2, -1)
    return (jax.nn.silu(gate) * up) @ w_out


def dwconv_centred(x, w):
    k = w.shape[0]
    return lax.conv_general_dilated(x, w[:, None, :], (1,), [(k // 2, k // 2)],
                                    dimension_numbers=('NWC', 'WIO', 'NWC'),
                                    feature_group_count=x.shape[-1])


def rope_1d(x, pos):
    half = x.shape[-1] // 2
    inv = ROPE_THETA ** (-jnp.arange(half, dtype=F32) / half)
    ang = pos.astype(F32)[:, None] * inv[None, :]
    cos, sin = jnp.cos(ang).astype(x.dtype), jnp.sin(ang).astype(x.dtype)
    x1, x2 = x[..., :half], x[..., half:]
    return jnp.concatenate([x1 * cos - x2 * sin, x1 * sin + x2 * cos], -1)


def rope_2d(x):
    t = jnp.arange(x.shape[-2])
    h = x.shape[-1] // 2
    return jnp.concatenate([rope_1d(x[..., :h], t // GRID_W), rope_1d(x[..., h:], t % GRID_W)], -1)


def merge_heads(x):
    b, h, t, d = x.shape
    return x.transpose(0, 2, 1, 3).reshape(b, t, h * d)


def block_attention(q, k, v, scale):
    b, h, tq, dk = q.shape
    nb = tq // Q_BLOCK
    qb = jnp.moveaxis(q.reshape(b, h, nb, Q_BLOCK, dk), 2, 0)

    def one(qi):
        s = jnp.einsum('bhqd,bhkd->bhqk', qi, k).astype(F32) * scale
        p = jax.nn.softmax(s, -1).astype(v.dtype)
        return jnp.einsum('bhqk,bhkd->bhqd', p, v)

    o = lax.map(one, qb)
    return jnp.moveaxis(o, 0, 2).reshape(b, h, tq, v.shape[-1])


def chunk_gated_delta(q, k, v, g, beta, s0):
    *lead, t, dk = q.shape
    dv = v.shape[-1]
    n = t // DN_CHUNK
    nl = len(lead)

    def chunks(x, trailing):
        return x.astype(F32).reshape(*lead, n, DN_CHUNK, *trailing)

    q, k, v = chunks(q, (dk,)), chunks(k, (dk,)), chunks(v, (dv,))
    g, beta = chunks(g, ()), chunks(beta, ())
    gc = jnp.cumsum(g, -1)
    idx = jnp.arange(DN_CHUNK)
    causal = idx[:, None] >= idx[None, :]
    strict = idx[:, None] > idx[None, :]
    decay = jnp.exp(jnp.where(causal, gc[..., :, None] - gc[..., None, :], -jnp.inf))
    kb = k * beta[..., None]
    a_mat = jnp.where(strict, jnp.einsum('...id,...jd->...ij', kb, k) * decay, 0.0)
    rhs = jnp.concatenate([v * beta[..., None], kb * jnp.exp(gc)[..., None]], -1)
    sol = lax.linalg.triangular_solve(a_mat + jnp.eye(DN_CHUNK, dtype=F32), rhs,
                                      left_side=True, lower=True, unit_diagonal=True)
    u, w = sol[..., :dv], sol[..., dv:]
    attn = jnp.where(causal, jnp.einsum('...id,...jd->...ij', q, k) * decay, 0.0)
    xs = [jnp.moveaxis(x, nl, 0) for x in (q, k, u, w, gc, attn)]

    def step(s, inp):
        qi, ki, ui, wi, gi, ai = inp
        v_new = ui - jnp.einsum('...cd,...de->...ce', wi, s)
        o = (jnp.einsum('...cd,...de->...ce', qi * jnp.exp(gi)[..., None], s)
             + jnp.einsum('...ij,...je->...ie', ai, v_new))
        g_last = gi[..., -1:]
        s = (s * jnp.exp(g_last)[..., None]
             + jnp.einsum('...cd,...ce->...de', ki * jnp.exp(g_last - gi)[..., None], v_new))
        return s, o

    s, o = lax.scan(step, s0.astype(F32), xs)
    return jnp.moveaxis(o, 0, nl).reshape(*lead, t, dv), s


def deltanet_mixer(h, s0, w_in, w_gate, a_log, dt_bias, conv_w, out_norm, w_out):
    b, t, _ = h.shape
    q, k, v, z = jnp.split(h @ w_in, 4, -1)
    qkv = jax.nn.silu(dwconv_centred(jnp.concatenate([q, k, v], -1), conv_w))
    q, k, v = [x.reshape(b, t, DN_HEADS, DN_DIM) for x in jnp.split(qkv, 3, -1)]
    q = l2norm(q) * DN_DIM ** -0.5
    k = l2norm(k)
    gates = (h @ w_gate).astype(F32).reshape(b, t, 2, 2, DN_HEADS)
    g = -jnp.exp(a_log.astype(F32)) * jax.nn.softplus(gates[:, :, :, 0] + dt_bias.astype(F32))
    beta = jax.nn.sigmoid(gates[:, :, :, 1])

    def both(x):
        return jnp.stack([x, x[:, ::-1]], 1).transpose(0, 1, 3, 2, 4)

    def per_dir(x):
        return jnp.stack([x[:, :, 0], x[:, ::-1, 1]], 1).transpose(0, 1, 3, 2)

    o, s = chunk_gated_delta(both(q), both(k), both(v), per_dir(g), per_dir(beta), s0)
    o = (o[:, 0] + o[:, 1, :, ::-1]).transpose(0, 2, 1, 3).astype(h.dtype)
    o = rmsnorm(o, out_norm) * jax.nn.silu(z.reshape(b, t, DN_HEADS, DN_DIM))
    return o.reshape(b, t, -1) @ w_out, s.astype(h.dtype)


def na_project(h, w_qkv, q_norm, k_norm):
    b, t, _ = h.shape
    qkv = (h @ w_qkv).reshape(b, t, 3, NA_HEADS, NA_DIM).transpose(2, 0, 3, 1, 4)
    return rmsnorm(qkv[0], q_norm), rmsnorm(qkv[1], k_norm), qkv[2]


def neighbourhood_attention(q, k, v, kc, vc, rel_bias):
    b, h, t, d = q.shape
    rows = t // GRID_W
    wr = min(NA_WIN_ROWS, rows)
    ncb = GRID_W // NA_QCOLS
    qg = q.reshape(b, h, rows, ncb, NA_QCOLS, d)
    kg = k.reshape(b, h, rows, GRID_W, d)
    vg = v.reshape(b, h, rows, GRID_W, d)
    row_start = jnp.clip(jnp.arange(rows) - wr // 2, 0, rows - wr)
    qcol = jnp.arange(GRID_W).reshape(ncb, NA_QCOLS)
    win_start = jnp.clip(qcol - NA_WIN_COLS // 2, 0, GRID_W - NA_WIN_COLS)
    blk_start = jnp.clip(qcol[:, 0] - NA_WIN_COLS // 2, 0, GRID_W - NA_KCOLS)
    kcol = blk_start[:, None] + jnp.arange(NA_KCOLS)
    col_ok = ((kcol[:, None, :] >= win_start[:, :, None])
              & (kcol[:, None, :] < win_start[:, :, None] + NA_WIN_COLS))
    col_idx = jnp.clip(kcol[:, None, :] - qcol[:, :, None] + NA_WIN_COLS - 1, 0, 2 * NA_WIN_COLS - 2)
    scale = d ** -0.5
    n_loc = wr * NA_KCOLS

    def row_block(r):
        rs = row_start[r]
        k_blk = lax.dynamic_slice_in_dim(kg, rs, wr, axis=2)[:, :, :, kcol]
        v_blk = lax.dynamic_slice_in_dim(vg, rs, wr, axis=2)[:, :, :, kcol]
        q_r = lax.dynamic_index_in_dim(qg, r, axis=2, keepdims=False)
        row_idx = rs + jnp.arange(wr) - r + NA_WIN_ROWS - 1
        bias = rel_bias[:, row_idx[None, None, :, None], col_idx[:, :, None, :]].astype(F32)
        s_loc = jnp.einsum('bhnqd,bhankd->bhnqak', q_r, k_blk).astype(F32) * scale + bias
        s_loc = jnp.where(col_ok[:, :, None, :], s_loc, NEG_INF)
        s_ctx = jnp.einsum('bhnqd,bhld->bhnql', q_r, kc).astype(F32) * scale
        s = jnp.concatenate([s_loc.reshape(b, h, ncb, NA_QCOLS, n_loc), s_ctx], -1)
        p = jax.nn.softmax(s, -1).astype(v.dtype)
        p_loc = p[..., :n_loc].reshape(b, h, ncb, NA_QCOLS, wr, NA_KCOLS)
        o = (jnp.einsum('bhnqak,bhankd->bhnqd', p_loc, v_blk)
             + jnp.einsum('bhnql,bhld->bhnqd', p[..., n_loc:], vc))
        return o.reshape(b, h, GRID_W, d)

    o = lax.map(row_block, jnp.arange(rows))
    return o.transpose(1, 2, 0, 3, 4).reshape(b, h, t, d)


def mla_project(h, w_a, q_a_norm, kv_a_norm, w_q_b, q_norm):
    b, t, _ = h.shape
    cq, ckv, kpe = jnp.split(h @ w_a, [MLA_Q_LORA, MLA_Q_LORA + MLA_KV_LORA], -1)
    q = (rmsnorm(cq, q_a_norm) @ w_q_b).reshape(b, t, MLA_HEADS, MLA_NOPE + MLA_ROPE)
    return rmsnorm(q, q_norm).transpose(0, 2, 1, 3), rmsnorm(ckv, kv_a_norm), kpe


def mla_keys(ckv, kpe, w_kv_b, k_norm):
    b, l, _ = ckv.shape
    kv = (ckv @ w_kv_b).reshape(b, l, MLA_HEADS, MLA_NOPE + MLA_V)
    k = jnp.concatenate([kv[..., :MLA_NOPE],
                         jnp.broadcast_to(kpe[:, :, None, :], (b, l, MLA_HEADS, MLA_ROPE))], -1)
    return rmsnorm(k, k_norm).transpose(0, 2, 1, 3), kv[..., MLA_NOPE:].transpose(0, 2, 1, 3)


def rope_tail(x):
    return jnp.concatenate([x[..., :MLA_NOPE], rope_2d(x[..., MLA_NOPE:])], -1)


def rwkv_step(s, inp):
    r, w, k, v, a, bb = inp
    sa = jnp.einsum('...ij,...j->...i', s, a)
    s = s * w[..., None, :] + sa[..., :, None] * bb[..., None, :] + v[..., :, None] * k[..., None, :]
    return s, jnp.einsum('...ij,...j->...i', s, r)


def rwkv_mixer(h, s0, mu, w_rkv, w0, w1, w2, a0, a1, a2, g1, g2, k_k, k_a, r_k, ln_g, ln_b, w_out):
    b, t, d = h.shape
    pad = jnp.zeros_like(h[:, :1])
    xx = 0.5 * (jnp.concatenate([pad, h[:, :-1]], 1) + jnp.concatenate([h[:, 1:], pad], 1)) - h
    xm = h[None] + xx[None] * mu[:, None, None, :]
    r, k, v = jnp.einsum('sbtd,sde->sbte', xm[:3], w_rkv)

    def lora(x, wa, wb, f):
        return jnp.einsum('zbtr,zrd->zbtd', f(jnp.einsum('btd,zdr->zbtr', x, wa)), wb)

    w = -jax.nn.softplus(-(w0[:, None, None, :] + lora(xm[3], w1, w2, jnp.tanh))) - 0.5
    decay = jnp.exp(-jnp.exp(w.astype(F32)))
    a = jax.nn.sigmoid(a0[:, None, None, :] + lora(xm[4], a1, a2, lambda x: x))
    g = jax.nn.sigmoid(xm[5] @ g1) @ g2

    def heads(x):
        return x.reshape(*x.shape[:-1], RW_HEADS, RW_HEAD)

    kk = l2norm(heads(k * k_k))
    a_h = heads(a)
    k_dir = heads(k)[None] * (1 + (a_h - 1) * heads(k_a))
    r_h, v_h = heads(r), heads(v)

    def tm(x2):
        return jnp.stack([x2[0], x2[1][:, ::-1]], 1).transpose(2, 0, 1, 3, 4).astype(F32)

    def both(x):
        return tm(jnp.stack([x, x], 0))

    s, y = lax.scan(rwkv_step, s0.astype(F32),
                    (both(r_h), tm(heads(decay)), tm(k_dir), both(v_h), both(-kk), tm(kk[None] * a_h)))
    y = (y[:, :, 0] + y[::-1, :, 1]).transpose(1, 0, 2, 3)
    mean = jnp.mean(y, -1, keepdims=True)
    var = jnp.mean(jnp.square(y - mean), -1, keepdims=True)
    y = ((y - mean) * lax.rsqrt(var + RW_LN_EPS)).reshape(b, t, d) * ln_g.astype(F32) + ln_b.astype(F32)
    bonus = jnp.sum(r_h[None] * k_dir * r_k, -1, keepdims=True) * v_h[None]
    y = y.astype(h.dtype) + jnp.sum(bonus, 0).reshape(b, t, d)
    return (y * g) @ w_out, s.astype(h.dtype)


def setup_inputs(seed: int = 0) -> dict:
    key = jax.random.key(seed)
    keys = iter(jax.random.split(key, 64))

    def nrm(shape, scale=1.0):
        return jax.random.normal(next(keys), shape, F32) * scale

    def gain(shape):
        return 1.0 + nrm(shape, 0.05)

    def unif(shape, lo, hi):
        return jax.random.uniform(next(keys), shape, F32, lo, hi)

    D = D_MODEL
    d_dn = DN_HEADS * DN_DIM
    d_na = NA_HEADS * NA_DIM
    d_qk = MLA_NOPE + MLA_ROPE
    dt = jnp.exp(unif((N_DN, 2, DN_HEADS), float(np.log(1e-3)), float(np.log(1e-1))))
    return {
        'x_prompt': nrm((BATCH, SEQ, D)),
        'x_sample': nrm((DEC_BATCH, DEC_SEQ, D)),
        'state_dn': nrm((DEC_BATCH, N_DN, 2, DN_HEADS, DN_DIM, DN_DIM), 0.5),
        'cache_na_k': nrm((DEC_BATCH, N_NA, NA_HEADS, PAST_LEN, NA_DIM)),
        'cache_na_v': nrm((DEC_BATCH, N_NA, NA_HEADS, PAST_LEN, NA_DIM)),
        'cache_mla_ckv': nrm((DEC_BATCH, N_MLA, PAST_LEN, MLA_KV_LORA)),
        'cache_mla_kpe': nrm((DEC_BATCH, N_MLA, PAST_LEN, MLA_ROPE)),
        'state_rwkv': nrm((DEC_BATCH, N_RW, 2, RW_HEADS, RW_HEAD, RW_HEAD)),
        'c': nrm((DEC_BATCH, D)),
        'c_ctx': nrm((D,)),
        'ada_w': nrm((DEPTH, D, 6 * D), 0.5 * D ** -0.5),
        'ada_b': nrm((DEPTH, 6 * D), 0.01),
        'norm_mix': gain((DEPTH, D)),
        'norm_ffn': gain((DEPTH, D)),
        'ffn_w_in': nrm((DEPTH, D, 2 * D_FF), D ** -0.5),
        'ffn_w_out': nrm((DEPTH, D_FF, D), D_FF ** -0.5),
        'dn_w_in': nrm((N_DN, D, 4 * d_dn), D ** -0.5),
        'dn_w_gate': nrm((N_DN, D, 4 * DN_HEADS), 0.2 * D ** -0.5),
        'dn_a_log': jnp.log(unif((N_DN, 2, DN_HEADS), 1.0, 16.0)),
        'dn_dt_bias': jnp.log(jnp.expm1(dt)),
        'dn_conv': nrm((N_DN, DN_CONV, 3 * d_dn), DN_CONV ** -0.5),
        'dn_out_norm': gain((N_DN, DN_DIM)),
        'dn_w_out': nrm((N_DN, d_dn, D), d_dn ** -0.5),
        'na_w_qkv': nrm((N_NA, D, 3 * d_na), D ** -0.5),
        'na_q_norm': gain((N_NA, NA_DIM)),
        'na_k_norm': gain((N_NA, NA_DIM)),
        'na_rel_bias': nrm((N_NA, NA_HEADS, 2 * NA_WIN_ROWS - 1, 2 * NA_WIN_COLS - 1), 0.1),
        'na_w_out': nrm((N_NA, d_na, D), d_na ** -0.5),
        'mla_w_a': nrm((N_MLA, D, MLA_Q_LORA + MLA_KV_LORA + MLA_ROPE), D ** -0.5),
        'mla_q_a_norm': gain((N_MLA, MLA_Q_LORA)),
        'mla_kv_a_norm': gain((N_MLA, MLA_KV_LORA)),
        'mla_w_q_b': nrm((N_MLA, MLA_Q_LORA, MLA_HEADS * d_qk), MLA_Q_LORA ** -0.5),
        'mla_w_kv_b': nrm((N_MLA, MLA_KV_LORA, MLA_HEADS * (MLA_NOPE + MLA_V)), MLA_KV_LORA ** -0.5),
        'mla_q_norm': gain((N_MLA, d_qk)),
        'mla_k_norm': gain((N_MLA, d_qk)),
        'mla_w_out': nrm((N_MLA, MLA_HEADS * MLA_V, D), (MLA_HEADS * MLA_V) ** -0.5),
        'rw_mu': unif((N_RW, 6, D), 0.0, 1.0),
        'rw_w_rkv': nrm((N_RW, 3, D, D), D ** -0.5),
        'rw_w0': nrm((N_RW, 2, D), 0.5),
        'rw_w1': nrm((N_RW, 2, D, RW_DECAY_LORA), D ** -0.5),
        'rw_w2': nrm((N_RW, 2, RW_DECAY_LORA, D), 0.1 * RW_DECAY_LORA ** -0.5),
        'rw_a0': nrm((N_RW, 2, D), 0.1),
        'rw_a1': nrm((N_RW, 2, D, RW_A_LORA), D ** -0.5),
        'rw_a2': nrm((N_RW, 2, RW_A_LORA, D), 0.5 * RW_A_LORA ** -0.5),
        'rw_g1': nrm((N_RW, D, RW_GATE_LORA), D ** -0.5),
        'rw_g2': nrm((N_RW, RW_GATE_LORA, D), RW_GATE_LORA ** -0.5),
        'rw_k_k': 0.85 + nrm((N_RW, D), 0.05),
        'rw_k_a': gain((N_RW, D)),
        'rw_r_k': nrm((N_RW, RW_HEADS, RW_HEAD), 0.1),
        'rw_ln_g': gain((N_RW, D)),
        'rw_ln_b': nrm((N_RW, D), 0.01),
        'rw_w_out': nrm((N_RW, D, D), D ** -0.5),
    }


def reference(x_prompt, x_sample, state_dn, cache_na_k, cache_na_v, cache_mla_ckv, cache_mla_kpe, state_rwkv,
              c, c_ctx, ada_w, ada_b, norm_mix, norm_ffn, ffn_w_in, ffn_w_out,
              dn_w_in, dn_w_gate, dn_a_log, dn_dt_bias, dn_conv, dn_out_norm, dn_w_out,
              na_w_qkv, na_q_norm, na_k_norm, na_rel_bias, na_w_out,
              mla_w_a, mla_q_a_norm, mla_kv_a_norm, mla_w_q_b, mla_w_kv_b, mla_q_norm, mla_k_norm, mla_w_out,
              rw_mu, rw_w_rkv, rw_w0, rw_w1, rw_w2, rw_a0, rw_a1, rw_a2, rw_g1, rw_g2,
              rw_k_k, rw_k_a, rw_r_k, rw_ln_g, rw_ln_b, rw_w_out):
    yp, ys = x_prompt, x_sample
    s_ctx = jax.nn.silu(c_ctx)
    s_lat = jax.nn.silu(c)[:, None, :]
    mla_scale = (MLA_NOPE + MLA_ROPE) ** -0.5
    out_dn, out_na_k, out_na_v, out_ckv, out_kpe, out_rw = [], [], [], [], [], []
    for l in range(DEPTH):
        kind, j = l % N_MIXERS, l // N_MIXERS
        mp = jnp.split(s_ctx @ ada_w[l] + ada_b[l], 6, -1)
        ms = jnp.split(s_lat @ ada_w[l] + ada_b[l], 6, -1)
        hp = modulate(rmsnorm(yp, norm_mix[l]), mp[0], mp[1])
        hs = modulate(rmsnorm(ys, norm_mix[l]), ms[0], ms[1])
        if kind == 0:
            prm = (dn_w_in[j], dn_w_gate[j], dn_a_log[j], dn_dt_bias[j], dn_conv[j], dn_out_norm[j], dn_w_out[j])
            s0 = jnp.zeros((yp.shape[0], 2, DN_HEADS, DN_DIM, DN_DIM), F32)
            op, st = deltanet_mixer(hp, s0, *prm)
            os_, _ = deltanet_mixer(hs, state_dn[:, j], *prm)
            out_dn.append(st)
        elif kind == 1:
            q, k, v = na_project(hp, na_w_qkv[j], na_q_norm[j], na_k_norm[j])
            op = merge_heads(block_attention(q, k, v, NA_DIM ** -0.5)) @ na_w_out[j]
            out_na_k.append(k)
            out_na_v.append(v)
            q, k, v = na_project(hs, na_w_qkv[j], na_q_norm[j], na_k_norm[j])
            o = neighbourhood_attention(q, k, v, cache_na_k[:, j], cache_na_v[:, j], na_rel_bias[j])
            os_ = merge_heads(o) @ na_w_out[j]
        elif kind == 2:
            prm = (mla_w_a[j], mla_q_a_norm[j], mla_kv_a_norm[j], mla_w_q_b[j], mla_q_norm[j])
            q, ckv, kpe = mla_project(hp, *prm)
            k, v = mla_keys(ckv, kpe, mla_w_kv_b[j], mla_k_norm[j])
            op = merge_heads(block_attention(q, k, v, mla_scale)) @ mla_w_out[j]
            out_ckv.append(ckv)
            out_kpe.append(kpe)
            q, ckv, kpe = mla_project(hs, *prm)
            k, v = mla_keys(ckv, kpe, mla_w_kv_b[j], mla_k_norm[j])
            kc, vc = mla_keys(cache_mla_ckv[:, j], cache_mla_kpe[:, j], mla_w_kv_b[j], mla_k_norm[j])
            o = block_attention(rope_tail(q), jnp.concatenate([kc, rope_tail(k)], 2),
                                jnp.concatenate([vc, v], 2), mla_scale)
            os_ = merge_heads(o) @ mla_w_out[j]
        else:
            prm = (rw_mu[j], rw_w_rkv[j], rw_w0[j], rw_w1[j], rw_w2[j], rw_a0[j], rw_a1[j], rw_a2[j],
                   rw_g1[j], rw_g2[j], rw_k_k[j], rw_k_a[j], rw_r_k[j], rw_ln_g[j], rw_ln_b[j], rw_w_out[j])
            s0 = jnp.zeros((yp.shape[0], 2, RW_HEADS, RW_HEAD, RW_HEAD), F32)
            op, st = rwkv_mixer(hp, s0, *prm)
            os_, _ = rwkv_mixer(hs, state_rwkv[:, j], *prm)
            out_rw.append(st)
        yp = yp + mp[2] * op
        ys = ys + ms[2] * os_
        yp = yp + mp[5] * swiglu(modulate(rmsnorm(yp, norm_ffn[l]), mp[3], mp[4]), ffn_w_in[l], ffn_w_out[l])
        ys = ys + ms[5] * swiglu(modulate(rmsnorm(ys, norm_ffn[l]), ms[3], ms[4]), ffn_w_in[l], ffn_w_out[l])
    return (yp, ys, jnp.stack(out_dn, 1), jnp.stack(out_na_k, 1), jnp.stack(out_na_v, 1),
            jnp.stack(out_ckv, 1), jnp.stack(out_kpe, 1), jnp.stack(out_rw, 1))
```

```python
import numpy as np
import concourse.bass as bass
import concourse.mybir as mybir
from concourse.bass_utils import run_bass_kernel_spmd
from contextlib import ExitStack

F32 = mybir.dt.float32
BF16 = mybir.dt.bfloat16
I32 = mybir.dt.int32
AF = mybir.ActivationFunctionType
ALU = mybir.AluOpType
AX = mybir.AxisListType


class V:
    __slots__ = ("ap", "keys")

    def __init__(self, ap, keys):
        self.ap = ap
        self.keys = keys

    def __getitem__(self, idx):
        return V(self.ap[idx], self.keys)

    def re(self, pat, **kw):
        return V(self.ap.rearrange(pat, **kw), self.keys)

    def bc(self, shape):
        return V(self.ap.broadcast_to(shape), self.keys)

    def bitcast(self, dt):
        return V(self.ap.bitcast(dt), self.keys)

    def all(self):
        return self

    @property
    def shape(self):
        return self.ap.shape


class T:
    def __init__(self, name, handle, shape, split=None):
        self.name = name
        self.h = handle
        self.shape = tuple(shape)
        self.split = split
        if split is None:
            self.allkeys = frozenset([(name, 0)])
        else:
            ax, blk = split
            self.allkeys = frozenset((name, i) for i in range((shape[ax] + blk - 1) // blk))

    def __getitem__(self, idx):
        if not isinstance(idx, tuple):
            idx = (idx,)
        keys = self.allkeys
        if self.split is not None:
            ax, blk = self.split
            if ax < len(idx):
                ix = idx[ax]
                if isinstance(ix, int):
                    keys = frozenset([(self.name, ix // blk)])
                elif isinstance(ix, slice):
                    st = 0 if ix.start is None else ix.start
                    sp = self.shape[ax] if ix.stop is None else ix.stop
                    keys = frozenset((self.name, i) for i in range(st // blk, (sp - 1) // blk + 1))
        return V(self.h[idx], keys)

    def all(self):
        return V(self.h[tuple(slice(None) for _ in self.shape)], self.allkeys)


class Op:
    __slots__ = ("eng", "fn", "deps", "ddeps", "signal", "count", "dma", "idx", "cost", "alldeps")


class Prog:
    ENG = ("pe", "act", "dve", "pool", "sp")

    def __init__(self, nc):
        self.nc = nc
        self.es = ExitStack()
        self.streams = {e: [] for e in self.ENG}
        self.res = {}
        self.dsem_count = {}
        self.dsem_waited = {}
        self.psum_names = set()
        self.bank_readers = {}
        self.in_names = []
        self.ops = []
        self.last_dma = {}
        self.last_dma_sem = {}
        self.n_t = 0

    def sb(self, name, shape, dtype, split=None):
        h = self.es.enter_context(self.nc.sbuf_tensor(name, list(shape), dtype))
        return T(name, h, shape, split)

    def ps(self, name, shape, dtype, split=None):
        h = self.es.enter_context(self.nc.psum_tensor(name, list(shape), dtype))
        self.psum_names.add(name)
        return T(name, h, shape, split)

    def dram(self, name, shape, dtype, kind, split=None):
        h = self.nc.dram_tensor(name, list(shape), dtype, kind=kind).ap()
        if kind == "ExternalInput":
            self.in_names.append(name)
        return T(name, h, shape, split)

    def _st(self, k):
        s = self.res.get(k)
        if s is None:
            s = self.res[k] = [None, []]
        return s

    def _record(self, eng, fn, reads, writes, dma=None, cost=0.3):
        op = Op()
        op.eng, op.fn, op.signal, op.count, op.dma = eng, fn, False, 0, dma
        op.cost = cost
        deps = {}
        ddeps = {}
        alld = {}

        def add(ev):
            if ev is None:
                return
            alld[id(ev)] = ev
            if ev.dma is not None:
                s = ev.dma
                ld_ = self.last_dma_sem[s]
                alld[id(ld_)] = ld_
                v = self.dsem_count[s]
                if ddeps.get(s, 0) < v:
                    ddeps[s] = v
                if self.dsem_waited.get(s, 0) < v:
                    self.dsem_waited[s] = v
            else:
                if ev.eng == "pe" and eng == "pe" and dma is None:
                    return
                deps[id(ev)] = ev

        rk = set()
        for v in reads:
            rk |= v.keys
        wk = set()
        for v in writes:
            wk |= v.keys
        for k in rk:
            add(self._st(k)[0])
        for k in wk:
            s = self._st(k)
            add(s[0])
            for r in s[1]:
                add(r)
        for k in rk:
            if k[0] in self.psum_names:
                r = self.bank_readers.get(k[0])
                if r is not None:
                    if r.eng != eng:
                        add(r)
                    else:
                        alld[id(r)] = r
        op.deps = list(deps.values())
        op.ddeps = ddeps
        for d in op.deps:
            d.signal = True
        for k in rk:
            if k[0] in self.psum_names and dma is None:
                self.bank_readers[k[0]] = op
        if dma is not None:
            pw = self.dsem_waited.get(dma, 0)
            if pw > ddeps.get(dma, 0):
                ddeps[dma] = pw
            c = self.dsem_count.get(dma, 0) + 16
            self.dsem_count[dma] = c
            pq_ = self.last_dma.get(eng)
            if pq_ is not None:
                alld[id(pq_)] = pq_
            self.last_dma[eng] = op
            self.last_dma_sem[dma] = op
        op.alldeps = list(alld.values())
        for k in rk:
            if k not in wk:
                self._st(k)[1].append(op)
        for k in wk:
            s = self._st(k)
            s[0] = op
            s[1] = []
        op.idx = len(self.ops)
        self.ops.append(op)
        return op

    def op(self, eng, name, *, out=None, accum_out=None, extra_reads=(), extra_writes=(), cost_=None, **kw):
        reads = list(extra_reads)
        writes = list(extra_writes)
        args = {}
        for k, v in kw.items():
            if isinstance(v, V):
                reads.append(v)
                args[k] = v.ap
            else:
                args[k] = v
        if out is not None:
            writes.append(out)
            args["out"] = out.ap
        if accum_out is not None:
            writes.append(accum_out)
            args["accum_out"] = accum_out.ap

        def fn(e):
            return getattr(e, name)(**args)

        if cost_ is None:
            ref = out if out is not None else reads[0]
            n = int(np.prod(ref.ap.shape[1:]))
            if eng == "pe":
                cost_ = 0.3
            else:
                cost_ = 0.2 + n / 960.0
        return self._record(eng, fn, reads, writes, cost=cost_)

    def dma(self, q, out, in_, sem, **kw):
        def fn(e):
            return e.dma_start(out=out.ap, in_=in_.ap, **kw)

        nb = int(np.prod(out.ap.shape)) * 4
        return self._record(q, fn, [in_], [out], dma=sem + "_" + q, cost=nb / 150e3)

    def barrier(self):
        self.ops.append(("BARRIER", dict(self.dsem_count)))

    def mode(self, reorder):
        self.ops.append(("MODE", reorder))

    def schedule(self, reorder=True):
        import heapq
        LAT = 0.25
        streams = {e: [] for e in self.ENG}
        seg = []
        reorder0 = reorder

        def flush(seg):
            if not seg:
                return
            if not reorder:
                for o in seg:
                    streams[o.eng].append(o)
                return
            inseg = {id(o) for o in seg}
            indeg = {}
            succ = {}
            ready_t = {}
            for o in seg:
                n = 0
                for d in o.alldeps:
                    if id(d) in inseg:
                        n += 1
                        succ.setdefault(id(d), []).append(o)
                indeg[id(o)] = n
                ready_t[id(o)] = 0.0
            cp = {}
            for o in reversed(seg):
                m = 0.0
                for s_ in succ.get(id(o), ()):
                    v = cp[id(s_)]
                    if v > m:
                        m = v
                cp[id(o)] = m + o.cost + LAT
            future = {e: [] for e in self.ENG}
            avail = {e: [] for e in self.ENG}
            for o in seg:
                if indeg[id(o)] == 0:
                    heapq.heappush(avail[o.eng], (-cp[id(o)], o.idx, o))
            free = {e: 0.0 for e in self.ENG}
            left = len(seg)
            while left:
                best = None
                for e in self.ENG:
                    fu, av = future[e], avail[e]
                    while fu and fu[0][0] <= free[e]:
                        _, i_, o_ = heapq.heappop(fu)
                        heapq.heappush(av, (-cp[id(o_)], i_, o_))
                    if av:
                        cand = (free[e], av[0][0], e, 0)
                    elif fu:
                        cand = (fu[0][0], -cp[id(fu[0][2])], e, 1)
                    else:
                        continue
                    if best is None or cand[:2] < best[:2]:
                        best = cand
                st_, _, e, src = best
                if src == 0:
                    _, _, o = heapq.heappop(avail[e])
                else:
                    _, _, o = heapq.heappop(future[e])
                left -= 1
                streams[e].append(o)
                if o.dma is not None:
                    free[e] = st_ + 0.1
                    fin = st_ + 2.0 + o.cost
                else:
                    fin = st_ + o.cost
                    free[e] = fin
                for s_ in succ.get(id(o), ()):
                    k = id(s_)
                    if ready_t[k] < fin + LAT:
                        ready_t[k] = fin + LAT
                    indeg[k] -= 1
                    if indeg[k] == 0:
                        heapq.heappush(future[s_.eng], (ready_t[k], s_.idx, s_))

        for it in self.ops:
            if isinstance(it, tuple) and it[0] == "MODE":
                flush(seg)
                seg = []
                reorder = it[1] and reorder0
                continue
            if isinstance(it, tuple):
                flush(seg)
                seg = []
                lasts = []
                for e in self.ENG:
                    for o in reversed(streams[e]):
                        if o.dma is None and o.fn is not None:
                            lasts.append(o)
                            break
                for e in self.ENG:
                    b_ = Op()
                    b_.eng, b_.fn, b_.signal, b_.count, b_.dma = e, None, False, 0, None
                    b_.deps = [x for x in lasts if x.eng != e]
                    b_.ddeps = dict(it[1])
                    for d in b_.deps:
                        d.signal = True
                    streams[e].append(b_)
            else:
                seg.append(it)
        flush(seg)
        self.streams = streams

    def emit(self, final_wait=True, reorder=True):
        self.schedule(reorder)
        nc = self.nc
        es = self.es
        esem = {e: es.enter_context(nc.semaphore("sem_" + e)) for e in self.ENG}
        dsem = {n: es.enter_context(nc.semaphore("d_" + n)) for n in self.dsem_count}
        for e in self.ENG:
            c = 0
            for op in self.streams[e]:
                if op.signal and op.dma is None:
                    c += 1
                    op.count = c
        block = es.enter_context(nc.Block())
        streams = self.streams
        dcount = self.dsem_count

        def run(ename):
            def body(eng):
                waited = {}
                for op in streams[ename]:
                    w = {}
                    for d in op.deps:
                        s = esem[d.eng]
                        if w.get(s, (0,))[0] < d.count:
                            w[s] = (d.count, s)
                    for sn, v in op.ddeps.items():
                        s = dsem[sn]
                        if w.get(s, (0,))[0] < v:
                            w[s] = (v, s)
                    for s, (v, _) in w.items():
                        if waited.get(s, 0) < v:
                            eng.wait_ge(s, v)
                            waited[s] = v
                    if op.fn is None:
                        continue
                    ins = op.fn(eng)
                    if op.dma is not None:
                        ins.then_inc(dsem[op.dma], 16)
                    elif op.signal:
                        ins.then_inc(esem[ename], 1)
                if ename == "sp" and final_wait:
                    for sn, c in dcount.items():
                        eng.wait_ge(dsem[sn], c)
            return body

        block.tensor(run("pe"))
        block.scalar(run("act"))
        block.vector(run("dve"))
        block.gpsimd(run("pool"))
        block.sync(run("sp"))
        es.close()

    def mm(self, out, lhsT, rhs, start=True, stop=True, **kw):
        n = int(np.prod(rhs.ap.shape[1:]))
        c = 0.04 + n / 2400.0 * (4.0 if rhs.ap.dtype == F32 else 1.0)
        return self.op("pe", "matmul", out=out, lhsT=lhsT, rhs=rhs, start=start, stop=stop, cost_=c, **kw)

    def tr(self, out, in_, ident):
        return self.op("pe", "transpose", out=out, in_=in_, identity=ident)

    def act(self, out, in_, func, eng="act", **kw):
        return self.op(eng, "activation", out=out, in_=in_, func=func, **kw)

    def tt(self, eng, out, in0, in1, op):
        return self.op(eng, "tensor_tensor", out=out, in0=in0, in1=in1, op=op)

    def ts(self, eng, out, in0, s1, op0, s2=None, op1=None, **kw):
        if op1 is None:
            return self.op(eng, "tensor_scalar", out=out, in0=in0, scalar1=s1, scalar2=s2, op0=op0, **kw)
        return self.op(eng, "tensor_scalar", out=out, in0=in0, scalar1=s1, scalar2=s2, op0=op0, op1=op1, **kw)

    def stt(self, eng, out, in0, scalar, in1, op0, op1, **kw):
        return self.op(eng, "scalar_tensor_tensor", out=out, in0=in0, scalar=scalar, in1=in1, op0=op0, op1=op1, **kw)

    def copy(self, eng, out, in_):
        if eng == "act":
            return self.op("act", "copy", out=out, in_=in_)
        return self.op(eng, "tensor_copy", out=out, in_=in_)

    def memset(self, eng, out, val):
        def fn(e):
            return e.memset(out.ap, val)
        return self._record(eng, fn, [], [out])


D = 1024
NT = 1536
NG = 3
DFF = 2816
NF = 22
DEPTH = 4
ARENA = 72 * 1024
REORDER = True


class Ctx:
    pass


def build(layers=(0, 1, 2, 3), mixers=True):
    nc = bass.Bass("TRN2", target_bir_lowering=False)
    P = Prog(nc)
    g = Ctx()
    g.P = P
    g.nc = nc
    g.x = P.dram("x", [NT, D], F32, "ExternalInput")
    g.y = P.dram("y", [NT, D], F32, "ExternalOutput")
    g.cv = P.dram("cv", [128, 8, 2], F32, "ExternalInput")
    g.ada_w = P.dram("ada_w", [DEPTH, D, 6 * D], F32, "ExternalInput")
    g.adabT = P.dram("adabT", [128, DEPTH, 48], F32, "ExternalInput")
    g.nrmT = P.dram("nrmT", [128, 2, DEPTH, 8], F32, "ExternalInput")
    g.ffn_w_in = P.dram("ffn_w_in", [DEPTH, D, 2 * DFF], F32, "ExternalInput")
    g.ffn_w_out = P.dram("ffn_w_out", [DEPTH, DFF, D], F32, "ExternalInput")
    g.identd = P.dram("ident", [128, 128], F32, "ExternalInput")
    g.xT = P.sb("xT", [128, 8, NT], F32, split=(2, 512))
    g.hT = P.sb("hT", [128, 8, NT], BF16, split=(2, 512))
    g.ident = P.sb("identS", [128, 128], F32)
    g.identb = P.sb("identB", [128, 128], BF16)
    g.onesb = P.sb("onesb", [128, 128], BF16)
    g.sc = P.sb("sc", [128, 8, 2], BF16)
    g.cvs = P.sb("cvs", [128, 8, 2], F32)
    g.adab = P.sb("adab", [128, DEPTH, 48], F32)
    g.nrm = P.sb("nrm", [128, 2, DEPTH, 8], F32)
    g.mod = [P.sb("mod%d" % l, [128, 48, 2], F32) for l in range(DEPTH)]
    g.gs = [P.sb("gs%d" % l, [128, 2, 8, 2], F32) for l in range(DEPTH)]
    g.arena = P.sb("arena", [128, ARENA // 4], F32)
    g.psb = [P.ps("ps%d" % i, [128, 512], F32) for i in range(8)]
    g.pq = 0
    g.pi = 0
    g.psrot = list(range(8))
    g.nsq = g.nrc = g.nE = 0
    g.ones2 = P.sb("ones2", [128, 128], BF16)
    g.onesb1 = P.sb("onesb1", [128, 128], BF16)
    g.Ebuf = [P.sb("Ebuf%d" % i, [128, 512], BF16) for i in range(3)]
    if 1 in layers and mixers:
        g.na_w_qkv = P.dram("na_w_qkv", [D, 3 * D], F32, "ExternalInput")
        g.na_w_out = P.dram("na_w_out", [D, D], F32, "ExternalInput")
        g.na_gn_d = P.dram("na_gn", [128, 2], F32, "ExternalInput")
        g.na_cm_d = P.dram("na_cm", [128, 12, 512], F32, "ExternalInput")
        g.na_tz_d = P.dram("na_tz", [16, 128, 31 * 64], F32, "ExternalInput")
        g.na_kc_d = P.dram("na_kc", [16, 256, 64], F32, "ExternalInput")
        g.na_vc_d = P.dram("na_vc", [16, 256, 64], F32, "ExternalInput")
        g.na_ko = P.dram("na_ko", [2, 16, 256, 64], F32, "ExternalOutput")
        g.na_vo = P.dram("na_vo", [2, 16, 256, 64], F32, "ExternalOutput")
    if 0 in layers and mixers:
        g.dn_w_in = P.dram("dn_w_in", [D, 4 * D], F32, "ExternalInput")
        g.dn_wg_d = P.dram("dn_wg", [D, 32], F32, "ExternalInput")
        g.dn_vec_d = P.dram("dn_vec", [128, 32], F32, "ExternalInput")
        g.dn_cw_d = P.dram("dn_cw", [128, 72], F32, "ExternalInput")
        g.dn_gout_d = P.dram("dn_gout", [128, 1], F32, "ExternalInput")
        g.dn_mk_d = P.dram("dn_mk", [128, 11, 128], F32, "ExternalInput")
        g.dn_w_out = P.dram("dn_w_out", [D, D], F32, "ExternalInput")
        g.dn_s0_d = P.dram("dn_s0", [2, 8, 128, 128], F32, "ExternalInput")
        g.dn_st_out = P.dram("dn_st_out", [2, 2, 8, 128, 128], F32, "ExternalOutput")
    if 3 in layers and mixers:
        if not hasattr(g, "dn_mk_d"):
            g.dn_mk_d = P.dram("dn_mk", [128, 11, 128], F32, "ExternalInput")
        g.rw_pv_d = P.dram("rw_pv", [128, 120], F32, "ExternalInput")
        g.rw_wrkv_d = P.dram("rw_wrkv", [3, D, D], F32, "ExternalInput")
        g.rw_w1_d = P.dram("rw_w1", [D, 128], F32, "ExternalInput")
        g.rw_a1_d = P.dram("rw_a1", [D, 128], F32, "ExternalInput")
        g.rw_g1_d = P.dram("rw_g1", [D, 128], F32, "ExternalInput")
        g.rw_l2_d = P.dram("rw_l2", [128, 3, D], F32, "ExternalInput")
        g.rw_wout_d = P.dram("rw_wout", [D, D], F32, "ExternalInput")
        g.rw_z0_d = P.dram("rw_z0", [2, 8, 128, 64], F32, "ExternalInput")
        g.rw_st_out = P.dram("rw_st_out", [2, 2, 16, 64, 64], F32, "ExternalOutput")
        g.rw_oT = P.sb("rw_oT", [128, NT], BF16, split=(1, 512))
        g.rwh = P.sb("rw_wh", [128, 8, 128], BF16)
        g.onesblk = P.sb("onesblk", [128, 128], BF16)
        g.eps_ln = P.sb("eps_ln", [128, 2], F32)
        P.memset("dve", g.eps_ln.all(), 64e-5)
        P.memset("dve", g.onesblk.all(), 0.0)
        P.memset("dve", g.onesblk[0:64, 0:64], 1.0)
        P.memset("dve", g.onesblk[64:128, 64:128], 1.0)
    if 2 in layers and mixers:
        g.ml_wa = P.dram("ml_wa", [D, 736], F32, "ExternalInput")
        g.ml_wqb = P.dram("ml_wqb", [384, 1536], F32, "ExternalInput")
        g.ml_wk = P.dram("ml_wk", [256, 1024], F32, "ExternalInput")
        g.ml_wv = P.dram("ml_wv", [256, 1024], F32, "ExternalInput")
        g.ml_wout = P.dram("ml_wout", [D, D], F32, "ExternalInput")
        g.ml_gn_d = P.dram("ml_gn", [128, 8], F32, "ExternalInput")
        g.ml_cos_d = P.dram("ml_cos", [96, 1024], F32, "ExternalInput")
        g.ml_sin_d = P.dram("ml_sin", [96, 1024], F32, "ExternalInput")
        g.ml_R_d = P.dram("ml_R", [96, 96], F32, "ExternalInput")
        g.ml_ckvc = P.dram("ml_ckvc", [256, 256], F32, "ExternalInput")
        g.ml_kpec = P.dram("ml_kpec", [256, 96], F32, "ExternalInput")
        g.ckv_out = P.dram("ckv_out", [2, 256, 256], F32, "ExternalOutput")
        g.kpe_out = P.dram("kpe_out", [2, 256, 32], F32, "ExternalOutput")

    P.dma("sp", g.ident.all(), g.identd.all(), "c0")
    P.dma("sp", g.cvs.all(), g.cv.all(), "c0")
    P.dma("sp", g.adab.all(), g.adabT.all(), "c0")
    P.dma("sp", g.nrm.all(), g.nrmT.all(), "c0")
    P.copy("dve", g.identb.all(), g.ident.all())
    P.memset("dve", g.onesb.all(), 1.0 / D)
    P.memset("dve", g.onesb1.all(), 1.0)
    P.memset("dve", g.ones2.all(), 0.0)
    P.memset("dve", g.ones2[0:64, 0:64], 1.0 / 64)
    P.memset("dve", g.ones2[64:128, 64:128], 1.0 / 64)
    g.sq = [P.sb("sq%d" % i, [128, 512], BF16) for i in range(2)]
    g.rstd = P.sb("rstd", [128, 512], F32)
    g.ntmp = [P.sb("ntmp%d" % i, [128, 512], F32) for i in range(2)]
    wbufs(g)
    g.one1 = P.sb("one1", [128, 2], F32)
    P.memset("dve", g.one1.all(), 1.0)
    g.eps6 = P.sb("eps6", [128, 2], F32)
    P.memset("dve", g.eps6.all(), 1e-6)
    P.act(g.sc.all(), g.cvs.all(), AF.Silu)

    load_x(g)
    first = True
    for li, l in enumerate(layers):
        if li == 0:
            adaln(g, l)
        norm_mod(g, l, 0)
        if mixers:
            [dn_layer, na_layer, mla_layer, rw_layer][l % 4](g, l)
        norm_mod(g, l, 1)
        ffn(g, l, layers[li + 1] if li + 1 < len(layers) else None)
    store_y(g)
    P.emit(reorder=REORDER)
    nc.in_names = list(P.in_names)
    global LASTP
    LASTP = P
    return nc


def carve(g, name, off, shape, dtype, split=None):
    n = int(np.prod(shape[1:]))
    esz = 4 if dtype == F32 else 2
    assert off % 4 == 0 and off + n * esz <= ARENA
    ap = g.arena.h[:, off // 4:(off + n * esz) // 4]
    if dtype != F32:
        ap = ap.bitcast(dtype)
    if len(shape) == 3:
        ap = ap.rearrange("p (a b) -> p a b", a=shape[1])
    elif len(shape) == 4:
        ap = ap.rearrange("p (a b c) -> p a b c", a=shape[1], b=shape[2])
    return T(name, ap, shape, split)


class PsCtx:
    def __init__(self, rot):
        self.psrot = rot
        self.pos = 0

    def take(self, g, k):
        total = 4 * len(self.psrot)
        if self.pos % k:
            self.pos += k - self.pos % k
        s_ = self.pos % total
        self.pos += k
        if k == 4:
            return g.psb[self.psrot[s_ // 4]], 0
        nb = len(self.psrot)
        p_ = s_ // 2
        return g.psb[self.psrot[p_ % nb]], ((p_ // nb) % 2) * 256 + (s_ % 2) * 128


def nextq(g, c=None):
    if c is not None:
        b, o = c.take(g, 1)
        return b[:, o:o + 128]
    r = g.psrot
    n = g.pq % (4 * len(r))
    g.pq += 1
    b = g.psb[r[n % len(r)]]
    q = n // len(r)
    return b[:, q * 128:(q + 1) * 128]


def nexth(g, c):
    b, o = c.take(g, 2)
    return b[:, o:o + 256]


class Pool:
    def __init__(self, tiles):
        self.t = tiles
        self.i = 0

    def get(self):
        x = self.t[self.i % len(self.t)]
        self.i += 1
        return x


def apool(g, name, off, n, shape, dtype, split=None):
    sz = int(np.prod(shape[1:])) * (4 if dtype == F32 else 2)
    return Pool([carve(g, "%s%d" % (name, i), off + i * sz, shape, dtype, split) for i in range(n)]), off + n * sz


def nextps(g, c=None):
    if c is not None:
        return c.take(g, 4)[0]
    r = g.psrot
    p = g.psb[r[g.pi % len(r)]]
    g.pi += 1
    return p


def load_x(g):
    P = g.P
    xin = [carve(g, "xin%d" % i, i * 4096, [128, D], F32) for i in range(2)]
    for t in range(NT // 128):
        b = xin[t % 2]
        P.dma("sp", b.all(), g.x[t * 128:(t + 1) * 128, :], "xin%d" % (t % 2))
        for hf in range(2):
            ps = nextps(g)
            for j in range(4):
                c = hf * 4 + j
                P.tr(ps[:, j * 128:(j + 1) * 128], b[:, c * 128:(c + 1) * 128], g.ident.all())
            P.copy("dve" if hf == 0 else "act", g.xT[:, hf * 4:(hf + 1) * 4, t * 128:(t + 1) * 128],
                   ps.all().re("p (j t) -> p j t", j=4))
    g.xin = xin
    P.barrier()


def store_y(g):
    P = g.P
    P.barrier()
    for t in range(NT // 128):
        b = g.xin[t % 2]
        for hf in range(2):
            ps = nextps(g)
            for j in range(4):
                c = hf * 4 + j
                P.tr(ps[:, j * 128:(j + 1) * 128], g.xT[:, c, t * 128:(t + 1) * 128], g.ident.all())
            P.copy("dve" if hf == 0 else "act", b[:, hf * 512:(hf + 1) * 512], ps.all())
        P.dma("sp", g.y[t * 128:(t + 1) * 128, :], b.all(), "xin%d" % (t % 2))


def adaln(g, l):
    P = g.P
    wbufs(g)
    P.barrier()
    g.awb = g.wi
    g.rowb = carve(g, "rowb", 0, [128, 6 * D], F32)
    src = g.ada_w.h[l].rearrange("(k p) n -> p k n", p=128)
    for n in range(12):
        wb = g.awb[n % 2]
        P.dma("pool", wb.all(), V(src[:, :, n * 512:(n + 1) * 512], g.ada_w.allkeys), "wi%d" % (n % 2))
        ps = nextps(g)
        for k in range(8):
            P.mm(ps[0:2, :], g.sc[:, k, :], wb[:, k, :], start=(k == 0), stop=(k == 7))
        P.copy("act", g.rowb[0:2, n * 512:(n + 1) * 512], ps[0:2, :])
    ps = nextps(g)
    for j in range(48):
        P.tr(ps[:, j * 2:(j + 1) * 2], g.rowb[0:2, j * 128:(j + 1) * 128], g.ident[0:2, 0:2])
    m = g.mod[l]
    P.tt("dve", m.all(), ps[:, 0:96].re("p (j s) -> p j s", s=2),
         g.adab[:, l, :].re("p (j o) -> p j o", o=1).bc([128, 48, 2]), ALU.add)
    for w in range(2):
        sl = m[:, (3 * w + 1) * 8:(3 * w + 2) * 8, :]
        P.stt("dve", g.gs[l][:, w, :, :], sl, 1.0,
              g.nrm[:, w, l, :].re("p (c o) -> p c o", o=1).bc([128, 8, 2]), ALU.add, ALU.mult)
    P.barrier()


def norm_mod(g, l, w):
    P = g.P
    m = g.mod[l]
    for tg in range(NG):
        cd = 0 if tg == 0 else 1
        ts_ = slice(tg * 512, (tg + 1) * 512)
        ps = nextps(g)
        for c in range(8):
            sq = g.sq[c % 2]
            P.act(sq.all(), g.xT[:, c, ts_], AF.Square)
            P.mm(ps.all(), g.onesb.all(), sq.all(), start=(c == 0), stop=(c == 7))
        P.act(g.rstd.all(), ps.all(), AF.Sqrt, bias=g.eps6[:, 0:1], scale=1.0)
        P.op("dve", "reciprocal", out=g.rstd.all(), in_=g.rstd.all())
        for c in range(8):
            tmp = g.ntmp[c % 2]
            P.tt("dve", tmp.all(), g.xT[:, c, ts_], g.rstd.all(), ALU.mult)
            P.act(g.hT[:, c, ts_], tmp.all(), AF.Identity,
                  scale=g.gs[l][:, w, c, cd:cd + 1], bias=m[:, (3 * w) * 8 + c, cd:cd + 1])


def wbufs(g):
    P = g.P
    if not hasattr(g, "wi"):
        g.wi = [P.sb("wi%d" % i, [128, 8, 512], BF16) for i in range(3)]
        g.wo = [P.sb("wo%d" % i, [128, NF, 128], BF16) for i in range(2)]
        g.sg = [P.sb("sg%d" % i, [128, 512], F32) for i in range(2)]


def chunk_masks():
    i = np.arange(128)
    same = (i[:, None] // 64) == (i[None, :] // 64)
    le = i[:, None] <= i[None, :]
    ge = i[:, None] >= i[None, :]
    f = np.float32
    McumF = (same & le).astype(f)
    McumB = (same & ge).astype(f)
    validF = same & ge
    validB = same & le
    NEGF = np.where(validF, 0.0, -1e30).astype(f)
    NEGB = np.where(validB, 0.0, -1e30).astype(f)
    SMF = (validF & (i[:, None] != i[None, :])).astype(f)
    SMB = (validB & (i[:, None] != i[None, :])).astype(f)
    Mblk = same.astype(f)
    Mch0 = np.repeat((i < 64).astype(f)[:, None], 128, 1)
    Mch1 = np.repeat((i >= 64).astype(f)[:, None], 128, 1)
    return np.ascontiguousarray(np.stack([McumF, McumB, NEGF, NEGB, SMF, SMB, Mblk, Mch0, Mch1,
                                          validF.astype(f), validB.astype(f)], 1))


def neumann_solve(g, N, Nt, tp, steps=5):
    P = g.P
    Tt = tp.get()
    P.tt("dve", Tt.all(), Nt.all(), g.ident.all(), ALU.add)
    Pm, Pt = N, Nt
    for k in range(steps):
        q1 = nextq(g)
        P.mm(q1, Pt.all(), Pm.all())
        Pn = tp.get()
        P.copy("act", Pn.all(), q1)
        if k < steps - 1:
            q2 = nextq(g)
            P.mm(q2, Pm.all(), Pt.all())
            Ptn = tp.get()
            P.copy("act", Ptn.all(), q2)
        q3 = nextq(g)
        P.mm(q3, Pn.all(), Tt.all())
        Ttn = tp.get()
        P.tt("dve", Ttn.all(), q3, Tt.all(), ALU.add)
        Tt = Ttn
        Pm = Pn
        if k < steps - 1:
            Pt = Ptn
    return Tt


def interleave(gens):
    gens = list(gens)
    while gens:
        for gn in list(gens):
            try:
                next(gn)
            except StopIteration:
                gens.remove(gn)


def interleave_gen(gens):
    gens = list(gens)
    while gens:
        for gn in list(gens):
            try:
                next(gn)
            except StopIteration:
                gens.remove(gn)
        yield


def neumann_solve_gen(g, N, Nt, tp, steps=5, pc=None):
    P = g.P
    Tt = tp.get()
    P.tt("dve", Tt.all(), Nt.all(), g.ident.all(), ALU.add)
    Pm, Pt = N, Nt
    for k in range(steps):
        q1 = nextq(g, pc)
        P.mm(q1, Pt.all(), Pm.all())
        if k < steps - 1:
            q2 = nextq(g, pc)
            P.mm(q2, Pm.all(), Pt.all())
        yield
        Pn = tp.get()
        P.copy("act", Pn.all(), q1)
        if k < steps - 1:
            Ptn = tp.get()
            P.copy("dve", Ptn.all(), q2)
        yield
        q3 = nextq(g, pc)
        P.mm(q3, Pn.all(), Tt.all())
        yield
        Ttn = tp.get()
        P.tt("dve", Ttn.all(), q3, Tt.all(), ALU.add)
        Tt = Ttn
        Pm = Pn
        if k < steps - 1:
            Pt = Ptn
    return Tt


SEQS = ((0, 2), (2, 2), (4, 8))


def dn_layer(g, l):
    P = g.P
    P.barrier()
    P.mode(True)
    off = 0
    MK = carve(g, "dn_MK", off, [128, 11, 128], F32); off += 11 * 512
    ones = carve(g, "dn_ones", off, [128, 128], F32); off += 512
    raw = carve(g, "dn_raw", off, [128, NT], F32, split=(1, 512)); off += 6144
    qT = carve(g, "dn_qT", off, [128, NT], F32, split=(1, 128)); off += 6144
    kT = carve(g, "dn_kT", off, [128, NT], F32, split=(1, 128)); off += 6144
    vT = carve(g, "dn_vT", off, [128, NT], F32, split=(1, 128)); off += 6144
    zs = raw
    oTh = carve(g, "dn_oTh", off, [128, NT], BF16, split=(1, 512)); off += 3072
    qtok = carve(g, "dn_qtok", off, [128, 12, 128], F32, split=(1, 1)); off += 6144
    ktok = carve(g, "dn_ktok", off, [128, 12, 128], F32, split=(1, 1)); off += 6144
    vtok = carve(g, "dn_vtok", off, [128, 12, 128], F32, split=(1, 1)); off += 6144
    oacc = T("dn_vT", vT.h.rearrange("p (t c) -> p t c", t=12), [128, 12, 128], split=(1, 1))
    tp0, off = apool(g, "dn_tp", off, 9, [128, 128], F32)
    tq0, off = apool(g, "dn_tq", off, 6, [128, 128], F32)
    lp0, off = apool(g, "dn_lp", off, 3, [128, 6, 128], F32)
    uw0, off = apool(g, "dn_uw", off, 2, [128, 256], F32)
    Sb, off = apool(g, "dn_S", off, 4, [128, 128], F32)
    assert off <= ARENA, off
    X1 = T("dnx1", g.wi[1].h[:, :, :].rearrange("p a b -> p (a b)").bitcast(F32), [128, 2048], split=(1, 128))
    X2 = T("dnx2", g.wo[1].h[:, :, :].rearrange("p a b -> p (a b)").bitcast(F32), [128, 1408], split=(1, 128))
    tp1 = Pool([X1[:, i * 128:(i + 1) * 128] for i in range(9)])
    tq1 = Pool([X1[:, i * 128:(i + 1) * 128] for i in range(9, 15)])
    uw1 = Pool([X2[:, 0:256], X2[:, 256:512]])
    lp1 = Pool([X2[:, 512:1280].re("p (s c) -> p s c", s=6), lp0.t[2]])
    lp0 = Pool(lp0.t[0:2])
    tpd, tqd, uwd, lpd = [tp0, tp1], [tq0, tq1], [uw0, uw1], [lp0, lp1]
    pcs = [PsCtx([0, 1, 2]), PsCtx([3, 4, 5])]
    tp = tp0
    tb = g.wi[2].all().re("p a b -> p (a b)").bitcast(F32)
    gt = tb[:, 0:384].re("p (t c) -> p t c", t=12)
    gg = tb[:, 384:576].re("p (t c) -> p t c", t=12)
    gc = tb[:, 576:768].re("p (t c) -> p t c", t=12)
    egc = tb[:, 768:960].re("p (t c) -> p t c", t=12)
    egk = tb[:, 960:1152].re("p (t c) -> p t c", t=12)
    egl = tb[:, 1152:1536].re("p (t k c) -> p t k c", t=12, k=2)
    bex = tb[:, 1536:1728].re("p (t c) -> p t c", t=12)
    nbe = tb[:, 1728:1920].re("p (t c) -> p t c", t=12)
    vec = tb[:, 1920:1952]
    cw = tb[:, 1952:2024].re("p (h s k) -> p h s k", h=8, s=3)
    gout = tb[:, 2024:2025]
    P.dma("sp", MK.all(), g.dn_mk_d.all(), "c3")
    P.dma("sp", vec, g.dn_vec_d.all(), "c3")
    P.dma("sp", cw, g.dn_cw_d.all(), "c3")
    P.dma("sp", gout, g.dn_gout_d.all(), "c3")
    P.memset("dve", ones.all(), 1.0)
    McumD = [MK[:, 0, :], MK[:, 1, :]]
    NEGD = [MK[:, 2, :], MK[:, 3, :]]
    SMD = [MK[:, 4, :], MK[:, 5, :]]
    Mblk = MK[:, 6, :]
    Mch = [MK[:, 7, :], MK[:, 8, :]]
    g.psrot = [0, 1, 2, 3, 4, 5]
    wg = g.wo[0]
    P.dma("pool", wg[:, 0:8, 0:32], V(g.dn_wg_d.h.rearrange("(k p) n -> p k n", p=128), g.dn_wg_d.allkeys), "wo0")
    pg = g.psb[6]
    for t in range(12):
        for k in range(8):
            P.mm(pg[:, t * 32:(t + 1) * 32], g.hT[:, k, t * 128:(t + 1) * 128], wg[:, k, 0:32], start=(k == 0), stop=(k == 7))
    P.copy("dve", gt, pg[:, 0:384].re("p (t c) -> p t c", t=12))
    P.tt("dve", gg, gt[:, :, 0:16], vec[:, 16:32].re("p (o c) -> p o c", o=1).bc([128, 12, 16]), ALU.add)
    P.act(gg, gg, AF.Exp)
    P.act(gg, gg, AF.Ln, bias=g.one1[:, 0:1], scale=1.0)
    P.act(vec[:, 0:16], vec[:, 0:16], AF.Exp)
    P.stt("dve", gg, gg, -1.0, vec[:, 0:16].re("p (o c) -> p o c", o=1).bc([128, 12, 16]), ALU.mult, ALU.mult)
    P.act(gt[:, :, 16:32], gt[:, :, 16:32], AF.Sigmoid)
    beta = gt[:, :, 16:32]
    pc = g.psb[7]
    for t in range(12):
        for d_ in range(2):
            P.mm(pc[:, t * 16 + d_ * 8:t * 16 + (d_ + 1) * 8], McumD[d_], gg[:, t, d_ * 8:(d_ + 1) * 8])
    P.copy("dve", gc, pc[:, 0:192].re("p (t c) -> p t c", t=12))
    P.act(egc, gc, AF.Exp)
    pc2 = g.psb[6]
    for t in range(12):
        P.mm(pc2[:, t * 16:(t + 1) * 16], Mblk, gg[:, t, :])
    P.tt("dve", egk, pc2[:, 0:192].re("p (t c) -> p t c", t=12), gc, ALU.subtract)
    P.act(egk, egk, AF.Exp)
    pc3 = g.psb[7]
    for t in range(12):
        for c in range(2):
            P.mm(pc3[:, (t * 2 + c) * 16:(t * 2 + c + 1) * 16], Mch[c], gg[:, t, :])
    P.act(egl, pc3[:, 0:384].re("p (t k c) -> p t k c", t=12, k=2), AF.Exp)
    P.tt("dve", bex, beta, egc, ALU.mult)
    P.ts("dve", nbe, beta, -1.0, ALU.mult)
    P.ts("dve", gg, gg, -1.0, ALU.mult)
    win = g.dn_w_in.h.rearrange("(k p) n -> p k n", p=128)
    wout = g.dn_w_out.h
    m = g.mod[l]
    for h in range(8):
        wb = g.wi[0]
        for i_ in range(4):
            P.dma("pool", wb[:, :, i_ * 128:(i_ + 1) * 128], V(win[:, :, i_ * 1024 + h * 128:i_ * 1024 + (h + 1) * 128], g.dn_w_in.allkeys), "wi0")
        for i_, dst in enumerate((qT, kT, vT, None)):
            for tg in range(NG):
                ts_ = slice(tg * 512, (tg + 1) * 512)
                ps = nextps(g)
                for k in range(8):
                    P.mm(ps.all(), wb[:, k, i_ * 128:(i_ + 1) * 128], g.hT[:, k, ts_], start=(k == 0), stop=(k == 7))
                if dst is None:
                    P.act(zs[:, ts_], ps.all(), AF.Silu)
                else:
                    P.copy("act", raw[:, ts_], ps.all())
            if dst is None:
                continue
            P.ts("dve", dst.all(), raw.all(), cw[:, h, i_, 1:2], ALU.mult)
            for (s0, e0) in ((0, 256), (256, 512), (512, NT)):
                P.stt("dve", dst[:, s0 + 1:e0], raw[:, s0:e0 - 1], cw[:, h, i_, 0:1], dst[:, s0 + 1:e0], ALU.mult, ALU.add)
                P.stt("dve", dst[:, s0:e0 - 1], raw[:, s0 + 1:e0], cw[:, h, i_, 2:3], dst[:, s0:e0 - 1], ALU.mult, ALU.add)
            for tg in range(NG):
                ts_ = slice(tg * 512, (tg + 1) * 512)
                P.act(dst[:, ts_], dst[:, ts_], AF.Silu)
                if i_ < 2:
                    sq = g.sq[g.nsq % 2]
                    g.nsq += 1
                    P.act(sq.all(), dst[:, ts_], AF.Square)
                    pm = nextps(g)
                    P.mm(pm.all(), g.onesb1.all(), sq.all())
                    P.act(g.rstd.all(), pm.all(), AF.Sqrt, bias=g.eps6[:, 0:1], scale=1.0)
                    P.op("dve", "reciprocal", out=g.rstd.all(), in_=g.rstd.all())
                    P.stt("dve", dst[:, ts_], dst[:, ts_], (128 ** -0.5) if i_ == 0 else 1.0, g.rstd.all(), ALU.mult, ALU.mult)
        for src, dst in ((qT, qtok), (kT, ktok), (vT, vtok)):
            for t4 in range(3):
                ps = nextps(g)
                for j in range(4):
                    t = t4 * 4 + j
                    P.tr(ps[:, j * 128:(j + 1) * 128], src[:, t * 128:(t + 1) * 128], g.ident.all())
                P.copy("act", dst[:, t4 * 4:(t4 + 1) * 4, :], ps.all().re("p (t c) -> p t c", t=4))
        done = set()
        for si, (t0, nt) in enumerate(SEQS):
            Sc = [None, None]
            for d_ in range(2):
                Sc[d_] = Sb.get()
                if si < 2:
                    P.memset("dve", Sc[d_].all(), 0.0)
                else:
                    P.dma("sp", Sc[d_].all(), g.dn_s0_d[d_, h, :, :], "dn_s0")
            def dn_body(d_, step):
                t = t0 + step if d_ == 0 else t0 + nt - 1 - step
                col = d_ * 8 + h
                tk = slice(t * 128, (t + 1) * 128)
                pG = nextq(g, pcs[d_])
                P.mm(pG, kT[:, tk], kT[:, tk])
                pQK = nextq(g, pcs[d_])
                P.mm(pQK, qT[:, tk], kT[:, tk])
                ngb = tpd[d_].get()
                P.ts("dve", ngb.all(), ones.all(), gg[:, t, col:col + 1], ALU.mult)
                pA = nextq(g, pcs[d_])
                P.mm(pA, ngb.all(), McumD[d_], start=True, stop=False)
                P.mm(pA, g.ident.all(), NEGD[d_], start=False, stop=True)
                yield
                Dm = tpd[d_].get()
                P.act(Dm.all(), pA, AF.Exp, bias=gc[:, t, col:col + 1], scale=1.0)
                Ds = tpd[d_].get()
                P.tt("dve", Ds.all(), Dm.all(), SMD[d_], ALU.mult)
                N = tpd[d_].get()
                P.stt("dve", N.all(), pG, nbe[:, t, col:col + 1], Ds.all(), ALU.mult, ALU.mult)
                attn = tpd[d_].get()
                P.tt("dve", attn.all(), pQK, Dm.all(), ALU.mult)
                yield
                pT1 = nextq(g, pcs[d_])
                P.tr(pT1, N.all(), g.ident.all())
                yield
                Nt = tpd[d_].get()
                P.copy("act", Nt.all(), pT1)
                pT2 = nextq(g, pcs[d_])
                P.tr(pT2, attn.all(), g.ident.all())
                attnT = tpd[d_].get()
                P.copy("act", attnT.all(), pT2)
                Tt = yield from neumann_solve_gen(g, N, Nt, tqd[d_], pc=pcs[d_])
                yield
                rhs = uwd[d_].get()
                P.ts("dve", rhs[:, 0:128], vtok[:, t, :], beta[:, t, col:col + 1], ALU.mult)
                P.ts("dve", rhs[:, 128:256], ktok[:, t, :], bex[:, t, col:col + 1], ALU.mult)
                pU = nexth(g, pcs[d_])
                P.mm(pU[:, 0:256], Tt.all(), rhs.all())
                yield
                UW = uwd[d_].get()
                P.copy("act", UW.all(), pU[:, 0:256])
                L = lpd[d_].get()
                pq_ = nextq(g, pcs[d_])
                P.mm(pq_, attnT.all(), UW[:, 128:256])
                yield
                Qh = tpd[d_].get()
                P.stt("dve", Qh.all(), qtok[:, t, :], egc[:, t, col:col + 1], pq_, ALU.mult, ALU.subtract)
                yield
                pq2 = nextq(g, pcs[d_])
                P.tr(pq2, Qh.all(), g.ident.all())
                P.copy("act", L[:, 0, :], pq2)
                pq3 = nextq(g, pcs[d_])
                P.mm(pq3, attnT.all(), UW[:, 0:128])
                P.copy("act", L[:, 1, :], pq3)
                yield
                kg = tpd[d_].get()
                P.ts("dve", kg.all(), ktok[:, t, :], egk[:, t, col:col + 1], ALU.mult)
                for c in range(2):
                    cr = slice(c * 64, (c + 1) * 64)
                    pp = nextq(g, pcs[d_])
                    P.mm(pp, UW[cr, 128:256], kg[cr, :])
                    P.stt("dve", L[:, 2 + c, :], g.ident.all(), egl[:, t, c, col:col + 1], pp, ALU.mult, ALU.subtract)
                    ps_ = nextq(g, pcs[d_])
                    P.mm(ps_, kg[cr, :], UW[cr, 0:128])
                    P.copy("act", L[:, 4 + c, :], ps_)
                yield
                for c in ((0, 1) if d_ == 0 else (1, 0)):
                    cr = slice(c * 64, (c + 1) * 64)
                    po_ = nextq(g, pcs[d_])
                    P.mm(po_[cr, :], L[:, 0, cr], Sc[d_].all())
                    if (t, c) in done:
                        P.tt("dve", oacc[cr, t, :], po_[cr, :], oacc[cr, t, :], ALU.add)
                        P.tt("dve", oacc[cr, t, :], oacc[cr, t, :], L[cr, 1, :], ALU.add)
                    else:
                        P.tt("dve", oacc[cr, t, :], po_[cr, :], L[cr, 1, :], ALU.add)
                        done.add((t, c))
                    pn = nextq(g, pcs[d_])
                    P.mm(pn, L[:, 2 + c, :], Sc[d_].all())
                    Sn = Sb.get()
                    P.tt("dve", Sn.all(), pn, L[:, 4 + c, :], ALU.add)
                    Sc[d_] = Sn
            for step in range(nt):
                interleave([dn_body(0, step), dn_body(1, step)])
            if si < 2:
                for d_ in range(2):
                    P.dma("sp", g.dn_st_out[si, d_, h, :, :], Sc[d_].all(), "dn_so")
        for t4 in range(3):
            ssq = g.rstd[:, t4 * 4:(t4 + 1) * 4]
            for j in range(4):
                t = t4 * 4 + j
                junk = tp.get()
                P.act(junk.all(), oacc[:, t, :], AF.Square, accum_out=g.rstd[:, t:t + 1])
            P.act(g.rstd[:, 16 + t4 * 4:16 + (t4 + 1) * 4], ssq, AF.Sqrt, bias=g.eps6[:, 0:1], scale=1.0 / 128)
            P.op("dve", "reciprocal", out=g.rstd[:, 16 + t4 * 4:16 + (t4 + 1) * 4], in_=g.rstd[:, 16 + t4 * 4:16 + (t4 + 1) * 4])
            ps = nextps(g)
            for j in range(4):
                t = t4 * 4 + j
                on = tp.get()
                P.ts("dve", on.all(), oacc[:, t, :], g.rstd[:, 16 + t:17 + t], ALU.mult)
                P.tr(ps[:, j * 128:(j + 1) * 128], on.all(), g.ident.all())
            P.stt("dve", oTh[:, t4 * 512:(t4 + 1) * 512], ps.all(), gout, zs[:, t4 * 512:(t4 + 1) * 512], ALU.mult, ALU.mult)
        wo = g.wo[0]
        wov = wo.all().re("p a b -> p (a b)")[:, 0:1024]
        P.dma("pool", wov, V(wout[h * 128:(h + 1) * 128, :], g.dn_w_out.allkeys), "wo0")
        for d in range(8):
            for tg in range(NG):
                cd = 0 if tg == 0 else 1
                ts_ = slice(tg * 512, (tg + 1) * 512)
                ps = nextps(g)
                P.mm(ps.all(), wov[:, d * 128:(d + 1) * 128], oTh[:, ts_])
                P.stt("dve", g.xT[:, d, ts_], ps.all(), m[:, 2 * 8 + d, cd:cd + 1], g.xT[:, d, ts_], ALU.mult, ALU.add)
    g.psrot = list(range(8))
    P.barrier()


def mla_consts():
    t = np.arange(1024)
    cos = np.ones((96, 1024), np.float32)
    sin = np.zeros((96, 1024), np.float32)
    R = np.zeros((96, 96), np.float32)
    inv = 10000.0 ** (-np.arange(8, dtype=np.float32) / 8)
    for j in range(32):
        grp, jj = j // 16, j % 16
        pos = (t // 64) if grp == 0 else (t % 64)
        ang = pos.astype(np.float32) * inv[jj % 8]
        cos[64 + j] = np.cos(ang)
        sin[64 + j] = np.sin(ang)
        if jj < 8:
            R[64 + j + 8, 64 + j] = -1.0
        else:
            R[64 + j - 8, 64 + j] = 1.0
    return cos, sin, R


def mla_layer(g, l):
    P = g.P
    P.barrier()
    craw = carve(g, "ml_craw", 0, [128, 5, 512], F32, split=(1, 1))
    cqn = carve(g, "ml_cqn", 10240, [128, 3, NT], BF16, split=(2, 512))
    ckvn = carve(g, "ml_ckvn", 19456, [128, 2, 1792], BF16, split=(2, 256))
    kpe = carve(g, "ml_kpe", 26624, [128, 1792], BF16, split=(1, 256))
    Vp = [carve(g, "ml_Vp%d" % i, 30208 + i * 3584, [128, 14, 128], BF16, split=(1, 1)) for i in range(2)]
    qb = [carve(g, "ml_q%d" % i, 37376 + i * 6656, [128, NT], BF16, split=(1, 512)) for i in range(2)]
    kb = [carve(g, "ml_k%d" % i, 37376 + i * 6656 + 3072, [128, 1792], BF16, split=(1, 256)) for i in range(2)]
    cos = carve(g, "ml_cos", 50688, [128, 1024], F32)
    sin = carve(g, "ml_sin", 54784, [128, 1024], F32)
    Rm = carve(g, "ml_R", 58880, [128, 96], BF16)
    gn = carve(g, "ml_gn", 59136, [128, 8], F32)
    oT = g.hT
    P.dma("sp", cos[0:96, :], g.ml_cos_d.all(), "c2")
    P.dma("sp", sin[0:96, :], g.ml_sin_d.all(), "c2")
    P.dma("sp", gn.all(), g.ml_gn_d.all(), "c2")
    P.dma("pool", Rm[0:96, :], g.ml_R_d.all(), "c2")
    wa_src = g.ml_wa.h.rearrange("(k p) n -> p k n", p=128)
    P.dma("pool", g.wi[0].all(), V(wa_src[:, :, 0:512], g.ml_wa.allkeys), "wi0")
    P.dma("pool", g.wi[1][:, :, 0:224], V(wa_src[:, :, 512:736], g.ml_wa.allkeys), "wi1")
    P.ts("dve", gn[:, 5:6], gn[:, 5:6], 96 ** -0.5, ALU.mult)
    g.psrot = [0, 1, 2, 3]
    ckv_out = g.ckv_out
    kpe_out = g.kpe_out
    for tg in range(NG):
        ts_ = slice(tg * 512, (tg + 1) * 512)
        for c in range(6):
            ps = nextps(g)
            wsl = g.wi[0][:, :, c * 128:(c + 1) * 128] if c < 4 else \
                (g.wi[1][:, :, 0:128] if c == 4 else g.wi[1][:, :, 128:224])
            mrows = 128 if c < 5 else 96
            for k in range(8):
                P.mm(ps[0:mrows, :], wsl[:, k, :], g.hT[:, k, ts_], start=(k == 0), stop=(k == 7))
            if c < 5:
                P.copy("act", craw[:, c, :], ps.all())
            else:
                P.copy("act", kpe[64:96, ts_], ps[64:96, :])
                if tg == 0:
                    kf = g.ntmp[0]
                    P.copy("dve", kf[64:96, :], ps[64:96, :])
                    pt = nextps(g)
                    for t in range(4):
                        P.tr(pt[:, t * 32:(t + 1) * 32], kf[64:96, t * 128:(t + 1) * 128], g.ident[64:96, 64:96])
                    st = g.ntmp[1]
                    P.copy("act", st[:, 0:128], pt[:, 0:128])
                    for t in range(4):
                        P.dma("sp", kpe_out[t // 2, (t % 2) * 128:(t % 2 + 1) * 128, :], st[:, t * 32:(t + 1) * 32], "ntmp1")
        for (c0, nch, dst, gc0) in ((0, 3, cqn, 0), (3, 2, ckvn, 3)):
            pm = nextps(g)
            for c in range(nch):
                sq = g.sq[g.nsq % 2]
                g.nsq += 1
                P.act(sq.all(), craw[:, c0 + c, :], AF.Square)
                P.mm(pm.all(), g.onesb1.all(), sq.all(), start=(c == 0), stop=(c == nch - 1))
            P.act(g.rstd.all(), pm.all(), AF.Sqrt, bias=g.eps6[:, 0:1], scale=1.0 / (128 * nch))
            P.op("dve", "reciprocal", out=g.rstd.all(), in_=g.rstd.all())
            for c in range(nch):
                if dst is ckvn and tg == 0:
                    cf = g.sg[c % 2]
                    P.stt("dve", cf.all(), craw[:, c0 + c, :], gn[:, gc0 + c:gc0 + c + 1], g.rstd.all(), ALU.mult, ALU.mult)
                    P.copy("act", dst[:, c, ts_], cf.all())
                    pt = nextps(g)
                    for t in range(4):
                        P.tr(pt[:, t * 128:(t + 1) * 128], cf[:, t * 128:(t + 1) * 128], g.ident.all())
                    st = g.ntmp[c % 2]
                    P.copy("act", st.all(), pt.all())
                    for t in range(4):
                        P.dma("sp", ckv_out[t // 2, (t % 2) * 128:(t % 2 + 1) * 128, c * 128:(c + 1) * 128],
                              st[:, t * 128:(t + 1) * 128], "ntmp%d" % (c % 2))
                else:
                    P.stt("dve", dst[:, c, ts_], craw[:, c0 + c, :], gn[:, gc0 + c:gc0 + c + 1], g.rstd.all(), ALU.mult, ALU.mult)
    for t in range(2):
        st = g.sg[t]
        P.dma("sp", st[:, 0:256], g.ml_ckvc[t * 128:(t + 1) * 128, :], "sg%d" % t)
        P.dma("sp", st[:, 256:352], g.ml_kpec[t * 128:(t + 1) * 128, :], "sg%d" % t)
        ps = nextps(g)
        for c in range(2):
            P.tr(ps[:, c * 128:(c + 1) * 128], st[:, c * 128:(c + 1) * 128], g.ident.all())
        P.tr(ps[0:96, 256:384], st[:, 256:352], g.ident.all())
        P.copy("act", ckvn[:, :, NT + t * 128:NT + (t + 1) * 128], ps[:, 0:256].re("p (c t) -> p c t", c=2))
        P.copy("act", kpe[64:96, NT + t * 128:NT + (t + 1) * 128], ps[64:96, 256:384])
    wq_src = g.ml_wqb.h.rearrange("(k p) n -> p k n", p=128)
    wq0 = g.wi[0].all().re("p a b -> p (a b)")[:, 0:2304].re("p (k n) -> p k n", k=3)
    wq1 = g.wi[1].all().re("p a b -> p (a b)")[:, 0:2304].re("p (k n) -> p k n", k=3)
    P.dma("pool", wq0, V(wq_src[:, :, 0:768], g.ml_wqb.allkeys), "wi0")
    P.dma("pool", wq1, V(wq_src[:, :, 768:1536], g.ml_wqb.allkeys), "wi1")
    wkv = g.wi[2].all().re("p a b -> p (a b)").re("p (w k n) -> p w k n", w=2, k=2)
    P.dma("pool", wkv[:, 0, :, :], V(g.ml_wk.h.rearrange("(k p) n -> p k n", p=128), g.ml_wk.allkeys), "wi2")
    P.dma("pool", wkv[:, 1, :, :], V(g.ml_wv.h.rearrange("(k p) n -> p k n", p=128), g.ml_wv.allkeys), "wi2")
    nb = 0
    cols_groups = [(0, 512, False), (512, 512, True), (1024, 512, True), (1536, 256, False)]
    for a in range(8):
        Vt = Vp[a % 2]
        for t4 in range(4):
            ps = nextps(g)
            nt_ = 4 if t4 < 3 else 2
            for tt_ in range(nt_):
                t = t4 * 4 + tt_
                for k in range(2):
                    P.mm(ps[:, tt_ * 128:(tt_ + 1) * 128], ckvn[:, k, t * 128:(t + 1) * 128], wkv[:, 1, k, a * 128:(a + 1) * 128],
                         start=(k == 0), stop=(k == 1))
            P.copy("act", Vt[:, t4 * 4:t4 * 4 + nt_, :], ps[:, 0:nt_ * 128].re("p (t c) -> p t c", t=nt_))
        for hh in range(2):
            h = 2 * a + hh
            pb = hh * 64
            qT = qb[h % 2]
            kT = kb[h % 2]
            wq = wq0 if h < 8 else wq1
            hq = h % 8
            for tg in range(NG):
                ts_ = slice(tg * 512, (tg + 1) * 512)
                ps = nextps(g)
                for k in range(3):
                    P.mm(ps[0:96, :], wq[:, k, hq * 96:(hq + 1) * 96], cqn[:, k, ts_], start=(k == 0), stop=(k == 2))
                mla_norm96(g, ps, gn[0:96, 5:6], qT[0:96, ts_])
                if tg > 0:
                    mla_rope(g, qT, ts_, cos, sin, Rm, (tg - 1) * 512)
            for gi, (c0, n, roped) in enumerate(cols_groups):
                cs = slice(c0, c0 + n)
                ps = nextps(g)
                for k in range(2):
                    P.mm(ps[0:64, 0:n], wkv[:, 0, k, h * 64:(h + 1) * 64], ckvn[:, k, cs], start=(k == 0), stop=(k == 1))
                kr = g.ntmp[gi % 2]
                P.copy("act", kr[0:64, 0:n], ps[0:64, 0:n])
                P.copy("dve", kr[64:96, 0:n], kpe[64:96, cs])
                mla_norm96(g, kr, gn[0:96, 6:7], kT[0:96, cs], n=n)
                if roped:
                    mla_rope(g, kT, cs, cos, sin, Rm, c0 - 512)
            for s_ in range(2):
                po = g.psb[4 + 2 * (nb % 2)]
                pd = g.psb[5 + 2 * (nb % 2)]
                nb += 1
                qs = slice(s_ * 256, (s_ + 1) * 256)
                for kc in range(2):
                    ps = nextps(g)
                    tile = s_ * 2 + kc
                    P.mm(ps[:, 0:256], kT[0:96, tile * 128:(tile + 1) * 128], qT[0:96, qs])
                    E = g.Ebuf[g.nE % 3]
                    g.nE += 1
                    P.act(E[:, 0:256], ps[:, 0:256], AF.Exp)
                    P.mm(po[pb:pb + 64, 0:256], Vt[:, tile, pb:pb + 64], E[:, 0:256], start=(kc == 0), stop=(kc == 1))
                    P.mm(pd[pb:pb + 64, 0:256], g.onesb1[:, 0:64], E[:, 0:256], start=(kc == 0), stop=(kc == 1))
                attn_finish(g, po, pd, pb, 256, oT[pb:pb + 64, a, qs])
            for G in range(2):
                po = g.psb[4 + 2 * (nb % 2)]
                pd = g.psb[5 + 2 * (nb % 2)]
                nb += 1
                qs = slice(512 + G * 512, 512 + (G + 1) * 512)
                for ci in range(10):
                    ps = nextps(g)
                    tile = (12 + ci) if ci < 2 else (4 + ci - 2)
                    P.mm(ps.all(), kT[0:96, tile * 128:(tile + 1) * 128], qT[0:96, qs])
                    E = g.Ebuf[g.nE % 3]
                    g.nE += 1
                    P.act(E.all(), ps.all(), AF.Exp)
                    P.mm(po[pb:pb + 64, :], Vt[:, tile, pb:pb + 64], E.all(), start=(ci == 0), stop=(ci == 9))
                    P.mm(pd[pb:pb + 64, :], g.onesb1[:, 0:64], E.all(), start=(ci == 0), stop=(ci == 9))
                attn_finish(g, po, pd, pb, 512, oT[pb:pb + 64, a, qs])
    g.psrot = list(range(8))
    out_proj(g, l, g.ml_wout.h.rearrange("(c p) d -> p c d", p=128), g.ml_wout.allkeys, oT)
    P.barrier()


def mla_norm96(g, src, gain, out_bf, n=512):
    P = g.P
    sq = g.sq[g.nsq % 2]
    g.nsq += 1
    P.act(sq[0:96, 0:n], src[0:96, 0:n], AF.Square)
    pm = nextps(g)
    P.mm(pm[0:96, 0:n], g.onesb1[0:96, 0:96], sq[0:96, 0:n])
    P.act(g.rstd[0:96, 0:n], pm[0:96, 0:n], AF.Sqrt, bias=g.eps6[0:96, 0:1], scale=1.0 / 96)
    P.op("dve", "reciprocal", out=g.rstd[0:96, 0:n], in_=g.rstd[0:96, 0:n])
    P.stt("dve", out_bf, src[0:96, 0:n], gain, g.rstd[0:96, 0:n], ALU.mult, ALU.mult)


def mla_rope(g, xT, cs, cos, sin, Rm, p0):
    P = g.P
    n = cs.stop - cs.start
    pr = nextps(g)
    P.mm(pr[0:96, 0:n], Rm[0:96, 0:96], xT[0:96, cs])
    t1 = g.ntmp[0]
    t2 = g.ntmp[1]
    P.tt("dve", t1[64:96, 0:n], xT[64:96, cs], cos[64:96, p0:p0 + n], ALU.mult)
    P.tt("dve", t2[64:96, 0:n], pr[64:96, 0:n], sin[64:96, p0:p0 + n], ALU.mult)
    P.tt("dve", xT[64:96, cs], t1[64:96, 0:n], t2[64:96, 0:n], ALU.add)


def rw_layer(g, l):
    P = g.P
    P.barrier()
    off = 0
    MK = carve(g, "rw_MK", off, [128, 7, 128], F32); off += 7 * 512
    rT = carve(g, "rw_r", off, [128, NT], BF16, split=(1, 128)); off += 3072
    kT = carve(g, "rw_k", off, [128, NT], BF16, split=(1, 128)); off += 3072
    kkT = carve(g, "rw_kk", off, [128, NT], BF16, split=(1, 128)); off += 3072
    gT = carve(g, "rw_g", off, [128, NT], BF16, split=(1, 128)); off += 3072
    aT = [carve(g, "rw_a%d" % i, off + i * 3072, [128, NT], BF16, split=(1, 128)) for i in range(2)]; off += 6144
    vT = carve(g, "rw_v", off, [128, NT], F32, split=(1, 128)); off += 6144
    ldT = [carve(g, "rw_ld%d" % i, off + i * 6144, [128, NT], F32, split=(1, 128)) for i in range(2)]; off += 12288
    yacc = carve(g, "rw_yacc", off, [128, 12, 128], F32, split=(1, 1)); off += 6144
    tpA, off = apool(g, "rw_tp", off, 26, [128, 128], F32)
    wp, off = apool(g, "rw_wp", off, 4, [128, 256], F32)
    lp, off = apool(g, "rw_lp", off, 2, [128, 4, 128], F32)
    Zb, off = apool(g, "rw_Z", off, 6, [128, 64], F32)
    vtk, off = apool(g, "rw_vtk", off, 2, [128, 128], F32)
    assert off <= ARENA, off

    def alias_tiles(name, h, ncol):
        t_ = T(name, h, [128, ncol], split=(1, 128))
        return [t_[:, i * 128:(i + 1) * 128] for i in range(ncol // 128)]
    A_ = list(tpA.t)
    B_ = [T("Ebuf%d" % (i // 2), g.Ebuf[i // 2].h[:, :].bitcast(F32)[:, (i % 2) * 128:(i % 2 + 1) * 128], [128, 128]) for i in range(6)]
    C_ = alias_tiles("rwx0", g.wi[0].h[:, :, :].rearrange("p a b -> p (a b)").bitcast(F32), 2048) + \
        alias_tiles("rwx1", g.wi[1].h[:, :, :].rearrange("p a b -> p (a b)").bitcast(F32), 2048)
    D_ = alias_tiles("rwx2", g.sg[0].h[:, :], 512) + alias_tiles("rwx3", g.sg[1].h[:, :], 512) + \
        alias_tiles("rwx4", g.ntmp[0].h[:, :], 512) + alias_tiles("rwx5", g.ntmp[1].h[:, :], 512)
    E_ = alias_tiles("rwx6", g.rstd.h[:, :], 512) + alias_tiles("rwx7", g.sq[0].h[:, :].bitcast(F32), 256) + \
        alias_tiles("rwx8", g.sq[1].h[:, :].bitcast(F32), 256)
    tpd = [Pool(A_[0:16]), Pool(C_[10:26])]
    thd = [[Pool(A_[16:23]), Pool(A_[23:26] + C_[0:4])], [Pool(C_[26:32] + D_[0:1]), Pool(D_[7:14])]]
    tqd = [[Pool(B_), Pool(C_[4:10])], [Pool(D_[1:7]), Pool(D_[14:16] + E_[0:4])]]
    wpd = [Pool(wp.t[0:2]), Pool(wp.t[2:4])]
    lpd = [Pool(lp.t[0:1]), Pool(lp.t[1:2])]
    pcs = [[PsCtx([0, 1]), PsCtx([2, 3])], [PsCtx([4, 5]), PsCtx([6, 7])]]
    pv = g.wi[2].all().re("p a b -> p (a b)").bitcast(F32)
    muT = pv[:, 0:48].re("p (s k) -> p s k", s=6)
    w0v = pv[:, 48:64].re("p (d c) -> p d c", d=2)
    a0v = pv[:, 64:80].re("p (d c) -> p d c", d=2)
    kkv = pv[:, 80:88]
    kav = pv[:, 88:96]
    rkv = pv[:, 96:104]
    lng = pv[:, 104:112]
    lnb = pv[:, 112:120]
    omk = pv[:, 120:128]
    nw0 = pv[:, 128:144].re("p (d c) -> p d c", d=2)
    mh = pv[:, 144:192].re("p (s k) -> p s k", s=6)
    mm1 = pv[:, 192:240].re("p (s k) -> p s k", s=6)
    nhalf = pv[:, 240:241]
    mids = [g.wo[0].all().re("p a b -> p (a b)")[:, 0:NT], g.wo[1].all().re("p a b -> p (a b)")[:, 0:NT],
            g.wi[2].all().re("p a b -> p (a b)")[:, 1024:1024 + NT]]
    wsl = g.wi[2].all().re("p a b -> p (a b)")[:, 2560:3584]
    P.dma("sp", MK.all(), g.dn_mk_d[:, 0:7, :], "c4")
    P.dma("sp", pv[:, 0:120], g.rw_pv_d.all(), "c4")
    P.ts("dve", omk, kav, -1.0, ALU.mult, 1.0, ALU.add)
    P.ts("dve", nw0.re("p d c -> p (d c)"), w0v.re("p d c -> p (d c)"), -1.0, ALU.mult)
    P.ts("dve", mh.re("p s k -> p (s k)"), muT.re("p s k -> p (s k)"), 0.5, ALU.mult)
    P.ts("dve", mm1.re("p s k -> p (s k)"), muT.re("p s k -> p (s k)"), -1.0, ALU.mult, 1.0, ALU.add)
    P.memset("dve", nhalf, -0.5)
    McumD = [MK[:, 0, :], MK[:, 1, :]]
    SMD = [MK[:, 4, :], MK[:, 5, :]]
    SMT = [MK[:, 5, :], MK[:, 4, :]]
    INCT = [MK[:, 0, :], MK[:, 1, :]]
    Mblk = MK[:, 6, :]
    g.psrot = [0, 1, 2, 3, 4, 5]
    BOUNDS = (0, 256, 512, NT)

    def shifted_proj(ps, wd, wh, tg, mrows=128):
        s0 = tg * 512
        for k in range(8):
            P.mm(ps[0:mrows, :], wd[:, k, :], g.hT[:, k, s0:s0 + 512], start=(k == 0), stop=False)
        segs = [(0, 256), (256, 512)] if tg == 0 else [(0, 512)]
        n = 0
        tot = 16 * len(segs)
        for (a_, b_) in segs:
            lo = a_ + (1 if (s0 + a_) in BOUNDS else 0)
            hi = b_ - (1 if (s0 + b_) in BOUNDS else 0)
            for k in range(8):
                n += 1
                P.mm(ps[0:mrows, lo:b_], wh[:, k, :], g.hT[:, k, s0 + lo - 1:s0 + b_ - 1], start=False, stop=False)
            for k in range(8):
                n += 1
                P.mm(ps[0:mrows, a_:hi], wh[:, k, :], g.hT[:, k, s0 + a_ + 1:s0 + hi + 1], start=False, stop=(n == tot))

    def scaled_w(dst_d, dst_h, src, si):
        P.tt("dve", dst_d, src, mm1[:, si, :].re("p (k o) -> p k o", o=1).bc([128, 8, 128]), ALU.mult)
        P.tt("dve", dst_h, src, mh[:, si, :].re("p (k o) -> p k o", o=1).bc([128, 8, 128]), ALU.mult)

    wl = g.wi[0]
    for i_, (src, si, fn) in enumerate(((g.rw_w1_d, 3, AF.Tanh), (g.rw_a1_d, 4, AF.Identity), (g.rw_g1_d, 5, AF.Sigmoid))):
        P.dma("pool", wl[:, :, 0:128], V(src.h.rearrange("(k p) n -> p k n", p=128), src.allkeys), "wi0")
        scaled_w(wl[:, :, 128:256], wl[:, :, 256:384], wl[:, :, 0:128], si)
        for tg in range(NG):
            ps = nextps(g)
            shifted_proj(ps, wl[:, :, 128:256], wl[:, :, 256:384], tg)
            P.act(mids[i_][:, tg * 512:(tg + 1) * 512], ps.all(), fn)
    wr_src = g.rw_wrkv_d.h
    for c in range(8):
        wb = g.wi[0]
        wsc = g.wi[1]
        for i_ in range(3):
            P.dma("pool", wb[:, :, i_ * 128:(i_ + 1) * 128],
                  V(wr_src[i_].rearrange("(k p) n -> p k n", p=128)[:, :, c * 128:(c + 1) * 128], g.rw_wrkv_d.allkeys), "wi0")
        P.dma("pool", wsl[:, 0:384].re("p (i n) -> p i n", i=3), V(g.rw_l2_d.h[:, :, c * 128:(c + 1) * 128], g.rw_l2_d.allkeys), "rw_l2")
        whs = [wsc[:, :, 384:512], wb[:, :, 384:512], g.rwh.all()]
        for i_ in range(3):
            scaled_w(wsc[:, :, i_ * 128:(i_ + 1) * 128], whs[i_], wb[:, :, i_ * 128:(i_ + 1) * 128], i_)
        for tg in range(NG):
            ts_ = slice(tg * 512, (tg + 1) * 512)
            for i_, dst in enumerate((rT, kT, vT)):
                ps = nextps(g)
                shifted_proj(ps, wsc[:, :, i_ * 128:(i_ + 1) * 128], whs[i_], tg)
                P.copy("act", dst[:, ts_], ps.all())
            for d_ in range(2):
                dr = slice(d_ * 64, (d_ + 1) * 64)
                ps = nextps(g)
                P.mm(ps.all(), wsl[dr, 0:128], mids[0][dr, ts_])
                t1 = g.ntmp[0]
                P.act(t1.all(), ps.all(), AF.Exp, bias=nw0[:, d_, c:c + 1], scale=-1.0)
                P.act(t1.all(), t1.all(), AF.Ln, bias=g.one1[:, 0:1], scale=1.0)
                P.act(ldT[d_][:, ts_], t1.all(), AF.Exp, bias=nhalf, scale=-1.0)
                ps = nextps(g)
                P.mm(ps.all(), wsl[dr, 128:256], mids[1][dr, ts_])
                P.act(aT[d_][:, ts_], ps.all(), AF.Sigmoid, bias=a0v[:, d_, c:c + 1], scale=1.0)
            ps = nextps(g)
            P.mm(ps.all(), wsl[:, 256:384], mids[2][:, ts_])
            P.copy("act", gT[:, ts_], ps.all())
            t2 = g.ntmp[1]
            P.ts("dve", t2.all(), kT[:, ts_], kkv[:, c:c + 1], ALU.mult)
            sq = g.sq[g.nsq % 2]
            g.nsq += 1
            P.act(sq.all(), t2.all(), AF.Square)
            pm = nextps(g)
            P.mm(pm.all(), g.ones2.all(), sq.all())
            P.act(g.rstd.all(), pm.all(), AF.Sqrt, bias=g.eps6[:, 0:1], scale=64.0)
            P.op("dve", "reciprocal", out=g.rstd.all(), in_=g.rstd.all())
            P.tt("dve", kkT[:, ts_], t2.all(), g.rstd.all(), ALU.mult)
        P.barrier()
        done = set()
        for si, (t0, nt) in enumerate(SEQS):
            Zc = [None, None]
            for d_ in range(2):
                Zc[d_] = Zb.get()
                if si < 2:
                    P.memset("dve", Zc[d_].all(), 0.0)
                else:
                    P.dma("sp", Zc[d_].all(), g.rw_z0_d[d_, c, :, :], "rw_z0")
            def hh_body(d_, t, hh, AR, KB, Vt, eC, atok, khtok, bhtok, L):
                pc = pcs[d_][hh]
                th = thd[d_][hh]
                hr = slice(hh * 64, (hh + 1) * 64)
                pN = nextq(g, pc)
                P.mm(pN, AR[hr, 0:128], KB[hr, 128:256])
                pB = nexth(g, pc)
                P.mm(pB[:, 0:256], KB[hr, 128:256], AR[hr, 0:256])
                yield
                N = th.get()
                P.tt("dve", N.all(), pN, SMD[d_], ALU.mult)
                Nt = th.get()
                P.tt("dve", Nt.all(), pB[:, 0:128], SMT[d_], ALU.mult)
                MrbT = th.get()
                P.tt("dve", MrbT.all(), pB[:, 128:256], INCT[d_], ALU.mult)
                pK = nexth(g, pc)
                P.mm(pK[:, 0:256], KB[hr, 0:128], AR[hr, 0:256])
                yield
                LakT = th.get()
                P.tt("dve", LakT.all(), pK[:, 0:128], SMT[d_], ALU.mult)
                MrkT = th.get()
                P.tt("dve", MrkT.all(), pK[:, 128:256], INCT[d_], ALU.mult)
                Tt = yield from neumann_solve_gen(g, N, Nt, tqd[d_][hh], pc=pc)
                X = th.get()
                pLV = nextq(g, pc)
                P.mm(pLV[:, 0:64], LakT.all(), Vt[:, hr])
                yield
                P.copy("act", X[:, 0:64], pLV[:, 0:64])
                P.copy("dve", X[:, 64:128], atok[:, hr])
                yield
                pUA = nextq(g, pc)
                P.mm(pUA, Tt.all(), X.all())
                yield
                UA = th.get()
                P.copy("act", UA.all(), pUA)
                yield
                pR = nextq(g, pc)
                P.mm(pR[hr, :], UA[:, 64:128], MrbT.all())
                pY = nextq(g, pc)
                P.mm(pY[:, 0:64], MrbT.all(), UA[:, 0:64], start=True, stop=False)
                P.mm(pY[:, 0:64], MrkT.all(), Vt[:, hr], start=False, stop=True)
                yield
                P.tt("dve", L[hr, 0, :], pR[hr, :], AR[hr, 128:256], ALU.add)
                P.copy("act", L[:, 1, hr], pY[:, 0:64])
                for c2 in range(2):
                    cr = slice(c2 * 64, (c2 + 1) * 64)
                    pP = nextq(g, pc)
                    P.mm(pP[hr, 0:64], UA[cr, 64:128], bhtok[cr, hr])
                    pZ = nextq(g, pc)
                    P.mm(pZ[hr, 0:64], bhtok[cr, hr], UA[cr, 0:64], start=True, stop=False)
                    P.mm(pZ[hr, 0:64], khtok[cr, hr], Vt[cr, hr], start=False, stop=True)
                    yield
                    P.stt("dve", L[hr, 2, cr], g.ident[hr, hr], eC[hr, c2 * 64:c2 * 64 + 1], pP[hr, 0:64], ALU.mult, ALU.add)
                    P.copy("act", L[hr, 3, cr], pZ[hr, 0:64])

            def rw_body(d_, step):
                tp = tpd[d_]
                pc = pcs[d_][0]
                t = t0 + step if d_ == 0 else t0 + nt - 1 - step
                tk = slice(t * 128, (t + 1) * 128)
                Vt = vtk.get()
                pv_ = nextq(g, pc)
                P.tr(pv_, vT[:, tk], g.ident.all())
                pl = nextq(g, pc)
                P.tr(pl, ldT[d_][:, tk], g.ident.all())
                yield
                P.copy("act", Vt.all(), pv_)
                ldtok = tp.get()
                P.copy("act", ldtok.all(), pl)
                yield
                pnl = nextq(g, pc)
                P.mm(pnl, ldtok.all(), McumD[d_])
                pnc = nextq(g, pc)
                P.mm(pnc, ldtok.all(), Mblk)
                yield
                nlc = tp.get()
                P.copy("act", nlc.all(), pnc)
                eC = tp.get()
                P.act(eC.all(), pnc, AF.Exp, scale=-1.0)
                epos = tp.get()
                P.act(epos.all(), pnl, AF.Exp, scale=-1.0)
                eneg = tp.get()
                P.act(eneg.all(), pnl, AF.Exp)
                tA = tp.get()
                P.tt("dve", tA.all(), pnl, ldT[d_][:, tk], ALU.subtract)
                yield
                eprev = tp.get()
                P.act(eprev.all(), tA.all(), AF.Exp, scale=-1.0)
                tB = tp.get()
                P.tt("dve", tB.all(), pnl, nlc.all(), ALU.subtract)
                kd = tp.get()
                P.ts("dve", kd.all(), aT[d_][:, tk], kav[:, c:c + 1], ALU.mult, omk[:, c:c + 1], ALU.add)
                P.tt("dve", kd.all(), kd.all(), kT[:, tk], ALU.mult)
                bv = tp.get()
                P.tt("dve", bv.all(), kkT[:, tk], aT[d_][:, tk], ALU.mult)
                yield
                ehat = tp.get()
                P.act(ehat.all(), tB.all(), AF.Exp)
                AR = wpd[d_].get()
                P.stt("dve", AR[:, 0:128], kkT[:, tk], -1.0, eprev.all(), ALU.mult, ALU.mult)
                P.tt("dve", AR[:, 128:256], rT[:, tk], epos.all(), ALU.mult)
                KB = wpd[d_].get()
                P.tt("dve", KB[:, 0:128], kd.all(), eneg.all(), ALU.mult)
                P.tt("dve", KB[:, 128:256], bv.all(), eneg.all(), ALU.mult)
                yield
                khat = tp.get()
                P.tt("dve", khat.all(), kd.all(), ehat.all(), ALU.mult)
                bhat = tp.get()
                P.tt("dve", bhat.all(), bv.all(), ehat.all(), ALU.mult)
                pts = []
                for src in (AR[:, 0:128], khat.all(), bhat.all()):
                    pt_ = nextq(g, pc)
                    P.tr(pt_, src, g.ident.all())
                    pts.append(pt_)
                yield
                toks = []
                for pt_ in pts:
                    tk_ = tp.get()
                    P.copy("act", tk_.all(), pt_)
                    toks.append(tk_)
                atok, khtok, bhtok = toks
                L = lpd[d_].get()
                yield from interleave_gen([hh_body(d_, t, hh, AR, KB, Vt, eC, atok, khtok, bhtok, L) for hh in range(2)])
                yield
                for c2 in ((0, 1) if d_ == 0 else (1, 0)):
                    cr = slice(c2 * 64, (c2 + 1) * 64)
                    Zn = Zb.get()
                    for hh in range(2):
                        hr = slice(hh * 64, (hh + 1) * 64)
                        py = nextq(g, pc)
                        P.mm(py[cr, 0:64], L[hr, 0, cr], Zc[d_][hr, :])
                        if (t, c2, hh) in done:
                            P.tt("dve", yacc[cr, t, hr], py[cr, 0:64], yacc[cr, t, hr], ALU.add)
                            P.tt("dve", yacc[cr, t, hr], yacc[cr, t, hr], L[cr, 1, hr], ALU.add)
                        else:
                            P.tt("dve", yacc[cr, t, hr], py[cr, 0:64], L[cr, 1, hr], ALU.add)
                            done.add((t, c2, hh))
                        pz = nextq(g, pc)
                        P.mm(pz[hr, 0:64], L[hr, 2, cr], Zc[d_][hr, :])
                        P.tt("dve", Zn[hr, :], pz[hr, 0:64], L[hr, 3, cr], ALU.add)
                    Zc[d_] = Zn
            for step in range(nt):
                interleave([rw_body(0, step), rw_body(1, step)])
            if si < 2:
                for d_ in range(2):
                    pzt = nextq(g)
                    P.tr(pzt[0:64, :], Zc[d_].all(), g.ident.all())
                    zo = tpd[0].get()
                    P.copy("act", zo[0:64, :], pzt[0:64, :])
                    P.dma("sp", V(g.rw_st_out.h[si, d_, 2 * c:2 * c + 2].rearrange("h v k -> v h k"), g.rw_st_out.allkeys),
                          zo[0:64, :].re("p (h k) -> p h k", h=2), "rw_so")
        P.barrier()
        P.dma("pool", wsl, V(g.rw_wout_d.h[c * 128:(c + 1) * 128, :], g.rw_wout_d.allkeys), "rw_wo")
        oTc = g.rw_oT
        for tg in range(NG):
            ts_ = slice(tg * 512, (tg + 1) * 512)
            ps = nextps(g)
            for j in range(4):
                t = tg * 4 + j
                P.tr(ps[:, j * 128:(j + 1) * 128], yacc[:, t, :], g.ident.all())
            yT_ = g.ntmp[0]
            P.copy("act", yT_.all(), ps.all())
            sqb = g.sq[g.nsq % 2]
            g.nsq += 1
            P.copy("dve", sqb.all(), yT_.all())
            pm = nextps(g)
            P.mm(pm.all(), g.ones2.all(), sqb.all())
            cen = g.ntmp[1]
            P.tt("dve", cen.all(), yT_.all(), pm.all(), ALU.subtract)
            sq2 = g.sq[g.nsq % 2]
            g.nsq += 1
            P.act(sq2.all(), cen.all(), AF.Square)
            pv2 = nextps(g)
            P.mm(pv2.all(), g.ones2.all(), sq2.all())
            P.act(g.rstd.all(), pv2.all(), AF.Sqrt, bias=g.eps_ln[:, 0:1], scale=1.0)
            P.op("dve", "reciprocal", out=g.rstd.all(), in_=g.rstd.all())
            P.tt("dve", cen.all(), cen.all(), g.rstd.all(), ALU.mult)
            P.ts("dve", cen.all(), cen.all(), lng[:, c:c + 1], ALU.mult, lnb[:, c:c + 1], ALU.add)
            rk2 = g.sg[0]
            P.ts("dve", rk2.all(), rT[:, ts_], rkv[:, c:c + 1], ALU.mult)
            pb_ = nextps(g)
            for d_ in range(2):
                kd2 = g.sg[1]
                P.ts("dve", kd2.all(), aT[d_][:, ts_], kav[:, c:c + 1], ALU.mult, omk[:, c:c + 1], ALU.add)
                P.tt("dve", kd2.all(), kd2.all(), kT[:, ts_], ALU.mult)
                pr_ = g.Ebuf[d_]
                P.tt("dve", pr_.all(), kd2.all(), rk2.all(), ALU.mult)
                P.mm(pb_.all(), g.onesblk.all(), pr_.all(), start=(d_ == 0), stop=(d_ == 1))
            bon = g.sg[0]
            P.tt("dve", bon.all(), pb_.all(), vT[:, ts_], ALU.mult)
            P.tt("dve", cen.all(), cen.all(), bon.all(), ALU.add)
            P.tt("dve", oTc[:, ts_], cen.all(), gT[:, ts_], ALU.mult)
        m = g.mod[l]
        for d in range(8):
            for tg in range(NG):
                cd = 0 if tg == 0 else 1
                ts_ = slice(tg * 512, (tg + 1) * 512)
                ps = nextps(g)
                P.mm(ps.all(), wsl[:, d * 128:(d + 1) * 128], oTc[:, ts_])
                P.stt("dve", g.xT[:, d, ts_], ps.all(), m[:, 2 * 8 + d, cd:cd + 1], g.xT[:, d, ts_], ALU.mult, ALU.add)
    g.psrot = list(range(8))
    P.barrier()


def adaln_pieces(g, l):
    P = g.P
    aw = [carve(g, "aw%d" % i, 67584 + i * 2048, [128, 8, 128], BF16) for i in range(2)]
    rb = [carve(g, "rb%d" % i, 71680 + i * 1024, [128, 256], F32) for i in range(2)]
    src = g.ada_w.h[l].rearrange("(k p) n -> p k n", p=128)
    pm = g.psb[7]
    m = g.mod[l]

    def piece(n):
        wb = aw[n % 2]
        P.dma("pool", wb.all(), V(src[:, :, n * 128:(n + 1) * 128], g.ada_w.allkeys), "aw%d" % (n % 2))
        for k in range(8):
            P.mm(pm[0:2, 128:256], g.sc[:, k, :], wb[:, k, :], start=(k == 0), stop=(k == 7))
        r = rb[n % 2]
        P.copy("act", r[0:2, 0:128], pm[0:2, 128:256])
        P.tr(pm[:, n * 2:(n + 1) * 2], r[0:2, 0:128], g.ident[0:2, 0:2])

    def finish():
        P.tt("dve", m.all(), pm[:, 0:96].re("p (j s) -> p j s", s=2),
             g.adab[:, l, :].re("p (j o) -> p j o", o=1).bc([128, 48, 2]), ALU.add)
        for w in range(2):
            sl = m[:, (3 * w + 1) * 8:(3 * w + 2) * 8, :]
            P.stt("dve", g.gs[l][:, w, :, :], sl, 1.0,
                  g.nrm[:, w, l, :].re("p (c o) -> p c o", o=1).bc([128, 8, 2]), ALU.add, ALU.mult)
    return piece, finish


def ffn(g, l, lnext=None):
    P = g.P
    wbufs(g)
    P.barrier()
    actT = carve(g, "actT", 0, [128, NF, NT], BF16, split=(1, 1))
    if lnext is not None:
        g.psrot = list(range(7))
        ad_piece, ad_finish = adaln_pieces(g, lnext)
    n_ad = 0
    win = g.ffn_w_in.h[l].rearrange("(k p) n -> p k n", p=128)
    m = g.mod[l]
    n_sg = 0
    for j in range(NF // 2):
        wb = g.wi[j % 3]
        P.dma("pool", wb[:, :, 0:256], V(win[:, :, j * 256:(j + 1) * 256], g.ffn_w_in.allkeys), "wi%d" % (j % 3))
        P.dma("pool", wb[:, :, 256:512], V(win[:, :, DFF + j * 256:DFF + (j + 1) * 256], g.ffn_w_in.allkeys), "wi%d" % (j % 3))
        for tg in range(NG):
            ts_ = slice(tg * 512, (tg + 1) * 512)
            for hf in range(2):
                f = j * 2 + hf
                pg = nextps(g)
                pu = nextps(g)
                for k in range(8):
                    P.mm(pg.all(), wb[:, k, hf * 128:(hf + 1) * 128], g.hT[:, k, ts_], start=(k == 0), stop=(k == 7))
                for k in range(8):
                    P.mm(pu.all(), wb[:, k, 256 + hf * 128:256 + (hf + 1) * 128], g.hT[:, k, ts_], start=(k == 0), stop=(k == 7))
                sg = g.sg[n_sg % 2]
                n_sg += 1
                P.act(sg.all(), pg.all(), AF.Silu)
                P.tt("dve", actT[:, f, ts_], sg.all(), pu.all(), ALU.mult)
        if lnext is not None:
            for _ in range(3):
                ad_piece(n_ad)
                n_ad += 1
    wout = g.ffn_w_out.h[l].rearrange("(f p) d -> p f d", p=128)
    for d in range(8):
        wo = g.wo[d % 2]
        P.dma("pool", wo.all(), V(wout[:, :, d * 128:(d + 1) * 128], g.ffn_w_out.allkeys), "wo%d" % (d % 2))
        for tg in range(NG):
            cd = 0 if tg == 0 else 1
            ts_ = slice(tg * 512, (tg + 1) * 512)
            ps = nextps(g)
            for f in range(NF):
                P.mm(ps.all(), wo[:, f, :], actT[:, f, ts_], start=(f == 0), stop=(f == NF - 1))
            P.stt("dve", g.xT[:, d, ts_], ps.all(), m[:, 5 * 8 + d, cd:cd + 1], g.xT[:, d, ts_], ALU.mult, ALU.add)
        if lnext is not None:
            for _ in range(2 if d < 7 else 1):
                ad_piece(n_ad)
                n_ad += 1
    if lnext is not None:
        assert n_ad == 48, n_ad
        ad_finish()
        g.psrot = list(range(8))
    P.barrier()


def na_tables(rel_bias):
    H = rel_bias.shape[0]
    kcol = np.arange(64)[:, None]
    qcol = np.arange(64)[None, :]
    cidx = np.clip(kcol - qcol + 15, 0, 30)
    tz = np.zeros((H, 2, 64, 31, 64), np.float32)
    for krl in range(2):
        for m in range(31):
            ridx = 22 - m + krl
            if 0 <= ridx <= 14:
                tz[:, krl, :, m, :] = rel_bias[:, ridx][:, cidx]
    return tz.reshape(H, 128, 31 * 64)


def na_cmask():
    qc = np.arange(64)
    ws = np.clip(qc - 8, 0, 48)
    kc = np.arange(64)
    colok = (kc[:, None] >= ws[None, :]) & (kc[:, None] < ws[None, :] + 16)
    cm = np.full((2, 6, 2, 64, 8, 64), -1e30, np.float32)
    for G in range(2):
        for ci in range(6):
            for krl in range(2):
                kr = 2 * (2 * G + ci) + krl
                for rq in range(8):
                    r = 8 * G + rq
                    rs = min(max(r - 4, 0), 8)
                    if rs <= kr < rs + 8:
                        cm[G, ci, krl, :, rq, :] = np.where(colok, 0.0, -1e30)
    return cm.reshape(12, 128, 512).transpose(1, 0, 2).copy()


def attn_norm_pair(g, ps, gain, out_bf, out_f32=None):
    P = g.P
    sq = g.sq[g.nsq % 2]
    g.nsq += 1
    P.act(sq.all(), ps.all(), AF.Square)
    pm = nextps(g)
    P.mm(pm.all(), g.ones2.all(), sq.all())
    P.act(g.rstd.all(), pm.all(), AF.Sqrt, bias=g.eps6[:, 0:1], scale=1.0)
    P.op("dve", "reciprocal", out=g.rstd.all(), in_=g.rstd.all())
    if out_f32 is not None:
        P.stt("dve", out_f32, ps.all(), gain, g.rstd.all(), ALU.mult, ALU.mult)
        P.copy("act", out_bf, out_f32)
    else:
        P.stt("dve", out_bf, ps.all(), gain, g.rstd.all(), ALU.mult, ALU.mult)


def attn_finish(g, po, pd, pb, n, out):
    P = g.P
    rc = g.ntmp[g.nrc % 2]
    g.nrc += 1
    P.op("dve", "reciprocal", out=rc[pb:pb + 64, 0:n], in_=pd[pb:pb + 64, 0:n])
    P.tt("dve", out, po[pb:pb + 64, 0:n], rc[pb:pb + 64, 0:n], ALU.mult)


def na_layer(g, l):
    P = g.P
    P.barrier()
    Vc = carve(g, "na_Vc", 0, [128, 2, 1024], BF16)
    oT = carve(g, "na_oT", 4096, [128, 8, NT], BF16, split=(1, 1))
    kcT = carve(g, "na_kcT", 28672, [128, 8, 256], BF16, split=(1, 1))
    CM = carve(g, "na_CM", 32768, [128, 12, 512], BF16)
    TZ = carve(g, "na_TZ", 45056, [128, 31, 64], BF16)
    qk = [[carve(g, "na_q%d" % i, 49152 + i * 6144, [128, NT], BF16, split=(1, 512)),
           carve(g, "na_k%d" % i, 49152 + i * 6144 + 3072, [128, NT], BF16, split=(1, 512))] for i in range(2)]
    Vp = [carve(g, "na_Vp%d" % i, 61440 + i * 3072, [128, 12, 128], BF16, split=(1, 1)) for i in range(2)]
    gn = carve(g, "na_gn", 67584, [128, 2], F32)
    P.dma("sp", gn.all(), g.na_gn_d.all(), "c1")
    P.ts("dve", gn[:, 0:1], gn[:, 0:1], 0.125, ALU.mult)
    P.dma("pool", CM.all(), g.na_cm_d.all(), "c1")
    g.psrot = [0, 1, 2, 3]
    kc_src = g.na_kc_d.h.rearrange("h t d -> t h d")
    vc_src = g.na_vc_d.h.rearrange("h t d -> t h d")
    for t in range(2):
        P.dma("pool", Vc[:, t, :].re("p (h d) -> p h d", h=16),
              V(vc_src[t * 128:(t + 1) * 128], g.na_vc_d.allkeys), "na_vc")
        for hf in range(2):
            st = g.ntmp[hf]
            P.dma("sp", st.all().re("p (h d) -> p h d", h=8),
                  V(kc_src[t * 128:(t + 1) * 128, hf * 8:(hf + 1) * 8, :], g.na_kc_d.allkeys), "na_kc%d" % hf)
            ps = nextps(g)
            for a in range(4):
                P.tr(ps[:, a * 128:(a + 1) * 128], st[:, a * 128:(a + 1) * 128], g.ident.all())
            P.copy("act", kcT[:, hf * 4:(hf + 1) * 4, t * 128:(t + 1) * 128], ps.all().re("p (a t) -> p a t", a=4))
    STOP = 99.0
    if STOP <= 1:
        return
    wsrc = g.na_w_qkv.h.rearrange("(k p) n -> p k n", p=128)
    wk_ = g.na_w_qkv.allkeys
    vout = g.na_vo.h.rearrange("s h t d -> s t h d")
    kout = g.na_ko.h.rearrange("s h t d -> s t h d")
    nb = 0
    for a in range(8):
        wb = g.wi[a % 3]
        for i_ in range(3):
            P.dma("pool", wb[:, :, i_ * 128:(i_ + 1) * 128], V(wsrc[:, :, i_ * 1024 + a * 128:i_ * 1024 + (a + 1) * 128], wk_), "wi%d" % (a % 3))
        qT, kT = qk[a % 2]
        Vt = Vp[a % 2]
        for t4 in range(3):
            ps = nextps(g)
            for tt_ in range(4):
                t = t4 * 4 + tt_
                for k in range(8):
                    P.mm(ps[:, tt_ * 128:(tt_ + 1) * 128], g.hT[:, k, t * 128:(t + 1) * 128], wb[:, k, 256:384], start=(k == 0), stop=(k == 7))
            P.copy("act", Vt[:, t4 * 4:(t4 + 1) * 4, :], ps.all().re("p (t c) -> p t c", t=4))
            if t4 == 0 and STOP > 1.2:
                st = g.sg[0]
                P.copy("dve", st.all(), ps.all())
                VAR = "0"
                for t in range(4):
                    if VAR == "0":
                        P.dma("sp", V(vout[t // 2, (t % 2) * 128:(t % 2 + 1) * 128, 2 * a:2 * a + 2, :], g.na_vo.allkeys),
                              st[:, t * 128:(t + 1) * 128].re("p (h d) -> p h d", h=2), "sg0")
                    elif VAR == "1":
                        pass
                    elif VAR == "2":
                        for hh_ in range(2):
                            P.dma("sp", V(g.na_vo.h[t // 2, 2 * a + hh_, (t % 2) * 128:(t % 2 + 1) * 128, :], g.na_vo.allkeys),
                                  st[:, t * 128 + hh_ * 64:t * 128 + (hh_ + 1) * 64], "sg0")
                    elif VAR == "3":
                        P.dma("pool", V(vout[t // 2, (t % 2) * 128:(t % 2 + 1) * 128, 2 * a:2 * a + 2, :], g.na_vo.allkeys),
                              st[:, t * 128:(t + 1) * 128].re("p (h d) -> p h d", h=2), "sg0")
        if STOP <= 1.4:
            return
        for tg in range(NG):
            ts_ = slice(tg * 512, (tg + 1) * 512)
            ps = nextps(g)
            for k in range(8):
                P.mm(ps.all(), wb[:, k, 0:128], g.hT[:, k, ts_], start=(k == 0), stop=(k == 7))
            attn_norm_pair(g, ps, gn[:, 0:1], qT[:, ts_])
            ps = nextps(g)
            for k in range(8):
                P.mm(ps.all(), wb[:, k, 128:256], g.hT[:, k, ts_], start=(k == 0), stop=(k == 7))
            if tg == 0 and STOP > 1.6:
                kf = g.sg[1]
                attn_norm_pair(g, ps, gn[:, 1:2], kT[:, ts_], out_f32=kf.all())
                pt = nextps(g)
                for t in range(4):
                    P.tr(pt[:, t * 128:(t + 1) * 128], kf[:, t * 128:(t + 1) * 128], g.ident.all())
                st = g.ntmp[0]
                P.copy("act", st.all(), pt.all())
                for t in range(4):
                    P.dma("sp", V(kout[t // 2, (t % 2) * 128:(t % 2 + 1) * 128, 2 * a:2 * a + 2, :], g.na_ko.allkeys),
                          st[:, t * 128:(t + 1) * 128].re("p (h d) -> p h d", h=2), "ntmp0")
            else:
                attn_norm_pair(g, ps, gn[:, 1:2], kT[:, ts_])
        if STOP <= 2:
            return
        for hh in range(2):
            h = 2 * a + hh
            pb = hh * 64
            P.dma("pool", TZ.all().re("p m q -> p (m q)"), g.na_tz_d[h, :, :], "na_tz")
            for s_ in range(2):
                po = g.psb[4 + 2 * (nb % 2)]
                pd = g.psb[5 + 2 * (nb % 2)]
                nb += 1
                qs = slice(s_ * 256, (s_ + 1) * 256)
                for kc in range(2):
                    ps = nextps(g)
                    tile = s_ * 2 + kc
                    P.mm(ps[:, 0:256], kT[pb:pb + 64, tile * 128:(tile + 1) * 128], qT[pb:pb + 64, qs])
                    E = g.Ebuf[g.nE % 3]
                    g.nE += 1
                    P.act(E[:, 0:256], ps[:, 0:256], AF.Exp)
                    P.mm(po[pb:pb + 64, 0:256], Vt[:, tile, pb:pb + 64], E[:, 0:256], start=(kc == 0), stop=(kc == 1))
                    P.mm(pd[pb:pb + 64, 0:256], g.onesb1[:, 0:64], E[:, 0:256], start=(kc == 0), stop=(kc == 1))
                attn_finish(g, po, pd, pb, 256, oT[pb:pb + 64, a, qs])
            if STOP <= 3:
                return
            for G in range(2):
                po = g.psb[4 + 2 * (nb % 2)]
                pd = g.psb[5 + 2 * (nb % 2)]
                nb += 1
                qs = slice(512 + G * 512, 512 + (G + 1) * 512)
                for ci in range(8):
                    ps = nextps(g)
                    if ci < 6:
                        tile = 4 + 2 * G + ci
                        kr0 = 2 * (2 * G + ci)
                        m0 = 15 - kr0 + 8 * G
                        P.mm(ps.all(), kT[pb:pb + 64, tile * 128:(tile + 1) * 128], qT[pb:pb + 64, qs], start=True, stop=False)
                        P.mm(ps.all(), g.identb.all(), CM[:, G * 6 + ci, :], start=False, stop=False)
                        P.mm(ps.all(), g.identb.all(), TZ[:, m0:m0 + 8, :].re("p m q -> p (m q)"), start=False, stop=True)
                        vv = Vt[:, tile, pb:pb + 64]
                    else:
                        P.mm(ps.all(), kcT[pb:pb + 64, a, (ci - 6) * 128:(ci - 5) * 128], qT[pb:pb + 64, qs])
                        vv = Vc[:, ci - 6, h * 64:(h + 1) * 64]
                    E = g.Ebuf[g.nE % 3]
                    g.nE += 1
                    P.act(E.all(), ps.all(), AF.Exp)
                    P.mm(po[pb:pb + 64, :], vv, E.all(), start=(ci == 0), stop=(ci == 7))
                    P.mm(pd[pb:pb + 64, :], g.onesb1[:, 0:64], E.all(), start=(ci == 0), stop=(ci == 7))
                attn_finish(g, po, pd, pb, 512, oT[pb:pb + 64, a, qs])
            if STOP <= 4:
                return
    g.psrot = list(range(8))
    out_proj(g, l, g.na_w_out.h.rearrange("(c p) d -> p c d", p=128), g.na_w_out.allkeys, oT)
    P.barrier()


def out_proj(g, l, wsrc, wkeys, oT):
    P = g.P
    m = g.mod[l]
    for d in range(8):
        wo = g.wo[d % 2]
        P.dma("pool", wo[:, 0:8, :], V(wsrc[:, :, d * 128:(d + 1) * 128], wkeys), "wo%d" % (d % 2))
        for tg in range(NG):
            cd = 0 if tg == 0 else 1
            ts_ = slice(tg * 512, (tg + 1) * 512)
            ps = nextps(g)
            for c in range(8):
                P.mm(ps.all(), wo[:, c, :], oT[:, c, ts_], start=(c == 0), stop=(c == 7))
            P.stt("dve", g.xT[:, d, ts_], ps.all(), m[:, 2 * 8 + d, cd:cd + 1], g.xT[:, d, ts_], ALU.mult, ALU.add)


def _lay_vec(v):
    return np.ascontiguousarray(v.reshape(8, 128).T)


def make_inputs(core, inp):
    f = np.float32
    x = np.concatenate([inp["x_prompt"][2 * core], inp["x_prompt"][2 * core + 1], inp["x_sample"][core]], 0)
    cv = np.stack([_lay_vec(inp["c_ctx"]), _lay_vec(inp["c"][core])], -1)
    adabT = np.ascontiguousarray(inp["ada_b"].reshape(DEPTH, 48, 128).transpose(2, 0, 1))
    nrmT = np.stack([inp["norm_mix"].reshape(DEPTH, 8, 128).transpose(2, 0, 1),
                     inp["norm_ffn"].reshape(DEPTH, 8, 128).transpose(2, 0, 1)], 1)
    return {
        "x": np.ascontiguousarray(x, f), "cv": np.ascontiguousarray(cv, f),
        "ada_w": inp["ada_w"], "adabT": adabT.astype(f), "nrmT": np.ascontiguousarray(nrmT, f),
        "ffn_w_in": inp["ffn_w_in"], "ffn_w_out": inp["ffn_w_out"],
        "ident": np.eye(128, dtype=f),
        "na_w_qkv": inp["na_w_qkv"][0], "na_w_out": inp["na_w_out"][0],
        "na_gn": np.ascontiguousarray(np.stack([np.tile(inp["na_q_norm"][0], 2), np.tile(inp["na_k_norm"][0], 2)], -1), f),
        "na_cm": na_cmask(), "na_tz": na_tables(inp["na_rel_bias"][0]),
        "na_kc": np.ascontiguousarray(inp["cache_na_k"][core, 0]), "na_vc": np.ascontiguousarray(inp["cache_na_v"][core, 0]),
        **mla_inputs(core, inp), **dn_inputs(core, inp), **rw_inputs(core, inp),
    }


def rw_inputs(core, inp):
    f = np.float32
    pv = np.zeros((128, 120), f)
    pv[:, 0:48] = inp["rw_mu"][0].reshape(6, 8, 128).transpose(2, 0, 1).reshape(128, 48)
    pv[:, 48:64] = inp["rw_w0"][0].reshape(2, 8, 128).transpose(2, 0, 1).reshape(128, 16)
    pv[:, 64:80] = inp["rw_a0"][0].reshape(2, 8, 128).transpose(2, 0, 1).reshape(128, 16)
    pv[:, 80:88] = inp["rw_k_k"][0].reshape(8, 128).T
    pv[:, 88:96] = inp["rw_k_a"][0].reshape(8, 128).T
    pv[:, 96:104] = inp["rw_r_k"][0].reshape(8, 128).T
    pv[:, 104:112] = inp["rw_ln_g"][0].reshape(8, 128).T
    pv[:, 112:120] = inp["rw_ln_b"][0].reshape(8, 128).T
    cat = lambda w: np.ascontiguousarray(np.concatenate([w[0], w[1]], 1), f)
    l2 = np.stack([inp["rw_w2"][0].reshape(128, D), inp["rw_a2"][0].reshape(128, D), inp["rw_g2"][0]], 1)
    z0 = inp["state_rwkv"][core, 0].transpose(0, 1, 3, 2).reshape(2, 8, 128, 64)
    return {
        "rw_pv": pv, "rw_wrkv": inp["rw_w_rkv"][0], "rw_w1": cat(inp["rw_w1"][0]), "rw_a1": cat(inp["rw_a1"][0]),
        "rw_g1": inp["rw_g1"][0], "rw_l2": np.ascontiguousarray(l2, f), "rw_wout": inp["rw_w_out"][0],
        "rw_z0": np.ascontiguousarray(z0, f), "dn_mk": chunk_masks(),
    }


def dn_inputs(core, inp):
    f = np.float32
    wg = inp["dn_w_gate"][0].reshape(D, 2, 2, 8).transpose(0, 2, 1, 3).reshape(D, 32)
    vec = np.concatenate([inp["dn_a_log"][0].reshape(16), inp["dn_dt_bias"][0].reshape(16)])
    cw = inp["dn_conv"][0].reshape(3, 3, 8, 128).transpose(3, 2, 1, 0).reshape(128, 72)
    return {
        "dn_w_in": inp["dn_w_in"][0], "dn_wg": np.ascontiguousarray(wg, f),
        "dn_vec": np.ascontiguousarray(np.tile(vec[None], (128, 1)), f), "dn_cw": np.ascontiguousarray(cw, f),
        "dn_gout": np.ascontiguousarray(inp["dn_out_norm"][0].reshape(128, 1), f), "dn_mk": chunk_masks(),
        "dn_w_out": inp["dn_w_out"][0], "dn_s0": np.ascontiguousarray(inp["state_dn"][core, 0]),
    }


def mla_inputs(core, inp):
    f = np.float32
    wa = inp["mla_w_a"][0]
    wa_pad = np.concatenate([wa[:, :640], np.zeros((D, 64), f), wa[:, 640:672]], 1)
    wkv = inp["mla_w_kv_b"][0].reshape(256, 16, 128)
    gn = np.zeros((128, 8), f)
    gn[:, 0:3] = inp["mla_q_a_norm"][0].reshape(3, 128).T
    gn[:, 3:5] = inp["mla_kv_a_norm"][0].reshape(2, 128).T
    gn[:96, 5] = inp["mla_q_norm"][0]
    gn[:96, 6] = inp["mla_k_norm"][0]
    cos, sin, R = mla_consts()
    kpec = np.concatenate([np.zeros((256, 64), f), inp["cache_mla_kpe"][core, 0]], 1)
    return {
        "ml_wa": np.ascontiguousarray(wa_pad, f), "ml_wqb": inp["mla_w_q_b"][0],
        "ml_wk": np.ascontiguousarray(wkv[:, :, :64].reshape(256, 1024)), "ml_wv": np.ascontiguousarray(wkv[:, :, 64:].reshape(256, 1024)),
        "ml_wout": inp["mla_w_out"][0], "ml_gn": gn, "ml_cos": cos, "ml_sin": sin, "ml_R": R,
        "ml_ckvc": np.ascontiguousarray(inp["cache_mla_ckv"][core, 0]), "ml_kpec": np.ascontiguousarray(kpec, f),
    }


def kernel(**inp):
    inp = {k: np.asarray(v) for k, v in inp.items()}
    nc = build()
    in_maps = []
    for i in range(8):
        im = make_inputs(i, inp)
        in_maps.append({k: v for k, v in im.items() if k in nc.in_names})
    res = run_bass_kernel_spmd(nc, in_maps, core_ids=list(range(8)))
    r = res.results
    f = np.float32
    yp = np.stack([r[i // 2]["y"][(i % 2) * 256:(i % 2 + 1) * 256] for i in range(16)], 0).astype(f)
    ys = np.stack([r[i]["y"][512:] for i in range(8)], 0).astype(f)
    cat = lambda k: np.concatenate([np.asarray(r[i][k]) for i in range(8)], 0).astype(f)
    st_dn = cat("dn_st_out")[:, None]
    na_k = cat("na_ko")[:, None]
    na_v = cat("na_vo")[:, None]
    ckv = cat("ckv_out")[:, None]
    kpe = cat("kpe_out")[:, None]
    st_rw = cat("rw_st_out")[:, None]
    return yp, ys, st_dn, na_k, na_v, ckv, kpe, st_rw
```

```python
import numpy as np
import concourse.bass as bass
import concourse.mybir as mybir
from concourse.bass_utils import run_bass_kernel_spmd
from contextlib import ExitStack

F32 = mybir.dt.float32
BF16 = mybir.dt.bfloat16
I32 = mybir.dt.int32
AF = mybir.ActivationFunctionType
ALU = mybir.AluOpType
AX = mybir.AxisListType


class V:
    __slots__ = ("ap", "keys")

    def __init__(self, ap, keys):
        self.ap = ap
        self.keys = keys

    def __getitem__(self, idx):
        return V(self.ap[idx], self.keys)

    def re(self, pat, **kw):
        return V(self.ap.rearrange(pat, **kw), self.keys)

    def bc(self, shape):
        return V(self.ap.broadcast_to(shape), self.keys)

    def bitcast(self, dt):
        return V(self.ap.bitcast(dt), self.keys)

    def all(self):
        return self

    @property
    def shape(self):
        return self.ap.shape


class T:
    def __init__(self, name, handle, shape, split=None):
        self.name = name
        self.h = handle
        self.shape = tuple(shape)
        self.split = split
        if split is None:
            self.allkeys = frozenset([(name, 0)])
        else:
            ax, blk = split
            self.allkeys = frozenset((name, i) for i in range((shape[ax] + blk - 1) // blk))

    def __getitem__(self, idx):
        if not isinstance(idx, tuple):
            idx = (idx,)
        keys = self.allkeys
        if self.split is not None:
            ax, blk = self.split
            if ax < len(idx):
                ix = idx[ax]
                if isinstance(ix, int):
                    keys = frozenset([(self.name, ix // blk)])
                elif isinstance(ix, slice):
                    st = 0 if ix.start is None else ix.start
                    sp = self.shape[ax] if ix.stop is None else ix.stop
                    keys = frozenset((self.name, i) for i in range(st // blk, (sp - 1) // blk + 1))
        return V(self.h[idx], keys)

    def all(self):
        return V(self.h[tuple(slice(None) for _ in self.shape)], self.allkeys)


class Op:
    __slots__ = ("eng", "fn", "deps", "ddeps", "signal", "count", "dma", "idx", "cost", "alldeps")


class Prog:
    ENG = ("pe", "act", "dve", "pool", "sp")

    def __init__(self, nc):
        self.nc = nc
        self.es = ExitStack()
        self.streams = {e: [] for e in self.ENG}
        self.res = {}
        self.dsem_count = {}
        self.dsem_waited = {}
        self.psum_names = set()
        self.bank_readers = {}
        self.in_names = []
        self.ops = []
        self.last_dma = {}
        self.last_dma_sem = {}
        self.n_t = 0

    def sb(self, name, shape, dtype, split=None):
        h = self.es.enter_context(self.nc.sbuf_tensor(name, list(shape), dtype))
        return T(name, h, shape, split)

    def ps(self, name, shape, dtype, split=None):
        h = self.es.enter_context(self.nc.psum_tensor(name, list(shape), dtype))
        self.psum_names.add(name)
        return T(name, h, shape, split)

    def dram(self, name, shape, dtype, kind, split=None):
        h = self.nc.dram_tensor(name, list(shape), dtype, kind=kind).ap()
        if kind == "ExternalInput":
            self.in_names.append(name)
        return T(name, h, shape, split)

    def _st(self, k):
        s = self.res.get(k)
        if s is None:
            s = self.res[k] = [None, []]
        return s

    def _record(self, eng, fn, reads, writes, dma=None, cost=0.3):
        op = Op()
        op.eng, op.fn, op.signal, op.count, op.dma = eng, fn, False, 0, dma
        op.cost = cost
        deps = {}
        ddeps = {}
        alld = {}

        def add(ev):
            if ev is None:
                return
            alld[id(ev)] = ev
            if ev.dma is not None:
                s = ev.dma
                ld_ = self.last_dma_sem[s]
                alld[id(ld_)] = ld_
                v = self.dsem_count[s]
                if ddeps.get(s, 0) < v:
                    ddeps[s] = v
                if self.dsem_waited.get(s, 0) < v:
                    self.dsem_waited[s] = v
            else:
                if ev.eng == "pe" and eng == "pe" and dma is None:
                    return
                deps[id(ev)] = ev

        rk = set()
        for v in reads:
            rk |= v.keys
        wk = set()
        for v in writes:
            wk |= v.keys
        for k in rk:
            add(self._st(k)[0])
        for k in wk:
            s = self._st(k)
            add(s[0])
            for r in s[1]:
                add(r)
        for k in rk:
            if k[0] in self.psum_names:
                r = self.bank_readers.get(k[0])
                if r is not None:
                    if r.eng != eng:
                        add(r)
                    else:
                        alld[id(r)] = r
        op.deps = list(deps.values())
        op.ddeps = ddeps
        for d in op.deps:
            d.signal = True
        for k in rk:
            if k[0] in self.psum_names and dma is None:
                self.bank_readers[k[0]] = op
        if dma is not None:
            pw = self.dsem_waited.get(dma, 0)
            if pw > ddeps.get(dma, 0):
                ddeps[dma] = pw
            c = self.dsem_count.get(dma, 0) + 16
            self.dsem_count[dma] = c
            pq_ = self.last_dma.get(eng)
            if pq_ is not None:
                alld[id(pq_)] = pq_
            self.last_dma[eng] = op
            self.last_dma_sem[dma] = op
        op.alldeps = list(alld.values())
        for k in rk:
            if k not in wk:
                self._st(k)[1].append(op)
        for k in wk:
            s = self._st(k)
            s[0] = op
            s[1] = []
        op.idx = len(self.ops)
        self.ops.append(op)
        return op

    def op(self, eng, name, *, out=None, accum_out=None, extra_reads=(), extra_writes=(), cost_=None, **kw):
        reads = list(extra_reads)
        writes = list(extra_writes)
        args = {}
        for k, v in kw.items():
            if isinstance(v, V):
                reads.append(v)
                args[k] = v.ap
            else:
                args[k] = v
        if out is not None:
            writes.append(out)
            args["out"] = out.ap
        if accum_out is not None:
            writes.append(accum_out)
            args["accum_out"] = accum_out.ap

        def fn(e):
            return getattr(e, name)(**args)

        if cost_ is None:
            ref = out if out is not None else reads[0]
            n = int(np.prod(ref.ap.shape[1:]))
            if eng == "pe":
                cost_ = 0.3
            else:
                cost_ = 0.2 + n / 960.0
        return self._record(eng, fn, reads, writes, cost=cost_)

    def dma(self, q, out, in_, sem, **kw):
        def fn(e):
            return e.dma_start(out=out.ap, in_=in_.ap, **kw)

        nb = int(np.prod(out.ap.shape)) * 4
        return self._record(q, fn, [in_], [out], dma=sem + "_" + q, cost=nb / 150e3)

    def barrier(self):
        self.ops.append(("BARRIER", dict(self.dsem_count)))

    def mode(self, reorder):
        self.ops.append(("MODE", reorder))

    def schedule(self, reorder=True):
        import heapq
        LAT = 0.25
        streams = {e: [] for e in self.ENG}
        seg = []
        reorder0 = reorder

        def flush(seg):
            if not seg:
                return
            if not reorder:
                for o in seg:
                    streams[o.eng].append(o)
                return
            inseg = {id(o) for o in seg}
            indeg = {}
            succ = {}
            ready_t = {}
            for o in seg:
                n = 0
                for d in o.alldeps:
                    if id(d) in inseg:
                        n += 1
                        succ.setdefault(id(d), []).append(o)
                indeg[id(o)] = n
                ready_t[id(o)] = 0.0
            cp = {}
            for o in reversed(seg):
                m = 0.0
                for s_ in succ.get(id(o), ()):
                    v = cp[id(s_)]
                    if v > m:
                        m = v
                cp[id(o)] = m + o.cost + LAT
            future = {e: [] for e in self.ENG}
            avail = {e: [] for e in self.ENG}
            for o in seg:
                if indeg[id(o)] == 0:
                    heapq.heappush(avail[o.eng], (-cp[id(o)], o.idx, o))
            free = {e: 0.0 for e in self.ENG}
            left = len(seg)
            while left:
                best = None
                for e in self.ENG:
                    fu, av = future[e], avail[e]
                    while fu and fu[0][0] <= free[e]:
                        _, i_, o_ = heapq.heappop(fu)
                        heapq.heappush(av, (-cp[id(o_)], i_, o_))
                    if av:
                        cand = (free[e], av[0][0], e, 0)
                    elif fu:
                        cand = (fu[0][0], -cp[id(fu[0][2])], e, 1)
                    else:
                        continue
                    if best is None or cand[:2] < best[:2]:
                        best = cand
                st_, _, e, src = best
                if src == 0:
                    _, _, o = heapq.heappop(avail[e])
                else:
                    _, _, o = heapq.heappop(future[e])
                left -= 1
                streams[e].append(o)
                if o.dma is not None:
                    free[e] = st_ + 0.1
                    fin = st_ + 2.0 + o.cost
                else:
                    fin = st_ + o.cost
                    free[e] = fin
                for s_ in succ.get(id(o), ()):
                    k = id(s_)
                    if ready_t[k] < fin + LAT:
                        ready_t[k] = fin + LAT
                    indeg[k] -= 1
                    if indeg[k] == 0:
                        heapq.heappush(future[s_.eng], (ready_t[k], s_.idx, s_))

        for it in self.ops:
            if isinstance(it, tuple) and it[0] == "MODE":
                flush(seg)
                seg = []
                reorder = it[1] and reorder0
                continue
            if isinstance(it, tuple):
                flush(seg)
                seg = []
                lasts = []
                for e in self.ENG:
                    for o in reversed(streams[e]):
                        if o.dma is None and o.fn is not None:
                            lasts.append(o)
                            break
                for e in self.ENG:
                    b_ = Op()
                    b_.eng, b_.fn, b_.signal, b_.count, b_.dma = e, None, False, 0, None
                    b_.deps = [x for x in lasts if x.eng != e]
                    b_.ddeps = dict(it[1])
                    for d in b_.deps:
                        d.signal = True
                    streams[e].append(b_)
            else:
                seg.append(it)
        flush(seg)
        self.streams = streams

    def emit(self, final_wait=True, reorder=True):
        self.schedule(reorder)
        nc = self.nc
        es = self.es
        esem = {e: es.enter_context(nc.semaphore("sem_" + e)) for e in self.ENG}
        dsem = {n: es.enter_context(nc.semaphore("d_" + n)) for n in self.dsem_count}
        for e in self.ENG:
            c = 0
            for op in self.streams[e]:
                if op.signal and op.dma is None:
                    c += 1
                    op.count = c
        block = es.enter_context(nc.Block())
        streams = self.streams
        dcount = self.dsem_count

        def run(ename):
            def body(eng):
                waited = {}
                for op in streams[ename]:
                    w = {}
                    for d in op.deps:
                        s = esem[d.eng]
                        if w.get(s, (0,))[0] < d.count:
                            w[s] = (d.count, s)
                    for sn, v in op.ddeps.items():
                        s = dsem[sn]
                        if w.get(s, (0,))[0] < v:
                            w[s] = (v, s)
                    for s, (v, _) in w.items():
                        if waited.get(s, 0) < v:
                            eng.wait_ge(s, v)
                            waited[s] = v
                    if op.fn is None:
                        continue
                    ins = op.fn(eng)
                    if op.dma is not None:
                        ins.then_inc(dsem[op.dma], 16)
                    elif op.signal:
                        ins.then_inc(esem[ename], 1)
                if ename == "sp" and final_wait:
                    for sn, c in dcount.items():
                        eng.wait_ge(dsem[sn], c)
            return body

        block.tensor(run("pe"))
        block.scalar(run("act"))
        block.vector(run("dve"))
        block.gpsimd(run("pool"))
        block.sync(run("sp"))
        es.close()

    def mm(self, out, lhsT, rhs, start=True, stop=True, **kw):
        n = int(np.prod(rhs.ap.shape[1:]))
        c = 0.04 + n / 2400.0 * (4.0 if rhs.ap.dtype == F32 else 1.0)
        return self.op("pe", "matmul", out=out, lhsT=lhsT, rhs=rhs, start=start, stop=stop, cost_=c, **kw)

    def tr(self, out, in_, ident):
        return self.op("pe", "transpose", out=out, in_=in_, identity=ident)

    def act(self, out, in_, func, eng="act", **kw):
        return self.op(eng, "activation", out=out, in_=in_, func=func, **kw)

    def tt(self, eng, out, in0, in1, op):
        return self.op(eng, "tensor_tensor", out=out, in0=in0, in1=in1, op=op)

    def ts(self, eng, out, in0, s1, op0, s2=None, op1=None, **kw):
        if op1 is None:
            return self.op(eng, "tensor_scalar", out=out, in0=in0, scalar1=s1, scalar2=s2, op0=op0, **kw)
        return self.op(eng, "tensor_scalar", out=out, in0=in0, scalar1=s1, scalar2=s2, op0=op0, op1=op1, **kw)

    def stt(self, eng, out, in0, scalar, in1, op0, op1, **kw):
        return self.op(eng, "scalar_tensor_tensor", out=out, in0=in0, scalar=scalar, in1=in1, op0=op0, op1=op1, **kw)

    def copy(self, eng, out, in_):
        if eng == "act":
            return self.op("act", "copy", out=out, in_=in_)
        return self.op(eng, "tensor_copy", out=out, in_=in_)

    def memset(self, eng, out, val):
        def fn(e):
            return e.memset(out.ap, val)
        return self._record(eng, fn, [], [out])


D = 1024
NT = 1536
NG = 3
DFF = 2816
NF = 22
DEPTH = 4
ARENA = 72 * 1024
REORDER = True


class Ctx:
    pass


def build(layers=(0, 1, 2, 3), mixers=True):
    nc = bass.Bass("TRN2", target_bir_lowering=False)
    P = Prog(nc)
    g = Ctx()
    g.P = P
    g.nc = nc
    g.x = P.dram("x", [NT, D], F32, "ExternalInput")
    g.y = P.dram("y", [NT, D], F32, "ExternalOutput")
    g.cv = P.dram("cv", [128, 8, 2], F32, "ExternalInput")
    g.ada_w = P.dram("ada_w", [DEPTH, D, 6 * D], F32, "ExternalInput")
    g.adabT = P.dram("adabT", [128, DEPTH, 48], F32, "ExternalInput")
    g.nrmT = P.dram("nrmT", [128, 2, DEPTH, 8], F32, "ExternalInput")
    g.ffn_w_in = P.dram("ffn_w_in", [DEPTH, D, 2 * DFF], F32, "ExternalInput")
    g.ffn_w_out = P.dram("ffn_w_out", [DEPTH, DFF, D], F32, "ExternalInput")
    g.identd = P.dram("ident", [128, 128], F32, "ExternalInput")
    g.xT = P.sb("xT", [128, 8, NT], F32, split=(2, 512))
    g.hT = P.sb("hT", [128, 8, NT], BF16, split=(2, 512))
    g.ident = P.sb("identS", [128, 128], F32)
    g.identb = P.sb("identB", [128, 128], BF16)
    g.onesb = P.sb("onesb", [128, 128], BF16)
    g.sc = P.sb("sc", [128, 8, 2], BF16)
    g.cvs = P.sb("cvs", [128, 8, 2], F32)
    g.adab = P.sb("adab", [128, DEPTH, 48], F32)
    g.nrm = P.sb("nrm", [128, 2, DEPTH, 8], F32)
    g.mod = [P.sb("mod%d" % l, [128, 48, 2], F32) for l in range(DEPTH)]
    g.gs = [P.sb("gs%d" % l, [128, 2, 8, 2], F32) for l in range(DEPTH)]
    g.arena = P.sb("arena", [128, ARENA // 4], F32)
    g.psb = [P.ps("ps%d" % i, [128, 512], F32) for i in range(8)]
    g.pq = 0
    g.pi = 0
    g.psrot = list(range(8))
    g.nsq = g.nrc = g.nE = 0
    g.ones2 = P.sb("ones2", [128, 128], BF16)
    g.onesb1 = P.sb("onesb1", [128, 128], BF16)
    g.Ebuf = [P.sb("Ebuf%d" % i, [128, 512], BF16) for i in range(3)]
    if 1 in layers and mixers:
        g.na_w_qkv = P.dram("na_w_qkv", [D, 3 * D], F32, "ExternalInput")
        g.na_w_out = P.dram("na_w_out", [D, D], F32, "ExternalInput")
        g.na_gn_d = P.dram("na_gn", [128, 2], F32, "ExternalInput")
        g.na_cm_d = P.dram("na_cm", [128, 12, 512], F32, "ExternalInput")
        g.na_tz_d = P.dram("na_tz", [16, 128, 31 * 64], F32, "ExternalInput")
        g.na_kc_d = P.dram("na_kc", [16, 256, 64], F32, "ExternalInput")
        g.na_vc_d = P.dram("na_vc", [16, 256, 64], F32, "ExternalInput")
        g.na_ko = P.dram("na_ko", [2, 16, 256, 64], F32, "ExternalOutput")
        g.na_vo = P.dram("na_vo", [2, 16, 256, 64], F32, "ExternalOutput")
    if 0 in layers and mixers:
        g.dn_w_in = P.dram("dn_w_in", [D, 4 * D], F32, "ExternalInput")
        g.dn_wg_d = P.dram("dn_wg", [D, 32], F32, "ExternalInput")
        g.dn_vec_d = P.dram("dn_vec", [128, 32], F32, "ExternalInput")
        g.dn_cw_d = P.dram("dn_cw", [128, 72], F32, "ExternalInput")
        g.dn_gout_d = P.dram("dn_gout", [128, 1], F32, "ExternalInput")
        g.dn_mk_d = P.dram("dn_mk", [128, 11, 128], F32, "ExternalInput")
        g.dn_w_out = P.dram("dn_w_out", [D, D], F32, "ExternalInput")
        g.dn_s0_d = P.dram("dn_s0", [2, 8, 128, 128], F32, "ExternalInput")
        g.dn_st_out = P.dram("dn_st_out", [2, 2, 8, 128, 128], F32, "ExternalOutput")
    if 3 in layers and mixers:
        if not hasattr(g, "dn_mk_d"):
            g.dn_mk_d = P.dram("dn_mk", [128, 11, 128], F32, "ExternalInput")
        g.rw_pv_d = P.dram("rw_pv", [128, 120], F32, "ExternalInput")
        g.rw_wrkv_d = P.dram("rw_wrkv", [3, D, D], F32, "ExternalInput")
        g.rw_w1_d = P.dram("rw_w1", [D, 128], F32, "ExternalInput")
        g.rw_a1_d = P.dram("rw_a1", [D, 128], F32, "ExternalInput")
        g.rw_g1_d = P.dram("rw_g1", [D, 128], F32, "ExternalInput")
        g.rw_l2_d = P.dram("rw_l2", [128, 3, D], F32, "ExternalInput")
        g.rw_wout_d = P.dram("rw_wout", [D, D], F32, "ExternalInput")
        g.rw_z0_d = P.dram("rw_z0", [2, 8, 128, 64], F32, "ExternalInput")
        g.rw_st_out = P.dram("rw_st_out", [2, 2, 16, 64, 64], F32, "ExternalOutput")
        g.rw_oT = P.sb("rw_oT", [128, NT], BF16, split=(1, 512))
        g.rwh = P.sb("rw_wh", [128, 8, 128], BF16)
        g.onesblk = P.sb("onesblk", [128, 128], BF16)
        g.eps_ln = P.sb("eps_ln", [128, 2], F32)
        P.memset("dve", g.eps_ln.all(), 64e-5)
        P.memset("dve", g.onesblk.all(), 0.0)
        P.memset("dve", g.onesblk[0:64, 0:64], 1.0)
        P.memset("dve", g.onesblk[64:128, 64:128], 1.0)
    if 2 in layers and mixers:
        g.ml_wa = P.dram("ml_wa", [D, 736], F32, "ExternalInput")
        g.ml_wqb = P.dram("ml_wqb", [384, 1536], F32, "ExternalInput")
        g.ml_wk = P.dram("ml_wk", [256, 1024], F32, "ExternalInput")
        g.ml_wv = P.dram("ml_wv", [256, 1024], F32, "ExternalInput")
        g.ml_wout = P.dram("ml_wout", [D, D], F32, "ExternalInput")
        g.ml_gn_d = P.dram("ml_gn", [128, 8], F32, "ExternalInput")
        g.ml_cos_d = P.dram("ml_cos", [96, 1024], F32, "ExternalInput")
        g.ml_sin_d = P.dram("ml_sin", [96, 1024], F32, "ExternalInput")
        g.ml_R_d = P.dram("ml_R", [96, 96], F32, "ExternalInput")
        g.ml_ckvc = P.dram("ml_ckvc", [256, 256], F32, "ExternalInput")
        g.ml_kpec = P.dram("ml_kpec", [256, 96], F32, "ExternalInput")
        g.ckv_out = P.dram("ckv_out", [2, 256, 256], F32, "ExternalOutput")
        g.kpe_out = P.dram("kpe_out", [2, 256, 32], F32, "ExternalOutput")

    P.dma("sp", g.ident.all(), g.identd.all(), "c0")
    P.dma("sp", g.cvs.all(), g.cv.all(), "c0")
    P.dma("sp", g.adab.all(), g.adabT.all(), "c0")
    P.dma("sp", g.nrm.all(), g.nrmT.all(), "c0")
    P.copy("dve", g.identb.all(), g.ident.all())
    P.memset("dve", g.onesb.all(), 1.0 / D)
    P.memset("dve", g.onesb1.all(), 1.0)
    P.memset("dve", g.ones2.all(), 0.0)
    P.memset("dve", g.ones2[0:64, 0:64], 1.0 / 64)
    P.memset("dve", g.ones2[64:128, 64:128], 1.0 / 64)
    g.sq = [P.sb("sq%d" % i, [128, 512], BF16) for i in range(2)]
    g.rstd = P.sb("rstd", [128, 512], F32)
    g.ntmp = [P.sb("ntmp%d" % i, [128, 512], F32) for i in range(2)]
    wbufs(g)
    g.one1 = P.sb("one1", [128, 2], F32)
    P.memset("dve", g.one1.all(), 1.0)
    g.eps6 = P.sb("eps6", [128, 2], F32)
    P.memset("dve", g.eps6.all(), 1e-6)
    P.act(g.sc.all(), g.cvs.all(), AF.Silu)

    load_x(g)
    first = True
    for l in layers:
        adaln(g, l)
        norm_mod(g, l, 0)
        if mixers:
            [dn_layer, na_layer, mla_layer, rw_layer][l % 4](g, l)
        norm_mod(g, l, 1)
        ffn(g, l)
    store_y(g)
    P.emit(reorder=REORDER)
    nc.in_names = list(P.in_names)
    global LASTP
    LASTP = P
    return nc


def carve(g, name, off, shape, dtype, split=None):
    n = int(np.prod(shape[1:]))
    esz = 4 if dtype == F32 else 2
    assert off % 4 == 0 and off + n * esz <= ARENA
    ap = g.arena.h[:, off // 4:(off + n * esz) // 4]
    if dtype != F32:
        ap = ap.bitcast(dtype)
    if len(shape) == 3:
        ap = ap.rearrange("p (a b) -> p a b", a=shape[1])
    elif len(shape) == 4:
        ap = ap.rearrange("p (a b c) -> p a b c", a=shape[1], b=shape[2])
    return T(name, ap, shape, split)


class PsCtx:
    def __init__(self, rot):
        self.psrot = rot
        self.pos = 0

    def take(self, g, k):
        total = 4 * len(self.psrot)
        if self.pos % k:
            self.pos += k - self.pos % k
        s_ = self.pos % total
        self.pos += k
        if k == 4:
            return g.psb[self.psrot[s_ // 4]], 0
        nb = len(self.psrot)
        p_ = s_ // 2
        return g.psb[self.psrot[p_ % nb]], ((p_ // nb) % 2) * 256 + (s_ % 2) * 128


def nextq(g, c=None):
    if c is not None:
        b, o = c.take(g, 1)
        return b[:, o:o + 128]
    r = g.psrot
    n = g.pq % (4 * len(r))
    g.pq += 1
    b = g.psb[r[n % len(r)]]
    q = n // len(r)
    return b[:, q * 128:(q + 1) * 128]


def nexth(g, c):
    b, o = c.take(g, 2)
    return b[:, o:o + 256]


class Pool:
    def __init__(self, tiles):
        self.t = tiles
        self.i = 0

    def get(self):
        x = self.t[self.i % len(self.t)]
        self.i += 1
        return x


def apool(g, name, off, n, shape, dtype, split=None):
    sz = int(np.prod(shape[1:])) * (4 if dtype == F32 else 2)
    return Pool([carve(g, "%s%d" % (name, i), off + i * sz, shape, dtype, split) for i in range(n)]), off + n * sz


def nextps(g, c=None):
    if c is not None:
        return c.take(g, 4)[0]
    r = g.psrot
    p = g.psb[r[g.pi % len(r)]]
    g.pi += 1
    return p


def load_x(g):
    P = g.P
    xin = [carve(g, "xin%d" % i, i * 4096, [128, D], F32) for i in range(2)]
    for t in range(NT // 128):
        b = xin[t % 2]
        P.dma("sp", b.all(), g.x[t * 128:(t + 1) * 128, :], "xin%d" % (t % 2))
        for hf in range(2):
            ps = nextps(g)
            for j in range(4):
                c = hf * 4 + j
                P.tr(ps[:, j * 128:(j + 1) * 128], b[:, c * 128:(c + 1) * 128], g.ident.all())
            P.copy("dve" if hf == 0 else "act", g.xT[:, hf * 4:(hf + 1) * 4, t * 128:(t + 1) * 128],
                   ps.all().re("p (j t) -> p j t", j=4))
    g.xin = xin
    P.barrier()


def store_y(g):
    P = g.P
    P.barrier()
    for t in range(NT // 128):
        b = g.xin[t % 2]
        for hf in range(2):
            ps = nextps(g)
            for j in range(4):
                c = hf * 4 + j
                P.tr(ps[:, j * 128:(j + 1) * 128], g.xT[:, c, t * 128:(t + 1) * 128], g.ident.all())
            P.copy("dve" if hf == 0 else "act", b[:, hf * 512:(hf + 1) * 512], ps.all())
        P.dma("sp", g.y[t * 128:(t + 1) * 128, :], b.all(), "xin%d" % (t % 2))


def adaln(g, l):
    P = g.P
    wbufs(g)
    P.barrier()
    g.awb = g.wi
    g.rowb = carve(g, "rowb", 0, [128, 6 * D], F32)
    src = g.ada_w.h[l].rearrange("(k p) n -> p k n", p=128)
    for n in range(12):
        wb = g.awb[n % 2]
        P.dma("pool", wb.all(), V(src[:, :, n * 512:(n + 1) * 512], g.ada_w.allkeys), "wi%d" % (n % 2))
        ps = nextps(g)
        for k in range(8):
            P.mm(ps[0:2, :], g.sc[:, k, :], wb[:, k, :], start=(k == 0), stop=(k == 7))
        P.copy("act", g.rowb[0:2, n * 512:(n + 1) * 512], ps[0:2, :])
    ps = nextps(g)
    for j in range(48):
        P.tr(ps[:, j * 2:(j + 1) * 2], g.rowb[0:2, j * 128:(j + 1) * 128], g.ident[0:2, 0:2])
    m = g.mod[l]
    P.tt("dve", m.all(), ps[:, 0:96].re("p (j s) -> p j s", s=2),
         g.adab[:, l, :].re("p (j o) -> p j o", o=1).bc([128, 48, 2]), ALU.add)
    for w in range(2):
        sl = m[:, (3 * w + 1) * 8:(3 * w + 2) * 8, :]
        P.stt("dve", g.gs[l][:, w, :, :], sl, 1.0,
              g.nrm[:, w, l, :].re("p (c o) -> p c o", o=1).bc([128, 8, 2]), ALU.add, ALU.mult)
    P.barrier()


def norm_mod(g, l, w):
    P = g.P
    m = g.mod[l]
    for tg in range(NG):
        cd = 0 if tg == 0 else 1
        ts_ = slice(tg * 512, (tg + 1) * 512)
        ps = nextps(g)
        for c in range(8):
            sq = g.sq[c % 2]
            P.act(sq.all(), g.xT[:, c, ts_], AF.Square)
            P.mm(ps.all(), g.onesb.all(), sq.all(), start=(c == 0), stop=(c == 7))
        P.act(g.rstd.all(), ps.all(), AF.Ln, bias=g.eps6[:, 0:1], scale=1.0)
        P.act(g.rstd.all(), g.rstd.all(), AF.Exp, scale=-0.5)
        for c in range(8):
            tmp = g.ntmp[c % 2]
            P.tt("dve", tmp.all(), g.xT[:, c, ts_], g.rstd.all(), ALU.mult)
            P.act(g.hT[:, c, ts_], tmp.all(), AF.Identity,
                  scale=g.gs[l][:, w, c, cd:cd + 1], bias=m[:, (3 * w) * 8 + c, cd:cd + 1])


def wbufs(g):
    P = g.P
    if not hasattr(g, "wi"):
        g.wi = [P.sb("wi%d" % i, [128, 8, 512], BF16) for i in range(3)]
        g.wo = [P.sb("wo%d" % i, [128, NF, 128], BF16) for i in range(2)]
        g.sg = [P.sb("sg%d" % i, [128, 512], F32) for i in range(2)]


def chunk_masks():
    i = np.arange(128)
    same = (i[:, None] // 64) == (i[None, :] // 64)
    le = i[:, None] <= i[None, :]
    ge = i[:, None] >= i[None, :]
    f = np.float32
    McumF = (same & le).astype(f)
    McumB = (same & ge).astype(f)
    validF = same & ge
    validB = same & le
    NEGF = np.where(validF, 0.0, -1e30).astype(f)
    NEGB = np.where(validB, 0.0, -1e30).astype(f)
    SMF = (validF & (i[:, None] != i[None, :])).astype(f)
    SMB = (validB & (i[:, None] != i[None, :])).astype(f)
    Mblk = same.astype(f)
    Mch0 = np.repeat((i < 64).astype(f)[:, None], 128, 1)
    Mch1 = np.repeat((i >= 64).astype(f)[:, None], 128, 1)
    return np.ascontiguousarray(np.stack([McumF, McumB, NEGF, NEGB, SMF, SMB, Mblk, Mch0, Mch1,
                                          validF.astype(f), validB.astype(f)], 1))


def neumann_solve(g, N, Nt, tp, steps=5):
    P = g.P
    Tt = tp.get()
    P.tt("dve", Tt.all(), Nt.all(), g.ident.all(), ALU.add)
    Pm, Pt = N, Nt
    for k in range(steps):
        q1 = nextq(g)
        P.mm(q1, Pt.all(), Pm.all())
        Pn = tp.get()
        P.copy("act", Pn.all(), q1)
        if k < steps - 1:
            q2 = nextq(g)
            P.mm(q2, Pm.all(), Pt.all())
            Ptn = tp.get()
            P.copy("act", Ptn.all(), q2)
        q3 = nextq(g)
        P.mm(q3, Pn.all(), Tt.all())
        Ttn = tp.get()
        P.tt("dve", Ttn.all(), q3, Tt.all(), ALU.add)
        Tt = Ttn
        Pm = Pn
        if k < steps - 1:
            Pt = Ptn
    return Tt


def interleave(gens):
    gens = list(gens)
    while gens:
        for gn in list(gens):
            try:
                next(gn)
            except StopIteration:
                gens.remove(gn)


def interleave_gen(gens):
    gens = list(gens)
    while gens:
        for gn in list(gens):
            try:
                next(gn)
            except StopIteration:
                gens.remove(gn)
        yield


def neumann_solve_gen(g, N, Nt, tp, steps=5, pc=None):
    P = g.P
    Tt = tp.get()
    P.tt("dve", Tt.all(), Nt.all(), g.ident.all(), ALU.add)
    Pm, Pt = N, Nt
    for k in range(steps):
        q1 = nextq(g, pc)
        P.mm(q1, Pt.all(), Pm.all())
        if k < steps - 1:
            q2 = nextq(g, pc)
            P.mm(q2, Pm.all(), Pt.all())
        yield
        Pn = tp.get()
        P.copy("act", Pn.all(), q1)
        if k < steps - 1:
            Ptn = tp.get()
            P.copy("dve", Ptn.all(), q2)
        yield
        q3 = nextq(g, pc)
        P.mm(q3, Pn.all(), Tt.all())
        yield
        Ttn = tp.get()
        P.tt("dve", Ttn.all(), q3, Tt.all(), ALU.add)
        Tt = Ttn
        Pm = Pn
        if k < steps - 1:
            Pt = Ptn
    return Tt


SEQS = ((0, 2), (2, 2), (4, 8))


def dn_layer(g, l):
    P = g.P
    P.barrier()
    P.mode(True)
    off = 0
    MK = carve(g, "dn_MK", off, [128, 11, 128], F32); off += 11 * 512
    ones = carve(g, "dn_ones", off, [128, 128], F32); off += 512
    raw = carve(g, "dn_raw", off, [128, NT], F32, split=(1, 512)); off += 6144
    qT = carve(g, "dn_qT", off, [128, NT], F32, split=(1, 128)); off += 6144
    kT = carve(g, "dn_kT", off, [128, NT], F32, split=(1, 128)); off += 6144
    vT = carve(g, "dn_vT", off, [128, NT], F32, split=(1, 128)); off += 6144
    zs = raw
    oTh = carve(g, "dn_oTh", off, [128, NT], BF16, split=(1, 512)); off += 3072
    qtok = carve(g, "dn_qtok", off, [128, 12, 128], F32, split=(1, 1)); off += 6144
    ktok = carve(g, "dn_ktok", off, [128, 12, 128], F32, split=(1, 1)); off += 6144
    vtok = carve(g, "dn_vtok", off, [128, 12, 128], F32, split=(1, 1)); off += 6144
    oacc = T("dn_vT", vT.h.rearrange("p (t c) -> p t c", t=12), [128, 12, 128], split=(1, 1))
    tp0, off = apool(g, "dn_tp", off, 9, [128, 128], F32)
    tq0, off = apool(g, "dn_tq", off, 6, [128, 128], F32)
    lp0, off = apool(g, "dn_lp", off, 3, [128, 6, 128], F32)
    uw0, off = apool(g, "dn_uw", off, 2, [128, 256], F32)
    Sb, off = apool(g, "dn_S", off, 4, [128, 128], F32)
    assert off <= ARENA, off
    X1 = T("dnx1", g.wi[1].h[:, :, :].rearrange("p a b -> p (a b)").bitcast(F32), [128, 2048], split=(1, 128))
    X2 = T("dnx2", g.wo[1].h[:, :, :].rearrange("p a b -> p (a b)").bitcast(F32), [128, 1408], split=(1, 128))
    tp1 = Pool([X1[:, i * 128:(i + 1) * 128] for i in range(9)])
    tq1 = Pool([X1[:, i * 128:(i + 1) * 128] for i in range(9, 15)])
    uw1 = Pool([X2[:, 0:256], X2[:, 256:512]])
    lp1 = Pool([X2[:, 512:1280].re("p (s c) -> p s c", s=6), lp0.t[2]])
    lp0 = Pool(lp0.t[0:2])
    tpd, tqd, uwd, lpd = [tp0, tp1], [tq0, tq1], [uw0, uw1], [lp0, lp1]
    pcs = [PsCtx([0, 1, 2]), PsCtx([3, 4, 5])]
    tp = tp0
    tb = g.wi[2].all().re("p a b -> p (a b)").bitcast(F32)
    gt = tb[:, 0:384].re("p (t c) -> p t c", t=12)
    gg = tb[:, 384:576].re("p (t c) -> p t c", t=12)
    gc = tb[:, 576:768].re("p (t c) -> p t c", t=12)
    egc = tb[:, 768:960].re("p (t c) -> p t c", t=12)
    egk = tb[:, 960:1152].re("p (t c) -> p t c", t=12)
    egl = tb[:, 1152:1536].re("p (t k c) -> p t k c", t=12, k=2)
    bex = tb[:, 1536:1728].re("p (t c) -> p t c", t=12)
    nbe = tb[:, 1728:1920].re("p (t c) -> p t c", t=12)
    vec = tb[:, 1920:1952]
    cw = tb[:, 1952:2024].re("p (h s k) -> p h s k", h=8, s=3)
    gout = tb[:, 2024:2025]
    P.dma("sp", MK.all(), g.dn_mk_d.all(), "c3")
    P.dma("sp", vec, g.dn_vec_d.all(), "c3")
    P.dma("sp", cw, g.dn_cw_d.all(), "c3")
    P.dma("sp", gout, g.dn_gout_d.all(), "c3")
    P.memset("dve", ones.all(), 1.0)
    McumD = [MK[:, 0, :], MK[:, 1, :]]
    NEGD = [MK[:, 2, :], MK[:, 3, :]]
    SMD = [MK[:, 4, :], MK[:, 5, :]]
    Mblk = MK[:, 6, :]
    Mch = [MK[:, 7, :], MK[:, 8, :]]
    g.psrot = [0, 1, 2, 3, 4, 5]
    wg = g.wo[0]
    P.dma("pool", wg[:, 0:8, 0:32], V(g.dn_wg_d.h.rearrange("(k p) n -> p k n", p=128), g.dn_wg_d.allkeys), "wo0")
    pg = g.psb[6]
    for t in range(12):
        for k in range(8):
            P.mm(pg[:, t * 32:(t + 1) * 32], g.hT[:, k, t * 128:(t + 1) * 128], wg[:, k, 0:32], start=(k == 0), stop=(k == 7))
    P.copy("dve", gt, pg[:, 0:384].re("p (t c) -> p t c", t=12))
    P.tt("dve", gg, gt[:, :, 0:16], vec[:, 16:32].re("p (o c) -> p o c", o=1).bc([128, 12, 16]), ALU.add)
    P.act(gg, gg, AF.Exp)
    P.act(gg, gg, AF.Ln, bias=g.one1[:, 0:1], scale=1.0)
    P.act(vec[:, 0:16], vec[:, 0:16], AF.Exp)
    P.stt("dve", gg, gg, -1.0, vec[:, 0:16].re("p (o c) -> p o c", o=1).bc([128, 12, 16]), ALU.mult, ALU.mult)
    P.act(gt[:, :, 16:32], gt[:, :, 16:32], AF.Sigmoid)
    beta = gt[:, :, 16:32]
    pc = g.psb[7]
    for t in range(12):
        for d_ in range(2):
            P.mm(pc[:, t * 16 + d_ * 8:t * 16 + (d_ + 1) * 8], McumD[d_], gg[:, t, d_ * 8:(d_ + 1) * 8])
    P.copy("dve", gc, pc[:, 0:192].re("p (t c) -> p t c", t=12))
    P.act(egc, gc, AF.Exp)
    pc2 = g.psb[6]
    for t in range(12):
        P.mm(pc2[:, t * 16:(t + 1) * 16], Mblk, gg[:, t, :])
    P.tt("dve", egk, pc2[:, 0:192].re("p (t c) -> p t c", t=12), gc, ALU.subtract)
    P.act(egk, egk, AF.Exp)
    pc3 = g.psb[7]
    for t in range(12):
        for c in range(2):
            P.mm(pc3[:, (t * 2 + c) * 16:(t * 2 + c + 1) * 16], Mch[c], gg[:, t, :])
    P.act(egl, pc3[:, 0:384].re("p (t k c) -> p t k c", t=12, k=2), AF.Exp)
    P.tt("dve", bex, beta, egc, ALU.mult)
    P.ts("dve", nbe, beta, -1.0, ALU.mult)
    P.ts("dve", gg, gg, -1.0, ALU.mult)
    win = g.dn_w_in.h.rearrange("(k p) n -> p k n", p=128)
    wout = g.dn_w_out.h
    m = g.mod[l]
    for h in range(8):
        wb = g.wi[0]
        for i_ in range(4):
            P.dma("pool", wb[:, :, i_ * 128:(i_ + 1) * 128], V(win[:, :, i_ * 1024 + h * 128:i_ * 1024 + (h + 1) * 128], g.dn_w_in.allkeys), "wi0")
        for i_, dst in enumerate((qT, kT, vT, None)):
            for tg in range(NG):
                ts_ = slice(tg * 512, (tg + 1) * 512)
                ps = nextps(g)
                for k in range(8):
                    P.mm(ps.all(), wb[:, k, i_ * 128:(i_ + 1) * 128], g.hT[:, k, ts_], start=(k == 0), stop=(k == 7))
                if dst is None:
                    P.act(zs[:, ts_], ps.all(), AF.Silu)
                else:
                    P.copy("act", raw[:, ts_], ps.all())
            if dst is None:
                continue
            P.ts("dve", dst.all(), raw.all(), cw[:, h, i_, 1:2], ALU.mult)
            for (s0, e0) in ((0, 256), (256, 512), (512, NT)):
                P.stt("dve", dst[:, s0 + 1:e0], raw[:, s0:e0 - 1], cw[:, h, i_, 0:1], dst[:, s0 + 1:e0], ALU.mult, ALU.add)
                P.stt("dve", dst[:, s0:e0 - 1], raw[:, s0 + 1:e0], cw[:, h, i_, 2:3], dst[:, s0:e0 - 1], ALU.mult, ALU.add)
            for tg in range(NG):
                ts_ = slice(tg * 512, (tg + 1) * 512)
                P.act(dst[:, ts_], dst[:, ts_], AF.Silu)
                if i_ < 2:
                    sq = g.sq[g.nsq % 2]
                    g.nsq += 1
                    P.act(sq.all(), dst[:, ts_], AF.Square)
                    pm = nextps(g)
                    P.mm(pm.all(), g.onesb1.all(), sq.all())
                    P.act(g.rstd.all(), pm.all(), AF.Ln, bias=g.eps6[:, 0:1], scale=1.0)
                    P.act(g.rstd.all(), g.rstd.all(), AF.Exp, scale=-0.5)
                    P.stt("dve", dst[:, ts_], dst[:, ts_], (128 ** -0.5) if i_ == 0 else 1.0, g.rstd.all(), ALU.mult, ALU.mult)
        for src, dst in ((qT, qtok), (kT, ktok), (vT, vtok)):
            for t4 in range(3):
                ps = nextps(g)
                for j in range(4):
                    t = t4 * 4 + j
                    P.tr(ps[:, j * 128:(j + 1) * 128], src[:, t * 128:(t + 1) * 128], g.ident.all())
                P.copy("act", dst[:, t4 * 4:(t4 + 1) * 4, :], ps.all().re("p (t c) -> p t c", t=4))
        done = set()
        for si, (t0, nt) in enumerate(SEQS):
            Sc = [None, None]
            for d_ in range(2):
                Sc[d_] = Sb.get()
                if si < 2:
                    P.memset("dve", Sc[d_].all(), 0.0)
                else:
                    P.dma("sp", Sc[d_].all(), g.dn_s0_d[d_, h, :, :], "dn_s0")
            def dn_body(d_, step):
                t = t0 + step if d_ == 0 else t0 + nt - 1 - step
                col = d_ * 8 + h
                tk = slice(t * 128, (t + 1) * 128)
                pG = nextq(g, pcs[d_])
                P.mm(pG, kT[:, tk], kT[:, tk])
                pQK = nextq(g, pcs[d_])
                P.mm(pQK, qT[:, tk], kT[:, tk])
                ngb = tpd[d_].get()
                P.ts("dve", ngb.all(), ones.all(), gg[:, t, col:col + 1], ALU.mult)
                pA = nextq(g, pcs[d_])
                P.mm(pA, ngb.all(), McumD[d_], start=True, stop=False)
                P.mm(pA, g.ident.all(), NEGD[d_], start=False, stop=True)
                yield
                Dm = tpd[d_].get()
                P.act(Dm.all(), pA, AF.Exp, bias=gc[:, t, col:col + 1], scale=1.0)
                Ds = tpd[d_].get()
                P.tt("dve", Ds.all(), Dm.all(), SMD[d_], ALU.mult)
                N = tpd[d_].get()
                P.stt("dve", N.all(), pG, nbe[:, t, col:col + 1], Ds.all(), ALU.mult, ALU.mult)
                attn = tpd[d_].get()
                P.tt("dve", attn.all(), pQK, Dm.all(), ALU.mult)
                yield
                pT1 = nextq(g, pcs[d_])
                P.tr(pT1, N.all(), g.ident.all())
                yield
                Nt = tpd[d_].get()
                P.copy("act", Nt.all(), pT1)
                pT2 = nextq(g, pcs[d_])
                P.tr(pT2, attn.all(), g.ident.all())
                attnT = tpd[d_].get()
                P.copy("act", attnT.all(), pT2)
                Tt = yield from neumann_solve_gen(g, N, Nt, tqd[d_], pc=pcs[d_])
                yield
                rhs = uwd[d_].get()
                P.ts("dve", rhs[:, 0:128], vtok[:, t, :], beta[:, t, col:col + 1], ALU.mult)
                P.ts("dve", rhs[:, 128:256], ktok[:, t, :], bex[:, t, col:col + 1], ALU.mult)
                pU = nexth(g, pcs[d_])
                P.mm(pU[:, 0:256], Tt.all(), rhs.all())
                yield
                UW = uwd[d_].get()
                P.copy("act", UW.all(), pU[:, 0:256])
                L = lpd[d_].get()
                pq_ = nextq(g, pcs[d_])
                P.mm(pq_, attnT.all(), UW[:, 128:256])
                yield
                Qh = tpd[d_].get()
                P.stt("dve", Qh.all(), qtok[:, t, :], egc[:, t, col:col + 1], pq_, ALU.mult, ALU.subtract)
                yield
                pq2 = nextq(g, pcs[d_])
                P.tr(pq2, Qh.all(), g.ident.all())
                P.copy("act", L[:, 0, :], pq2)
                pq3 = nextq(g, pcs[d_])
                P.mm(pq3, attnT.all(), UW[:, 0:128])
                P.copy("act", L[:, 1, :], pq3)
                yield
                kg = tpd[d_].get()
                P.ts("dve", kg.all(), ktok[:, t, :], egk[:, t, col:col + 1], ALU.mult)
                for c in range(2):
                    cr = slice(c * 64, (c + 1) * 64)
                    pp = nextq(g, pcs[d_])
                    P.mm(pp, UW[cr, 128:256], kg[cr, :])
                    P.stt("dve", L[:, 2 + c, :], g.ident.all(), egl[:, t, c, col:col + 1], pp, ALU.mult, ALU.subtract)
                    ps_ = nextq(g, pcs[d_])
                    P.mm(ps_, kg[cr, :], UW[cr, 0:128])
                    P.copy("act", L[:, 4 + c, :], ps_)
                yield
                for c in ((0, 1) if d_ == 0 else (1, 0)):
                    cr = slice(c * 64, (c + 1) * 64)
                    po_ = nextq(g, pcs[d_])
                    P.mm(po_[cr, :], L[:, 0, cr], Sc[d_].all())
                    if (t, c) in done:
                        P.tt("dve", oacc[cr, t, :], po_[cr, :], oacc[cr, t, :], ALU.add)
                        P.tt("dve", oacc[cr, t, :], oacc[cr, t, :], L[cr, 1, :], ALU.add)
                    else:
                        P.tt("dve", oacc[cr, t, :], po_[cr, :], L[cr, 1, :], ALU.add)
                        done.add((t, c))
                    pn = nextq(g, pcs[d_])
                    P.mm(pn, L[:, 2 + c, :], Sc[d_].all())
                    Sn = Sb.get()
                    P.tt("dve", Sn.all(), pn, L[:, 4 + c, :], ALU.add)
                    Sc[d_] = Sn
            for step in range(nt):
                interleave([dn_body(0, step), dn_body(1, step)])
            if si < 2:
                for d_ in range(2):
                    P.dma("sp", g.dn_st_out[si, d_, h, :, :], Sc[d_].all(), "dn_so")
        for t4 in range(3):
            ssq = g.rstd[:, t4 * 4:(t4 + 1) * 4]
            for j in range(4):
                t = t4 * 4 + j
                junk = tp.get()
                P.act(junk.all(), oacc[:, t, :], AF.Square, accum_out=g.rstd[:, t:t + 1])
            P.act(g.rstd[:, 16 + t4 * 4:16 + (t4 + 1) * 4], ssq, AF.Ln, bias=g.eps6[:, 0:1], scale=1.0 / 128)
            P.act(g.rstd[:, 16 + t4 * 4:16 + (t4 + 1) * 4], g.rstd[:, 16 + t4 * 4:16 + (t4 + 1) * 4], AF.Exp, scale=-0.5)
            ps = nextps(g)
            for j in range(4):
                t = t4 * 4 + j
                on = tp.get()
                P.ts("dve", on.all(), oacc[:, t, :], g.rstd[:, 16 + t:17 + t], ALU.mult)
                P.tr(ps[:, j * 128:(j + 1) * 128], on.all(), g.ident.all())
            P.stt("dve", oTh[:, t4 * 512:(t4 + 1) * 512], ps.all(), gout, zs[:, t4 * 512:(t4 + 1) * 512], ALU.mult, ALU.mult)
        wo = g.wo[0]
        wov = wo.all().re("p a b -> p (a b)")[:, 0:1024]
        P.dma("pool", wov, V(wout[h * 128:(h + 1) * 128, :], g.dn_w_out.allkeys), "wo0")
        for d in range(8):
            for tg in range(NG):
                cd = 0 if tg == 0 else 1
                ts_ = slice(tg * 512, (tg + 1) * 512)
                ps = nextps(g)
                P.mm(ps.all(), wov[:, d * 128:(d + 1) * 128], oTh[:, ts_])
                P.stt("dve", g.xT[:, d, ts_], ps.all(), m[:, 2 * 8 + d, cd:cd + 1], g.xT[:, d, ts_], ALU.mult, ALU.add)
    g.psrot = list(range(8))
    P.barrier()


def mla_consts():
    t = np.arange(1024)
    cos = np.ones((96, 1024), np.float32)
    sin = np.zeros((96, 1024), np.float32)
    R = np.zeros((96, 96), np.float32)
    inv = 10000.0 ** (-np.arange(8, dtype=np.float32) / 8)
    for j in range(32):
        grp, jj = j // 16, j % 16
        pos = (t // 64) if grp == 0 else (t % 64)
        ang = pos.astype(np.float32) * inv[jj % 8]
        cos[64 + j] = np.cos(ang)
        sin[64 + j] = np.sin(ang)
        if jj < 8:
            R[64 + j + 8, 64 + j] = -1.0
        else:
            R[64 + j - 8, 64 + j] = 1.0
    return cos, sin, R


def mla_layer(g, l):
    P = g.P
    P.barrier()
    craw = carve(g, "ml_craw", 0, [128, 5, 512], F32, split=(1, 1))
    cqn = carve(g, "ml_cqn", 10240, [128, 3, NT], BF16, split=(2, 512))
    ckvn = carve(g, "ml_ckvn", 19456, [128, 2, 1792], BF16, split=(2, 256))
    kpe = carve(g, "ml_kpe", 26624, [128, 1792], BF16, split=(1, 256))
    Vp = [carve(g, "ml_Vp%d" % i, 30208 + i * 3584, [128, 14, 128], BF16, split=(1, 1)) for i in range(2)]
    qb = [carve(g, "ml_q%d" % i, 37376 + i * 6656, [128, NT], BF16, split=(1, 512)) for i in range(2)]
    kb = [carve(g, "ml_k%d" % i, 37376 + i * 6656 + 3072, [128, 1792], BF16, split=(1, 256)) for i in range(2)]
    cos = carve(g, "ml_cos", 50688, [128, 1024], F32)
    sin = carve(g, "ml_sin", 54784, [128, 1024], F32)
    Rm = carve(g, "ml_R", 58880, [128, 96], BF16)
    gn = carve(g, "ml_gn", 59136, [128, 8], F32)
    oT = g.hT
    P.dma("sp", cos[0:96, :], g.ml_cos_d.all(), "c2")
    P.dma("sp", sin[0:96, :], g.ml_sin_d.all(), "c2")
    P.dma("sp", gn.all(), g.ml_gn_d.all(), "c2")
    P.dma("pool", Rm[0:96, :], g.ml_R_d.all(), "c2")
    wa_src = g.ml_wa.h.rearrange("(k p) n -> p k n", p=128)
    P.dma("pool", g.wi[0].all(), V(wa_src[:, :, 0:512], g.ml_wa.allkeys), "wi0")
    P.dma("pool", g.wi[1][:, :, 0:224], V(wa_src[:, :, 512:736], g.ml_wa.allkeys), "wi1")
    P.ts("dve", gn[:, 5:6], gn[:, 5:6], 96 ** -0.5, ALU.mult)
    g.psrot = [0, 1, 2, 3]
    ckv_out = g.ckv_out
    kpe_out = g.kpe_out
    for tg in range(NG):
        ts_ = slice(tg * 512, (tg + 1) * 512)
        for c in range(6):
            ps = nextps(g)
            wsl = g.wi[0][:, :, c * 128:(c + 1) * 128] if c < 4 else \
                (g.wi[1][:, :, 0:128] if c == 4 else g.wi[1][:, :, 128:224])
            mrows = 128 if c < 5 else 96
            for k in range(8):
                P.mm(ps[0:mrows, :], wsl[:, k, :], g.hT[:, k, ts_], start=(k == 0), stop=(k == 7))
            if c < 5:
                P.copy("act", craw[:, c, :], ps.all())
            else:
                P.copy("act", kpe[64:96, ts_], ps[64:96, :])
                if tg == 0:
                    kf = g.ntmp[0]
                    P.copy("dve", kf[64:96, :], ps[64:96, :])
                    pt = nextps(g)
                    for t in range(4):
                        P.tr(pt[:, t * 32:(t + 1) * 32], kf[64:96, t * 128:(t + 1) * 128], g.ident[64:96, 64:96])
                    st = g.ntmp[1]
                    P.copy("act", st[:, 0:128], pt[:, 0:128])
                    for t in range(4):
                        P.dma("sp", kpe_out[t // 2, (t % 2) * 128:(t % 2 + 1) * 128, :], st[:, t * 32:(t + 1) * 32], "ntmp1")
        for (c0, nch, dst, gc0) in ((0, 3, cqn, 0), (3, 2, ckvn, 3)):
            pm = nextps(g)
            for c in range(nch):
                sq = g.sq[g.nsq % 2]
                g.nsq += 1
                P.act(sq.all(), craw[:, c0 + c, :], AF.Square)
                P.mm(pm.all(), g.onesb1.all(), sq.all(), start=(c == 0), stop=(c == nch - 1))
            P.act(g.rstd.all(), pm.all(), AF.Ln, bias=g.eps6[:, 0:1], scale=1.0 / (128 * nch))
            P.act(g.rstd.all(), g.rstd.all(), AF.Exp, scale=-0.5)
            for c in range(nch):
                if dst is ckvn and tg == 0:
                    cf = g.sg[c % 2]
                    P.stt("dve", cf.all(), craw[:, c0 + c, :], gn[:, gc0 + c:gc0 + c + 1], g.rstd.all(), ALU.mult, ALU.mult)
                    P.copy("act", dst[:, c, ts_], cf.all())
                    pt = nextps(g)
                    for t in range(4):
                        P.tr(pt[:, t * 128:(t + 1) * 128], cf[:, t * 128:(t + 1) * 128], g.ident.all())
                    st = g.ntmp[c % 2]
                    P.copy("act", st.all(), pt.all())
                    for t in range(4):
                        P.dma("sp", ckv_out[t // 2, (t % 2) * 128:(t % 2 + 1) * 128, c * 128:(c + 1) * 128],
                              st[:, t * 128:(t + 1) * 128], "ntmp%d" % (c % 2))
                else:
                    P.stt("dve", dst[:, c, ts_], craw[:, c0 + c, :], gn[:, gc0 + c:gc0 + c + 1], g.rstd.all(), ALU.mult, ALU.mult)
    for t in range(2):
        st = g.sg[t]
        P.dma("sp", st[:, 0:256], g.ml_ckvc[t * 128:(t + 1) * 128, :], "sg%d" % t)
        P.dma("sp", st[:, 256:352], g.ml_kpec[t * 128:(t + 1) * 128, :], "sg%d" % t)
        ps = nextps(g)
        for c in range(2):
            P.tr(ps[:, c * 128:(c + 1) * 128], st[:, c * 128:(c + 1) * 128], g.ident.all())
        P.tr(ps[0:96, 256:384], st[:, 256:352], g.ident.all())
        P.copy("act", ckvn[:, :, NT + t * 128:NT + (t + 1) * 128], ps[:, 0:256].re("p (c t) -> p c t", c=2))
        P.copy("act", kpe[64:96, NT + t * 128:NT + (t + 1) * 128], ps[64:96, 256:384])
    wq_src = g.ml_wqb.h.rearrange("(k p) n -> p k n", p=128)
    wq0 = g.wi[0].all().re("p a b -> p (a b)")[:, 0:2304].re("p (k n) -> p k n", k=3)
    wq1 = g.wi[1].all().re("p a b -> p (a b)")[:, 0:2304].re("p (k n) -> p k n", k=3)
    P.dma("pool", wq0, V(wq_src[:, :, 0:768], g.ml_wqb.allkeys), "wi0")
    P.dma("pool", wq1, V(wq_src[:, :, 768:1536], g.ml_wqb.allkeys), "wi1")
    wkv = g.wi[2].all().re("p a b -> p (a b)").re("p (w k n) -> p w k n", w=2, k=2)
    P.dma("pool", wkv[:, 0, :, :], V(g.ml_wk.h.rearrange("(k p) n -> p k n", p=128), g.ml_wk.allkeys), "wi2")
    P.dma("pool", wkv[:, 1, :, :], V(g.ml_wv.h.rearrange("(k p) n -> p k n", p=128), g.ml_wv.allkeys), "wi2")
    nb = 0
    cols_groups = [(0, 512, False), (512, 512, True), (1024, 512, True), (1536, 256, False)]
    for a in range(8):
        Vt = Vp[a % 2]
        for t4 in range(4):
            ps = nextps(g)
            nt_ = 4 if t4 < 3 else 2
            for tt_ in range(nt_):
                t = t4 * 4 + tt_
                for k in range(2):
                    P.mm(ps[:, tt_ * 128:(tt_ + 1) * 128], ckvn[:, k, t * 128:(t + 1) * 128], wkv[:, 1, k, a * 128:(a + 1) * 128],
                         start=(k == 0), stop=(k == 1))
            P.copy("act", Vt[:, t4 * 4:t4 * 4 + nt_, :], ps[:, 0:nt_ * 128].re("p (t c) -> p t c", t=nt_))
        for hh in range(2):
            h = 2 * a + hh
            pb = hh * 64
            qT = qb[h % 2]
            kT = kb[h % 2]
            wq = wq0 if h < 8 else wq1
            hq = h % 8
            for tg in range(NG):
                ts_ = slice(tg * 512, (tg + 1) * 512)
                ps = nextps(g)
                for k in range(3):
                    P.mm(ps[0:96, :], wq[:, k, hq * 96:(hq + 1) * 96], cqn[:, k, ts_], start=(k == 0), stop=(k == 2))
                mla_norm96(g, ps, gn[0:96, 5:6], qT[0:96, ts_])
                if tg > 0:
                    mla_rope(g, qT, ts_, cos, sin, Rm, (tg - 1) * 512)
            for gi, (c0, n, roped) in enumerate(cols_groups):
                cs = slice(c0, c0 + n)
                ps = nextps(g)
                for k in range(2):
                    P.mm(ps[0:64, 0:n], wkv[:, 0, k, h * 64:(h + 1) * 64], ckvn[:, k, cs], start=(k == 0), stop=(k == 1))
                kr = g.ntmp[gi % 2]
                P.copy("act", kr[0:64, 0:n], ps[0:64, 0:n])
                P.copy("dve", kr[64:96, 0:n], kpe[64:96, cs])
                mla_norm96(g, kr, gn[0:96, 6:7], kT[0:96, cs], n=n)
                if roped:
                    mla_rope(g, kT, cs, cos, sin, Rm, c0 - 512)
            for s_ in range(2):
                po = g.psb[4 + 2 * (nb % 2)]
                pd = g.psb[5 + 2 * (nb % 2)]
                nb += 1
                qs = slice(s_ * 256, (s_ + 1) * 256)
                for kc in range(2):
                    ps = nextps(g)
                    tile = s_ * 2 + kc
                    P.mm(ps[:, 0:256], kT[0:96, tile * 128:(tile + 1) * 128], qT[0:96, qs])
                    E = g.Ebuf[g.nE % 3]
                    g.nE += 1
                    P.act(E[:, 0:256], ps[:, 0:256], AF.Exp)
                    P.mm(po[pb:pb + 64, 0:256], Vt[:, tile, pb:pb + 64], E[:, 0:256], start=(kc == 0), stop=(kc == 1))
                    P.mm(pd[pb:pb + 64, 0:256], g.onesb1[:, 0:64], E[:, 0:256], start=(kc == 0), stop=(kc == 1))
                attn_finish(g, po, pd, pb, 256, oT[pb:pb + 64, a, qs])
            for G in range(2):
                po = g.psb[4 + 2 * (nb % 2)]
                pd = g.psb[5 + 2 * (nb % 2)]
                nb += 1
                qs = slice(512 + G * 512, 512 + (G + 1) * 512)
                for ci in range(10):
                    ps = nextps(g)
                    tile = (12 + ci) if ci < 2 else (4 + ci - 2)
                    P.mm(ps.all(), kT[0:96, tile * 128:(tile + 1) * 128], qT[0:96, qs])
                    E = g.Ebuf[g.nE % 3]
                    g.nE += 1
                    P.act(E.all(), ps.all(), AF.Exp)
                    P.mm(po[pb:pb + 64, :], Vt[:, tile, pb:pb + 64], E.all(), start=(ci == 0), stop=(ci == 9))
                    P.mm(pd[pb:pb + 64, :], g.onesb1[:, 0:64], E.all(), start=(ci == 0), stop=(ci == 9))
                attn_finish(g, po, pd, pb, 512, oT[pb:pb + 64, a, qs])
    g.psrot = list(range(8))
    out_proj(g, l, g.ml_wout.h.rearrange("(c p) d -> p c d", p=128), g.ml_wout.allkeys, oT)
    P.barrier()


def mla_norm96(g, src, gain, out_bf, n=512):
    P = g.P
    sq = g.sq[g.nsq % 2]
    g.nsq += 1
    P.act(sq[0:96, 0:n], src[0:96, 0:n], AF.Square)
    pm = nextps(g)
    P.mm(pm[0:96, 0:n], g.onesb1[0:96, 0:96], sq[0:96, 0:n])
    P.act(g.rstd[0:96, 0:n], pm[0:96, 0:n], AF.Ln, bias=g.eps6[0:96, 0:1], scale=1.0 / 96)
    P.act(g.rstd[0:96, 0:n], g.rstd[0:96, 0:n], AF.Exp, scale=-0.5)
    P.stt("dve", out_bf, src[0:96, 0:n], gain, g.rstd[0:96, 0:n], ALU.mult, ALU.mult)


def mla_rope(g, xT, cs, cos, sin, Rm, p0):
    P = g.P
    n = cs.stop - cs.start
    pr = nextps(g)
    P.mm(pr[0:96, 0:n], Rm[0:96, 0:96], xT[0:96, cs])
    t1 = g.ntmp[0]
    t2 = g.ntmp[1]
    P.tt("dve", t1[64:96, 0:n], xT[64:96, cs], cos[64:96, p0:p0 + n], ALU.mult)
    P.tt("dve", t2[64:96, 0:n], pr[64:96, 0:n], sin[64:96, p0:p0 + n], ALU.mult)
    P.tt("dve", xT[64:96, cs], t1[64:96, 0:n], t2[64:96, 0:n], ALU.add)


def rw_layer(g, l):
    P = g.P
    P.barrier()
    off = 0
    MK = carve(g, "rw_MK", off, [128, 7, 128], F32); off += 7 * 512
    rT = carve(g, "rw_r", off, [128, NT], BF16, split=(1, 128)); off += 3072
    kT = carve(g, "rw_k", off, [128, NT], BF16, split=(1, 128)); off += 3072
    kkT = carve(g, "rw_kk", off, [128, NT], BF16, split=(1, 128)); off += 3072
    gT = carve(g, "rw_g", off, [128, NT], BF16, split=(1, 128)); off += 3072
    aT = [carve(g, "rw_a%d" % i, off + i * 3072, [128, NT], BF16, split=(1, 128)) for i in range(2)]; off += 6144
    vT = carve(g, "rw_v", off, [128, NT], F32, split=(1, 128)); off += 6144
    ldT = [carve(g, "rw_ld%d" % i, off + i * 6144, [128, NT], F32, split=(1, 128)) for i in range(2)]; off += 12288
    yacc = carve(g, "rw_yacc", off, [128, 12, 128], F32, split=(1, 1)); off += 6144
    tpA, off = apool(g, "rw_tp", off, 26, [128, 128], F32)
    wp, off = apool(g, "rw_wp", off, 4, [128, 256], F32)
    lp, off = apool(g, "rw_lp", off, 2, [128, 4, 128], F32)
    Zb, off = apool(g, "rw_Z", off, 6, [128, 64], F32)
    vtk, off = apool(g, "rw_vtk", off, 2, [128, 128], F32)
    assert off <= ARENA, off

    def alias_tiles(name, h, ncol):
        t_ = T(name, h, [128, ncol], split=(1, 128))
        return [t_[:, i * 128:(i + 1) * 128] for i in range(ncol // 128)]
    A_ = list(tpA.t)
    B_ = [T("Ebuf%d" % (i // 2), g.Ebuf[i // 2].h[:, :].bitcast(F32)[:, (i % 2) * 128:(i % 2 + 1) * 128], [128, 128]) for i in range(6)]
    C_ = alias_tiles("rwx0", g.wi[0].h[:, :, :].rearrange("p a b -> p (a b)").bitcast(F32), 2048) + \
        alias_tiles("rwx1", g.wi[1].h[:, :, :].rearrange("p a b -> p (a b)").bitcast(F32), 2048)
    D_ = alias_tiles("rwx2", g.sg[0].h[:, :], 512) + alias_tiles("rwx3", g.sg[1].h[:, :], 512) + \
        alias_tiles("rwx4", g.ntmp[0].h[:, :], 512) + alias_tiles("rwx5", g.ntmp[1].h[:, :], 512)
    E_ = alias_tiles("rwx6", g.rstd.h[:, :], 512) + alias_tiles("rwx7", g.sq[0].h[:, :].bitcast(F32), 256) + \
        alias_tiles("rwx8", g.sq[1].h[:, :].bitcast(F32), 256)
    tpd = [Pool(A_[0:16]), Pool(C_[10:26])]
    thd = [[Pool(A_[16:23]), Pool(A_[23:26] + C_[0:4])], [Pool(C_[26:32] + D_[0:1]), Pool(D_[7:14])]]
    tqd = [[Pool(B_), Pool(C_[4:10])], [Pool(D_[1:7]), Pool(D_[14:16] + E_[0:4])]]
    wpd = [Pool(wp.t[0:2]), Pool(wp.t[2:4])]
    lpd = [Pool(lp.t[0:1]), Pool(lp.t[1:2])]
    pcs = [[PsCtx([0, 1]), PsCtx([2, 3])], [PsCtx([4, 5]), PsCtx([6, 7])]]
    pv = g.wi[2].all().re("p a b -> p (a b)").bitcast(F32)
    muT = pv[:, 0:48].re("p (s k) -> p s k", s=6)
    w0v = pv[:, 48:64].re("p (d c) -> p d c", d=2)
    a0v = pv[:, 64:80].re("p (d c) -> p d c", d=2)
    kkv = pv[:, 80:88]
    kav = pv[:, 88:96]
    rkv = pv[:, 96:104]
    lng = pv[:, 104:112]
    lnb = pv[:, 112:120]
    omk = pv[:, 120:128]
    nw0 = pv[:, 128:144].re("p (d c) -> p d c", d=2)
    mh = pv[:, 144:192].re("p (s k) -> p s k", s=6)
    mm1 = pv[:, 192:240].re("p (s k) -> p s k", s=6)
    nhalf = pv[:, 240:241]
    mids = [g.wo[0].all().re("p a b -> p (a b)")[:, 0:NT], g.wo[1].all().re("p a b -> p (a b)")[:, 0:NT],
            g.wi[2].all().re("p a b -> p (a b)")[:, 1024:1024 + NT]]
    wsl = g.wi[2].all().re("p a b -> p (a b)")[:, 2560:3584]
    P.dma("sp", MK.all(), g.dn_mk_d[:, 0:7, :], "c4")
    P.dma("sp", pv[:, 0:120], g.rw_pv_d.all(), "c4")
    P.ts("dve", omk, kav, -1.0, ALU.mult, 1.0, ALU.add)
    P.ts("dve", nw0.re("p d c -> p (d c)"), w0v.re("p d c -> p (d c)"), -1.0, ALU.mult)
    P.ts("dve", mh.re("p s k -> p (s k)"), muT.re("p s k -> p (s k)"), 0.5, ALU.mult)
    P.ts("dve", mm1.re("p s k -> p (s k)"), muT.re("p s k -> p (s k)"), -1.0, ALU.mult, 1.0, ALU.add)
    P.memset("dve", nhalf, -0.5)
    McumD = [MK[:, 0, :], MK[:, 1, :]]
    SMD = [MK[:, 4, :], MK[:, 5, :]]
    SMT = [MK[:, 5, :], MK[:, 4, :]]
    INCT = [MK[:, 0, :], MK[:, 1, :]]
    Mblk = MK[:, 6, :]
    g.psrot = [0, 1, 2, 3, 4, 5]
    BOUNDS = (0, 256, 512, NT)

    def shifted_proj(ps, wd, wh, tg, mrows=128):
        s0 = tg * 512
        for k in range(8):
            P.mm(ps[0:mrows, :], wd[:, k, :], g.hT[:, k, s0:s0 + 512], start=(k == 0), stop=False)
        segs = [(0, 256), (256, 512)] if tg == 0 else [(0, 512)]
        n = 0
        tot = 16 * len(segs)
        for (a_, b_) in segs:
            lo = a_ + (1 if (s0 + a_) in BOUNDS else 0)
            hi = b_ - (1 if (s0 + b_) in BOUNDS else 0)
            for k in range(8):
                n += 1
                P.mm(ps[0:mrows, lo:b_], wh[:, k, :], g.hT[:, k, s0 + lo - 1:s0 + b_ - 1], start=False, stop=False)
            for k in range(8):
                n += 1
                P.mm(ps[0:mrows, a_:hi], wh[:, k, :], g.hT[:, k, s0 + a_ + 1:s0 + hi + 1], start=False, stop=(n == tot))

    def scaled_w(dst_d, dst_h, src, si):
        P.tt("dve", dst_d, src, mm1[:, si, :].re("p (k o) -> p k o", o=1).bc([128, 8, 128]), ALU.mult)
        P.tt("dve", dst_h, src, mh[:, si, :].re("p (k o) -> p k o", o=1).bc([128, 8, 128]), ALU.mult)

    wl = g.wi[0]
    for i_, (src, si, fn) in enumerate(((g.rw_w1_d, 3, AF.Tanh), (g.rw_a1_d, 4, AF.Identity), (g.rw_g1_d, 5, AF.Sigmoid))):
        P.dma("pool", wl[:, :, 0:128], V(src.h.rearrange("(k p) n -> p k n", p=128), src.allkeys), "wi0")
        scaled_w(wl[:, :, 128:256], wl[:, :, 256:384], wl[:, :, 0:128], si)
        for tg in range(NG):
            ps = nextps(g)
            shifted_proj(ps, wl[:, :, 128:256], wl[:, :, 256:384], tg)
            P.act(mids[i_][:, tg * 512:(tg + 1) * 512], ps.all(), fn)
    wr_src = g.rw_wrkv_d.h
    for c in range(8):
        wb = g.wi[0]
        wsc = g.wi[1]
        for i_ in range(3):
            P.dma("pool", wb[:, :, i_ * 128:(i_ + 1) * 128],
                  V(wr_src[i_].rearrange("(k p) n -> p k n", p=128)[:, :, c * 128:(c + 1) * 128], g.rw_wrkv_d.allkeys), "wi0")
        P.dma("pool", wsl[:, 0:384].re("p (i n) -> p i n", i=3), V(g.rw_l2_d.h[:, :, c * 128:(c + 1) * 128], g.rw_l2_d.allkeys), "rw_l2")
        whs = [wsc[:, :, 384:512], wb[:, :, 384:512], g.rwh.all()]
        for i_ in range(3):
            scaled_w(wsc[:, :, i_ * 128:(i_ + 1) * 128], whs[i_], wb[:, :, i_ * 128:(i_ + 1) * 128], i_)
        for tg in range(NG):
            ts_ = slice(tg * 512, (tg + 1) * 512)
            for i_, dst in enumerate((rT, kT, vT)):
                ps = nextps(g)
                shifted_proj(ps, wsc[:, :, i_ * 128:(i_ + 1) * 128], whs[i_], tg)
                P.copy("act", dst[:, ts_], ps.all())
            for d_ in range(2):
                dr = slice(d_ * 64, (d_ + 1) * 64)
                ps = nextps(g)
                P.mm(ps.all(), wsl[dr, 0:128], mids[0][dr, ts_])
                t1 = g.ntmp[0]
                P.act(t1.all(), ps.all(), AF.Exp, bias=nw0[:, d_, c:c + 1], scale=-1.0)
                P.act(t1.all(), t1.all(), AF.Ln, bias=g.one1[:, 0:1], scale=1.0)
                P.act(ldT[d_][:, ts_], t1.all(), AF.Exp, bias=nhalf, scale=-1.0)
                ps = nextps(g)
                P.mm(ps.all(), wsl[dr, 128:256], mids[1][dr, ts_])
                P.act(aT[d_][:, ts_], ps.all(), AF.Sigmoid, bias=a0v[:, d_, c:c + 1], scale=1.0)
            ps = nextps(g)
            P.mm(ps.all(), wsl[:, 256:384], mids[2][:, ts_])
            P.copy("act", gT[:, ts_], ps.all())
            t2 = g.ntmp[1]
            P.ts("dve", t2.all(), kT[:, ts_], kkv[:, c:c + 1], ALU.mult)
            sq = g.sq[g.nsq % 2]
            g.nsq += 1
            P.act(sq.all(), t2.all(), AF.Square)
            pm = nextps(g)
            P.mm(pm.all(), g.ones2.all(), sq.all())
            P.act(g.rstd.all(), pm.all(), AF.Ln, bias=g.eps6[:, 0:1], scale=64.0)
            P.act(g.rstd.all(), g.rstd.all(), AF.Exp, scale=-0.5)
            P.tt("dve", kkT[:, ts_], t2.all(), g.rstd.all(), ALU.mult)
        P.barrier()
        done = set()
        for si, (t0, nt) in enumerate(SEQS):
            Zc = [None, None]
            for d_ in range(2):
                Zc[d_] = Zb.get()
                if si < 2:
                    P.memset("dve", Zc[d_].all(), 0.0)
                else:
                    P.dma("sp", Zc[d_].all(), g.rw_z0_d[d_, c, :, :], "rw_z0")
            def hh_body(d_, t, hh, AR, KB, Vt, eC, atok, khtok, bhtok, L):
                pc = pcs[d_][hh]
                th = thd[d_][hh]
                hr = slice(hh * 64, (hh + 1) * 64)
                pN = nextq(g, pc)
                P.mm(pN, AR[hr, 0:128], KB[hr, 128:256])
                pB = nexth(g, pc)
                P.mm(pB[:, 0:256], KB[hr, 128:256], AR[hr, 0:256])
                yield
                N = th.get()
                P.tt("dve", N.all(), pN, SMD[d_], ALU.mult)
                Nt = th.get()
                P.tt("dve", Nt.all(), pB[:, 0:128], SMT[d_], ALU.mult)
                MrbT = th.get()
                P.tt("dve", MrbT.all(), pB[:, 128:256], INCT[d_], ALU.mult)
                pK = nexth(g, pc)
                P.mm(pK[:, 0:256], KB[hr, 0:128], AR[hr, 0:256])
                yield
                LakT = th.get()
                P.tt("dve", LakT.all(), pK[:, 0:128], SMT[d_], ALU.mult)
                MrkT = th.get()
                P.tt("dve", MrkT.all(), pK[:, 128:256], INCT[d_], ALU.mult)
                Tt = yield from neumann_solve_gen(g, N, Nt, tqd[d_][hh], pc=pc)
                X = th.get()
                pLV = nextq(g, pc)
                P.mm(pLV[:, 0:64], LakT.all(), Vt[:, hr])
                yield
                P.copy("act", X[:, 0:64], pLV[:, 0:64])
                P.copy("dve", X[:, 64:128], atok[:, hr])
                yield
                pUA = nextq(g, pc)
                P.mm(pUA, Tt.all(), X.all())
                yield
                UA = th.get()
                P.copy("act", UA.all(), pUA)
                yield
                pR = nextq(g, pc)
                P.mm(pR[hr, :], UA[:, 64:128], MrbT.all())
                pY = nextq(g, pc)
                P.mm(pY[:, 0:64], MrbT.all(), UA[:, 0:64], start=True, stop=False)
                P.mm(pY[:, 0:64], MrkT.all(), Vt[:, hr], start=False, stop=True)
                yield
                P.tt("dve", L[hr, 0, :], pR[hr, :], AR[hr, 128:256], ALU.add)
                P.copy("act", L[:, 1, hr], pY[:, 0:64])
                for c2 in range(2):
                    cr = slice(c2 * 64, (c2 + 1) * 64)
                    pP = nextq(g, pc)
                    P.mm(pP[hr, 0:64], UA[cr, 64:128], bhtok[cr, hr])
                    pZ = nextq(g, pc)
                    P.mm(pZ[hr, 0:64], bhtok[cr, hr], UA[cr, 0:64], start=True, stop=False)
                    P.mm(pZ[hr, 0:64], khtok[cr, hr], Vt[cr, hr], start=False, stop=True)
                    yield
                    P.stt("dve", L[hr, 2, cr], g.ident[hr, hr], eC[hr, c2 * 64:c2 * 64 + 1], pP[hr, 0:64], ALU.mult, ALU.add)
                    P.copy("act", L[hr, 3, cr], pZ[hr, 0:64])

            def rw_body(d_, step):
                tp = tpd[d_]
                pc = pcs[d_][0]
                t = t0 + step if d_ == 0 else t0 + nt - 1 - step
                tk = slice(t * 128, (t + 1) * 128)
                Vt = vtk.get()
                pv_ = nextq(g, pc)
                P.tr(pv_, vT[:, tk], g.ident.all())
                pl = nextq(g, pc)
                P.tr(pl, ldT[d_][:, tk], g.ident.all())
                yield
                P.copy("act", Vt.all(), pv_)
                ldtok = tp.get()
                P.copy("act", ldtok.all(), pl)
                yield
                pnl = nextq(g, pc)
                P.mm(pnl, ldtok.all(), McumD[d_])
                pnc = nextq(g, pc)
                P.mm(pnc, ldtok.all(), Mblk)
                yield
                nlc = tp.get()
                P.copy("act", nlc.all(), pnc)
                eC = tp.get()
                P.act(eC.all(), pnc, AF.Exp, scale=-1.0)
                epos = tp.get()
                P.act(epos.all(), pnl, AF.Exp, scale=-1.0)
                eneg = tp.get()
                P.act(eneg.all(), pnl, AF.Exp)
                tA = tp.get()
                P.tt("dve", tA.all(), pnl, ldT[d_][:, tk], ALU.subtract)
                yield
                eprev = tp.get()
                P.act(eprev.all(), tA.all(), AF.Exp, scale=-1.0)
                tB = tp.get()
                P.tt("dve", tB.all(), pnl, nlc.all(), ALU.subtract)
                kd = tp.get()
                P.ts("dve", kd.all(), aT[d_][:, tk], kav[:, c:c + 1], ALU.mult, omk[:, c:c + 1], ALU.add)
                P.tt("dve", kd.all(), kd.all(), kT[:, tk], ALU.mult)
                bv = tp.get()
                P.tt("dve", bv.all(), kkT[:, tk], aT[d_][:, tk], ALU.mult)
                yield
                ehat = tp.get()
                P.act(ehat.all(), tB.all(), AF.Exp)
                AR = wpd[d_].get()
                P.stt("dve", AR[:, 0:128], kkT[:, tk], -1.0, eprev.all(), ALU.mult, ALU.mult)
                P.tt("dve", AR[:, 128:256], rT[:, tk], epos.all(), ALU.mult)
                KB = wpd[d_].get()
                P.tt("dve", KB[:, 0:128], kd.all(), eneg.all(), ALU.mult)
                P.tt("dve", KB[:, 128:256], bv.all(), eneg.all(), ALU.mult)
                yield
                khat = tp.get()
                P.tt("dve", khat.all(), kd.all(), ehat.all(), ALU.mult)
                bhat = tp.get()
                P.tt("dve", bhat.all(), bv.all(), ehat.all(), ALU.mult)
                pts = []
                for src in (AR[:, 0:128], khat.all(), bhat.all()):
                    pt_ = nextq(g, pc)
                    P.tr(pt_, src, g.ident.all())
                    pts.append(pt_)
                yield
                toks = []
                for pt_ in pts:
                    tk_ = tp.get()
                    P.copy("act", tk_.all(), pt_)
                    toks.append(tk_)
                atok, khtok, bhtok = toks
                L = lpd[d_].get()
                yield from interleave_gen([hh_body(d_, t, hh, AR, KB, Vt, eC, atok, khtok, bhtok, L) for hh in range(2)])
                yield
                for c2 in ((0, 1) if d_ == 0 else (1, 0)):
                    cr = slice(c2 * 64, (c2 + 1) * 64)
                    Zn = Zb.get()
                    for hh in range(2):
                        hr = slice(hh * 64, (hh + 1) * 64)
                        py = nextq(g, pc)
                        P.mm(py[cr, 0:64], L[hr, 0, cr], Zc[d_][hr, :])
                        if (t, c2, hh) in done:
                            P.tt("dve", yacc[cr, t, hr], py[cr, 0:64], yacc[cr, t, hr], ALU.add)
                            P.tt("dve", yacc[cr, t, hr], yacc[cr, t, hr], L[cr, 1, hr], ALU.add)
                        else:
                            P.tt("dve", yacc[cr, t, hr], py[cr, 0:64], L[cr, 1, hr], ALU.add)
                            done.add((t, c2, hh))
                        pz = nextq(g, pc)
                        P.mm(pz[hr, 0:64], L[hr, 2, cr], Zc[d_][hr, :])
                        P.tt("dve", Zn[hr, :], pz[hr, 0:64], L[hr, 3, cr], ALU.add)
                    Zc[d_] = Zn
            for step in range(nt):
                interleave([rw_body(0, step), rw_body(1, step)])
            if si < 2:
                for d_ in range(2):
                    pzt = nextq(g)
                    P.tr(pzt[0:64, :], Zc[d_].all(), g.ident.all())
                    zo = tpd[0].get()
                    P.copy("act", zo[0:64, :], pzt[0:64, :])
                    P.dma("sp", V(g.rw_st_out.h[si, d_, 2 * c:2 * c + 2].rearrange("h v k -> v h k"), g.rw_st_out.allkeys),
                          zo[0:64, :].re("p (h k) -> p h k", h=2), "rw_so")
        P.barrier()
        P.dma("pool", wsl, V(g.rw_wout_d.h[c * 128:(c + 1) * 128, :], g.rw_wout_d.allkeys), "rw_wo")
        oTc = g.rw_oT
        for tg in range(NG):
            ts_ = slice(tg * 512, (tg + 1) * 512)
            ps = nextps(g)
            for j in range(4):
                t = tg * 4 + j
                P.tr(ps[:, j * 128:(j + 1) * 128], yacc[:, t, :], g.ident.all())
            yT_ = g.ntmp[0]
            P.copy("act", yT_.all(), ps.all())
            sqb = g.sq[g.nsq % 2]
            g.nsq += 1
            P.copy("dve", sqb.all(), yT_.all())
            pm = nextps(g)
            P.mm(pm.all(), g.ones2.all(), sqb.all())
            cen = g.ntmp[1]
            P.tt("dve", cen.all(), yT_.all(), pm.all(), ALU.subtract)
            sq2 = g.sq[g.nsq % 2]
            g.nsq += 1
            P.act(sq2.all(), cen.all(), AF.Square)
            pv2 = nextps(g)
            P.mm(pv2.all(), g.ones2.all(), sq2.all())
            P.act(g.rstd.all(), pv2.all(), AF.Ln, bias=g.eps_ln[:, 0:1], scale=1.0)
            P.act(g.rstd.all(), g.rstd.all(), AF.Exp, scale=-0.5)
            P.tt("dve", cen.all(), cen.all(), g.rstd.all(), ALU.mult)
            P.ts("dve", cen.all(), cen.all(), lng[:, c:c + 1], ALU.mult, lnb[:, c:c + 1], ALU.add)
            rk2 = g.sg[0]
            P.ts("dve", rk2.all(), rT[:, ts_], rkv[:, c:c + 1], ALU.mult)
            pb_ = nextps(g)
            for d_ in range(2):
                kd2 = g.sg[1]
                P.ts("dve", kd2.all(), aT[d_][:, ts_], kav[:, c:c + 1], ALU.mult, omk[:, c:c + 1], ALU.add)
                P.tt("dve", kd2.all(), kd2.all(), kT[:, ts_], ALU.mult)
                pr_ = g.Ebuf[d_]
                P.tt("dve", pr_.all(), kd2.all(), rk2.all(), ALU.mult)
                P.mm(pb_.all(), g.onesblk.all(), pr_.all(), start=(d_ == 0), stop=(d_ == 1))
            bon = g.sg[0]
            P.tt("dve", bon.all(), pb_.all(), vT[:, ts_], ALU.mult)
            P.tt("dve", cen.all(), cen.all(), bon.all(), ALU.add)
            P.tt("dve", oTc[:, ts_], cen.all(), gT[:, ts_], ALU.mult)
        m = g.mod[l]
        for d in range(8):
            for tg in range(NG):
                cd = 0 if tg == 0 else 1
                ts_ = slice(tg * 512, (tg + 1) * 512)
                ps = nextps(g)
                P.mm(ps.all(), wsl[:, d * 128:(d + 1) * 128], oTc[:, ts_])
                P.stt("dve", g.xT[:, d, ts_], ps.all(), m[:, 2 * 8 + d, cd:cd + 1], g.xT[:, d, ts_], ALU.mult, ALU.add)
    g.psrot = list(range(8))
    P.barrier()


def ffn(g, l):
    P = g.P
    wbufs(g)
    P.barrier()
    actT = carve(g, "actT", 0, [128, NF, NT], BF16, split=(1, 1))
    win = g.ffn_w_in.h[l].rearrange("(k p) n -> p k n", p=128)
    m = g.mod[l]
    n_sg = 0
    for j in range(NF // 2):
        wb = g.wi[j % 3]
        P.dma("pool", wb[:, :, 0:256], V(win[:, :, j * 256:(j + 1) * 256], g.ffn_w_in.allkeys), "wi%d" % (j % 3))
        P.dma("pool", wb[:, :, 256:512], V(win[:, :, DFF + j * 256:DFF + (j + 1) * 256], g.ffn_w_in.allkeys), "wi%d" % (j % 3))
        for tg in range(NG):
            ts_ = slice(tg * 512, (tg + 1) * 512)
            for hf in range(2):
                f = j * 2 + hf
                pg = nextps(g)
                pu = nextps(g)
                for k in range(8):
                    P.mm(pg.all(), wb[:, k, hf * 128:(hf + 1) * 128], g.hT[:, k, ts_], start=(k == 0), stop=(k == 7))
                for k in range(8):
                    P.mm(pu.all(), wb[:, k, 256 + hf * 128:256 + (hf + 1) * 128], g.hT[:, k, ts_], start=(k == 0), stop=(k == 7))
                sg = g.sg[n_sg % 2]
                n_sg += 1
                P.act(sg.all(), pg.all(), AF.Silu)
                P.tt("dve", actT[:, f, ts_], sg.all(), pu.all(), ALU.mult)
    wout = g.ffn_w_out.h[l].rearrange("(f p) d -> p f d", p=128)
    for d in range(8):
        wo = g.wo[d % 2]
        P.dma("pool", wo.all(), V(wout[:, :, d * 128:(d + 1) * 128], g.ffn_w_out.allkeys), "wo%d" % (d % 2))
        for tg in range(NG):
            cd = 0 if tg == 0 else 1
            ts_ = slice(tg * 512, (tg + 1) * 512)
            ps = nextps(g)
            for f in range(NF):
                P.mm(ps.all(), wo[:, f, :], actT[:, f, ts_], start=(f == 0), stop=(f == NF - 1))
            P.stt("dve", g.xT[:, d, ts_], ps.all(), m[:, 5 * 8 + d, cd:cd + 1], g.xT[:, d, ts_], ALU.mult, ALU.add)
    P.barrier()


def na_tables(rel_bias):
    H = rel_bias.shape[0]
    kcol = np.arange(64)[:, None]
    qcol = np.arange(64)[None, :]
    cidx = np.clip(kcol - qcol + 15, 0, 30)
    tz = np.zeros((H, 2, 64, 31, 64), np.float32)
    for krl in range(2):
        for m in range(31):
            ridx = 22 - m + krl
            if 0 <= ridx <= 14:
                tz[:, krl, :, m, :] = rel_bias[:, ridx][:, cidx]
    return tz.reshape(H, 128, 31 * 64)


def na_cmask():
    qc = np.arange(64)
    ws = np.clip(qc - 8, 0, 48)
    kc = np.arange(64)
    colok = (kc[:, None] >= ws[None, :]) & (kc[:, None] < ws[None, :] + 16)
    cm = np.full((2, 6, 2, 64, 8, 64), -1e30, np.float32)
    for G in range(2):
        for ci in range(6):
            for krl in range(2):
                kr = 2 * (2 * G + ci) + krl
                for rq in range(8):
                    r = 8 * G + rq
                    rs = min(max(r - 4, 0), 8)
                    if rs <= kr < rs + 8:
                        cm[G, ci, krl, :, rq, :] = np.where(colok, 0.0, -1e30)
    return cm.reshape(12, 128, 512).transpose(1, 0, 2).copy()


def attn_norm_pair(g, ps, gain, out_bf, out_f32=None):
    P = g.P
    sq = g.sq[g.nsq % 2]
    g.nsq += 1
    P.act(sq.all(), ps.all(), AF.Square)
    pm = nextps(g)
    P.mm(pm.all(), g.ones2.all(), sq.all())
    P.act(g.rstd.all(), pm.all(), AF.Ln, bias=g.eps6[:, 0:1], scale=1.0)
    P.act(g.rstd.all(), g.rstd.all(), AF.Exp, scale=-0.5)
    if out_f32 is not None:
        P.stt("dve", out_f32, ps.all(), gain, g.rstd.all(), ALU.mult, ALU.mult)
        P.copy("act", out_bf, out_f32)
    else:
        P.stt("dve", out_bf, ps.all(), gain, g.rstd.all(), ALU.mult, ALU.mult)


def attn_finish(g, po, pd, pb, n, out):
    P = g.P
    rc = g.ntmp[g.nrc % 2]
    g.nrc += 1
    P.op("dve", "reciprocal", out=rc[pb:pb + 64, 0:n], in_=pd[pb:pb + 64, 0:n])
    P.tt("dve", out, po[pb:pb + 64, 0:n], rc[pb:pb + 64, 0:n], ALU.mult)


def na_layer(g, l):
    P = g.P
    P.barrier()
    Vc = carve(g, "na_Vc", 0, [128, 2, 1024], BF16)
    oT = carve(g, "na_oT", 4096, [128, 8, NT], BF16, split=(1, 1))
    kcT = carve(g, "na_kcT", 28672, [128, 8, 256], BF16, split=(1, 1))
    CM = carve(g, "na_CM", 32768, [128, 12, 512], BF16)
    TZ = carve(g, "na_TZ", 45056, [128, 31, 64], BF16)
    qk = [[carve(g, "na_q%d" % i, 49152 + i * 6144, [128, NT], BF16, split=(1, 512)),
           carve(g, "na_k%d" % i, 49152 + i * 6144 + 3072, [128, NT], BF16, split=(1, 512))] for i in range(2)]
    Vp = [carve(g, "na_Vp%d" % i, 61440 + i * 3072, [128, 12, 128], BF16, split=(1, 1)) for i in range(2)]
    gn = carve(g, "na_gn", 67584, [128, 2], F32)
    P.dma("sp", gn.all(), g.na_gn_d.all(), "c1")
    P.ts("dve", gn[:, 0:1], gn[:, 0:1], 0.125, ALU.mult)
    P.dma("pool", CM.all(), g.na_cm_d.all(), "c1")
    g.psrot = [0, 1, 2, 3]
    kc_src = g.na_kc_d.h.rearrange("h t d -> t h d")
    vc_src = g.na_vc_d.h.rearrange("h t d -> t h d")
    for t in range(2):
        P.dma("pool", Vc[:, t, :].re("p (h d) -> p h d", h=16),
              V(vc_src[t * 128:(t + 1) * 128], g.na_vc_d.allkeys), "na_vc")
        for hf in range(2):
            st = g.ntmp[hf]
            P.dma("sp", st.all().re("p (h d) -> p h d", h=8),
                  V(kc_src[t * 128:(t + 1) * 128, hf * 8:(hf + 1) * 8, :], g.na_kc_d.allkeys), "na_kc%d" % hf)
            ps = nextps(g)
            for a in range(4):
                P.tr(ps[:, a * 128:(a + 1) * 128], st[:, a * 128:(a + 1) * 128], g.ident.all())
            P.copy("act", kcT[:, hf * 4:(hf + 1) * 4, t * 128:(t + 1) * 128], ps.all().re("p (a t) -> p a t", a=4))
    STOP = 99.0
    if STOP <= 1:
        return
    wsrc = g.na_w_qkv.h.rearrange("(k p) n -> p k n", p=128)
    wk_ = g.na_w_qkv.allkeys
    vout = g.na_vo.h.rearrange("s h t d -> s t h d")
    kout = g.na_ko.h.rearrange("s h t d -> s t h d")
    nb = 0
    for a in range(8):
        wb = g.wi[a % 3]
        for i_ in range(3):
            P.dma("pool", wb[:, :, i_ * 128:(i_ + 1) * 128], V(wsrc[:, :, i_ * 1024 + a * 128:i_ * 1024 + (a + 1) * 128], wk_), "wi%d" % (a % 3))
        qT, kT = qk[a % 2]
        Vt = Vp[a % 2]
        for t4 in range(3):
            ps = nextps(g)
            for tt_ in range(4):
                t = t4 * 4 + tt_
                for k in range(8):
                    P.mm(ps[:, tt_ * 128:(tt_ + 1) * 128], g.hT[:, k, t * 128:(t + 1) * 128], wb[:, k, 256:384], start=(k == 0), stop=(k == 7))
            P.copy("act", Vt[:, t4 * 4:(t4 + 1) * 4, :], ps.all().re("p (t c) -> p t c", t=4))
            if t4 == 0 and STOP > 1.2:
                st = g.sg[0]
                P.copy("dve", st.all(), ps.all())
                VAR = "0"
                for t in range(4):
                    if VAR == "0":
                        P.dma("sp", V(vout[t // 2, (t % 2) * 128:(t % 2 + 1) * 128, 2 * a:2 * a + 2, :], g.na_vo.allkeys),
                              st[:, t * 128:(t + 1) * 128].re("p (h d) -> p h d", h=2), "sg0")
                    elif VAR == "1":
                        pass
                    elif VAR == "2":
                        for hh_ in range(2):
                            P.dma("sp", V(g.na_vo.h[t // 2, 2 * a + hh_, (t % 2) * 128:(t % 2 + 1) * 128, :], g.na_vo.allkeys),
                                  st[:, t * 128 + hh_ * 64:t * 128 + (hh_ + 1) * 64], "sg0")
                    elif VAR == "3":
                        P.dma("pool", V(vout[t // 2, (t % 2) * 128:(t % 2 + 1) * 128, 2 * a:2 * a + 2, :], g.na_vo.allkeys),
                              st[:, t * 128:(t + 1) * 128].re("p (h d) -> p h d", h=2), "sg0")
        if STOP <= 1.4:
            return
        for tg in range(NG):
            ts_ = slice(tg * 512, (tg + 1) * 512)
            ps = nextps(g)
            for k in range(8):
                P.mm(ps.all(), wb[:, k, 0:128], g.hT[:, k, ts_], start=(k == 0), stop=(k == 7))
            attn_norm_pair(g, ps, gn[:, 0:1], qT[:, ts_])
            ps = nextps(g)
            for k in range(8):
                P.mm(ps.all(), wb[:, k, 128:256], g.hT[:, k, ts_], start=(k == 0), stop=(k == 7))
            if tg == 0 and STOP > 1.6:
                kf = g.sg[1]
                attn_norm_pair(g, ps, gn[:, 1:2], kT[:, ts_], out_f32=kf.all())
                pt = nextps(g)
                for t in range(4):
                    P.tr(pt[:, t * 128:(t + 1) * 128], kf[:, t * 128:(t + 1) * 128], g.ident.all())
                st = g.ntmp[0]
                P.copy("act", st.all(), pt.all())
                for t in range(4):
                    P.dma("sp", V(kout[t // 2, (t % 2) * 128:(t % 2 + 1) * 128, 2 * a:2 * a + 2, :], g.na_ko.allkeys),
                          st[:, t * 128:(t + 1) * 128].re("p (h d) -> p h d", h=2), "ntmp0")
            else:
                attn_norm_pair(g, ps, gn[:, 1:2], kT[:, ts_])
        if STOP <= 2:
            return
        for hh in range(2):
            h = 2 * a + hh
            pb = hh * 64
            P.dma("pool", TZ.all().re("p m q -> p (m q)"), g.na_tz_d[h, :, :], "na_tz")
            for s_ in range(2):
                po = g.psb[4 + 2 * (nb % 2)]
                pd = g.psb[5 + 2 * (nb % 2)]
                nb += 1
                qs = slice(s_ * 256, (s_ + 1) * 256)
                for kc in range(2):
                    ps = nextps(g)
                    tile = s_ * 2 + kc
                    P.mm(ps[:, 0:256], kT[pb:pb + 64, tile * 128:(tile + 1) * 128], qT[pb:pb + 64, qs])
                    E = g.Ebuf[g.nE % 3]
                    g.nE += 1
                    P.act(E[:, 0:256], ps[:, 0:256], AF.Exp)
                    P.mm(po[pb:pb + 64, 0:256], Vt[:, tile, pb:pb + 64], E[:, 0:256], start=(kc == 0), stop=(kc == 1))
                    P.mm(pd[pb:pb + 64, 0:256], g.onesb1[:, 0:64], E[:, 0:256], start=(kc == 0), stop=(kc == 1))
                attn_finish(g, po, pd, pb, 256, oT[pb:pb + 64, a, qs])
            if STOP <= 3:
                return
            for G in range(2):
                po = g.psb[4 + 2 * (nb % 2)]
                pd = g.psb[5 + 2 * (nb % 2)]
                nb += 1
                qs = slice(512 + G * 512, 512 + (G + 1) * 512)
                for ci in range(8):
                    ps = nextps(g)
                    if ci < 6:
                        tile = 4 + 2 * G + ci
                        kr0 = 2 * (2 * G + ci)
                        m0 = 15 - kr0 + 8 * G
                        P.mm(ps.all(), kT[pb:pb + 64, tile * 128:(tile + 1) * 128], qT[pb:pb + 64, qs], start=True, stop=False)
                        P.mm(ps.all(), g.identb.all(), CM[:, G * 6 + ci, :], start=False, stop=False)
                        P.mm(ps.all(), g.identb.all(), TZ[:, m0:m0 + 8, :].re("p m q -> p (m q)"), start=False, stop=True)
                        vv = Vt[:, tile, pb:pb + 64]
                    else:
                        P.mm(ps.all(), kcT[pb:pb + 64, a, (ci - 6) * 128:(ci - 5) * 128], qT[pb:pb + 64, qs])
                        vv = Vc[:, ci - 6, h * 64:(h + 1) * 64]
                    E = g.Ebuf[g.nE % 3]
                    g.nE += 1
                    P.act(E.all(), ps.all(), AF.Exp)
                    P.mm(po[pb:pb + 64, :], vv, E.all(), start=(ci == 0), stop=(ci == 7))
                    P.mm(pd[pb:pb + 64, :], g.onesb1[:, 0:64], E.all(), start=(ci == 0), stop=(ci == 7))
                attn_finish(g, po, pd, pb, 512, oT[pb:pb + 64, a, qs])
            if STOP <= 4:
                return
    g.psrot = list(range(8))
    out_proj(g, l, g.na_w_out.h.rearrange("(c p) d -> p c d", p=128), g.na_w_out.allkeys, oT)
    P.barrier()


def out_proj(g, l, wsrc, wkeys, oT):
    P = g.P
    m = g.mod[l]
    for d in range(8):
        wo = g.wo[d % 2]
        P.dma("pool", wo[:, 0:8, :], V(wsrc[:, :, d * 128:(d + 1) * 128], wkeys), "wo%d" % (d % 2))
        for tg in range(NG):
            cd = 0 if tg == 0 else 1
            ts_ = slice(tg * 512, (tg + 1) * 512)
            ps = nextps(g)
            for c in range(8):
                P.mm(ps.all(), wo[:, c, :], oT[:, c, ts_], start=(c == 0), stop=(c == 7))
            P.stt("dve", g.xT[:, d, ts_], ps.all(), m[:, 2 * 8 + d, cd:cd + 1], g.xT[:, d, ts_], ALU.mult, ALU.add)


def _lay_vec(v):
    return np.ascontiguousarray(v.reshape(8, 128).T)


def make_inputs(core, inp):
    f = np.float32
    x = np.concatenate([inp["x_prompt"][2 * core], inp["x_prompt"][2 * core + 1], inp["x_sample"][core]], 0)
    cv = np.stack([_lay_vec(inp["c_ctx"]), _lay_vec(inp["c"][core])], -1)
    adabT = np.ascontiguousarray(inp["ada_b"].reshape(DEPTH, 48, 128).transpose(2, 0, 1))
    nrmT = np.stack([inp["norm_mix"].reshape(DEPTH, 8, 128).transpose(2, 0, 1),
                     inp["norm_ffn"].reshape(DEPTH, 8, 128).transpose(2, 0, 1)], 1)
    return {
        "x": np.ascontiguousarray(x, f), "cv": np.ascontiguousarray(cv, f),
        "ada_w": inp["ada_w"], "adabT": adabT.astype(f), "nrmT": np.ascontiguousarray(nrmT, f),
        "ffn_w_in": inp["ffn_w_in"], "ffn_w_out": inp["ffn_w_out"],
        "ident": np.eye(128, dtype=f),
        "na_w_qkv": inp["na_w_qkv"][0], "na_w_out": inp["na_w_out"][0],
        "na_gn": np.ascontiguousarray(np.stack([np.tile(inp["na_q_norm"][0], 2), np.tile(inp["na_k_norm"][0], 2)], -1), f),
        "na_cm": na_cmask(), "na_tz": na_tables(inp["na_rel_bias"][0]),
        "na_kc": np.ascontiguousarray(inp["cache_na_k"][core, 0]), "na_vc": np.ascontiguousarray(inp["cache_na_v"][core, 0]),
        **mla_inputs(core, inp), **dn_inputs(core, inp), **rw_inputs(core, inp),
    }


def rw_inputs(core, inp):
    f = np.float32
    pv = np.zeros((128, 120), f)
    pv[:, 0:48] = inp["rw_mu"][0].reshape(6, 8, 128).transpose(2, 0, 1).reshape(128, 48)
    pv[:, 48:64] = inp["rw_w0"][0].reshape(2, 8, 128).transpose(2, 0, 1).reshape(128, 16)
    pv[:, 64:80] = inp["rw_a0"][0].reshape(2, 8, 128).transpose(2, 0, 1).reshape(128, 16)
    pv[:, 80:88] = inp["rw_k_k"][0].reshape(8, 128).T
    pv[:, 88:96] = inp["rw_k_a"][0].reshape(8, 128).T
    pv[:, 96:104] = inp["rw_r_k"][0].reshape(8, 128).T
    pv[:, 104:112] = inp["rw_ln_g"][0].reshape(8, 128).T
    pv[:, 112:120] = inp["rw_ln_b"][0].reshape(8, 128).T
    cat = lambda w: np.ascontiguousarray(np.concatenate([w[0], w[1]], 1), f)
    l2 = np.stack([inp["rw_w2"][0].reshape(128, D), inp["rw_a2"][0].reshape(128, D), inp["rw_g2"][0]], 1)
    z0 = inp["state_rwkv"][core, 0].transpose(0, 1, 3, 2).reshape(2, 8, 128, 64)
    return {
        "rw_pv": pv, "rw_wrkv": inp["rw_w_rkv"][0], "rw_w1": cat(inp["rw_w1"][0]), "rw_a1": cat(inp["rw_a1"][0]),
        "rw_g1": inp["rw_g1"][0], "rw_l2": np.ascontiguousarray(l2, f), "rw_wout": inp["rw_w_out"][0],
        "rw_z0": np.ascontiguousarray(z0, f), "dn_mk": chunk_masks(),
    }


def dn_inputs(core, inp):
    f = np.float32
    wg = inp["dn_w_gate"][0].reshape(D, 2, 2, 8).transpose(0, 2, 1, 3).reshape(D, 32)
    vec = np.concatenate([inp["dn_a_log"][0].reshape(16), inp["dn_dt_bias"][0].reshape(16)])
    cw = inp["dn_conv"][0].reshape(3, 3, 8, 128).transpose(3, 2, 1, 0).reshape(128, 72)
    return {
        "dn_w_in": inp["dn_w_in"][0], "dn_wg": np.ascontiguousarray(wg, f),
        "dn_vec": np.ascontiguousarray(np.tile(vec[None], (128, 1)), f), "dn_cw": np.ascontiguousarray(cw, f),
        "dn_gout": np.ascontiguousarray(inp["dn_out_norm"][0].reshape(128, 1), f), "dn_mk": chunk_masks(),
        "dn_w_out": inp["dn_w_out"][0], "dn_s0": np.ascontiguousarray(inp["state_dn"][core, 0]),
    }


def mla_inputs(core, inp):
    f = np.float32
    wa = inp["mla_w_a"][0]
    wa_pad = np.concatenate([wa[:, :640], np.zeros((D, 64), f), wa[:, 640:672]], 1)
    wkv = inp["mla_w_kv_b"][0].reshape(256, 16, 128)
    gn = np.zeros((128, 8), f)
    gn[:, 0:3] = inp["mla_q_a_norm"][0].reshape(3, 128).T
    gn[:, 3:5] = inp["mla_kv_a_norm"][0].reshape(2, 128).T
    gn[:96, 5] = inp["mla_q_norm"][0]
    gn[:96, 6] = inp["mla_k_norm"][0]
    cos, sin, R = mla_consts()
    kpec = np.concatenate([np.zeros((256, 64), f), inp["cache_mla_kpe"][core, 0]], 1)
    return {
        "ml_wa": np.ascontiguousarray(wa_pad, f), "ml_wqb": inp["mla_w_q_b"][0],
        "ml_wk": np.ascontiguousarray(wkv[:, :, :64].reshape(256, 1024)), "ml_wv": np.ascontiguousarray(wkv[:, :, 64:].reshape(256, 1024)),
        "ml_wout": inp["mla_w_out"][0], "ml_gn": gn, "ml_cos": cos, "ml_sin": sin, "ml_R": R,
        "ml_ckvc": np.ascontiguousarray(inp["cache_mla_ckv"][core, 0]), "ml_kpec": np.ascontiguousarray(kpec, f),
    }


def kernel(**inp):
    inp = {k: np.asarray(v) for k, v in inp.items()}
    nc = build()
    in_maps = []
    for i in range(8):
        im = make_inputs(i, inp)
        in_maps.append({k: v for k, v in im.items() if k in nc.in_names})
    res = run_bass_kernel_spmd(nc, in_maps, core_ids=list(range(8)))
    r = res.results
    f = np.float32
    yp = np.stack([r[i // 2]["y"][(i % 2) * 256:(i % 2 + 1) * 256] for i in range(16)], 0).astype(f)
    ys = np.stack([r[i]["y"][512:] for i in range(8)], 0).astype(f)
    cat = lambda k: np.concatenate([np.asarray(r[i][k]) for i in range(8)], 0).astype(f)
    st_dn = cat("dn_st_out")[:, None]
    na_k = cat("na_ko")[:, None]
    na_v = cat("na_vo")[:, None]
    ckv = cat("ckv_out")[:, None]
    kpe = cat("kpe_out")[:, None]
    st_rw = cat("rw_st_out")[:, None]
    return yp, ys, st_dn, na_k, na_v, ckv, kpe, st_rw
```

```python
import numpy as np
import concourse.bass as bass
import concourse.mybir as mybir
from concourse.bass_utils import run_bass_kernel_spmd
from contextlib import ExitStack

F32 = mybir.dt.float32
BF16 = mybir.dt.bfloat16
I32 = mybir.dt.int32
AF = mybir.ActivationFunctionType
ALU = mybir.AluOpType
AX = mybir.AxisListType


class V:
    __slots__ = ("ap", "keys")

    def __init__(self, ap, keys):
        self.ap = ap
        self.keys = keys

    def __getitem__(self, idx):
        return V(self.ap[idx], self.keys)

    def re(self, pat, **kw):
        return V(self.ap.rearrange(pat, **kw), self.keys)

    def bc(self, shape):
        return V(self.ap.broadcast_to(shape), self.keys)

    def bitcast(self, dt):
        return V(self.ap.bitcast(dt), self.keys)

    def all(self):
        return self

    @property
    def shape(self):
        return self.ap.shape


class T:
    def __init__(self, name, handle, shape, split=None):
        self.name = name
        self.h = handle
        self.shape = tuple(shape)
        self.split = split
        if split is None:
            self.allkeys = frozenset([(name, 0)])
        else:
            ax, blk = split
            self.allkeys = frozenset((name, i) for i in range((shape[ax] + blk - 1) // blk))

    def __getitem__(self, idx):
        if not isinstance(idx, tuple):
            idx = (idx,)
        keys = self.allkeys
        if self.split is not None:
            ax, blk = self.split
            if ax < len(idx):
                ix = idx[ax]
                if isinstance(ix, int):
                    keys = frozenset([(self.name, ix // blk)])
                elif isinstance(ix, slice):
                    st = 0 if ix.start is None else ix.start
                    sp = self.shape[ax] if ix.stop is None else ix.stop
                    keys = frozenset((self.name, i) for i in range(st // blk, (sp - 1) // blk + 1))
        return V(self.h[idx], keys)

    def all(self):
        return V(self.h[tuple(slice(None) for _ in self.shape)], self.allkeys)


class Op:
    __slots__ = ("eng", "fn", "deps", "ddeps", "signal", "count", "dma", "idx", "cost", "alldeps")


class Prog:
    ENG = ("pe", "act", "dve", "pool", "sp")

    def __init__(self, nc):
        self.nc = nc
        self.es = ExitStack()
        self.streams = {e: [] for e in self.ENG}
        self.res = {}
        self.dsem_count = {}
        self.dsem_waited = {}
        self.psum_names = set()
        self.bank_readers = {}
        self.in_names = []
        self.ops = []
        self.last_dma = {}
        self.last_dma_sem = {}
        self.n_t = 0

    def sb(self, name, shape, dtype, split=None):
        h = self.es.enter_context(self.nc.sbuf_tensor(name, list(shape), dtype))
        return T(name, h, shape, split)

    def ps(self, name, shape, dtype, split=None):
        h = self.es.enter_context(self.nc.psum_tensor(name, list(shape), dtype))
        self.psum_names.add(name)
        return T(name, h, shape, split)

    def dram(self, name, shape, dtype, kind, split=None):
        h = self.nc.dram_tensor(name, list(shape), dtype, kind=kind).ap()
        if kind == "ExternalInput":
            self.in_names.append(name)
        return T(name, h, shape, split)

    def _st(self, k):
        s = self.res.get(k)
        if s is None:
            s = self.res[k] = [None, []]
        return s

    def _record(self, eng, fn, reads, writes, dma=None, cost=0.3):
        op = Op()
        op.eng, op.fn, op.signal, op.count, op.dma = eng, fn, False, 0, dma
        op.cost = cost
        deps = {}
        ddeps = {}
        alld = {}

        def add(ev):
            if ev is None:
                return
            alld[id(ev)] = ev
            if ev.dma is not None:
                s = ev.dma
                ld_ = self.last_dma_sem[s]
                alld[id(ld_)] = ld_
                v = self.dsem_count[s]
                if ddeps.get(s, 0) < v:
                    ddeps[s] = v
                if self.dsem_waited.get(s, 0) < v:
                    self.dsem_waited[s] = v
            else:
                if ev.eng == "pe" and eng == "pe" and dma is None:
                    return
                deps[id(ev)] = ev

        rk = set()
        for v in reads:
            rk |= v.keys
        wk = set()
        for v in writes:
            wk |= v.keys
        for k in rk:
            add(self._st(k)[0])
        for k in wk:
            s = self._st(k)
            add(s[0])
            for r in s[1]:
                add(r)
        for k in rk:
            if k[0] in self.psum_names:
                r = self.bank_readers.get(k[0])
                if r is not None:
                    if r.eng != eng:
                        add(r)
                    else:
                        alld[id(r)] = r
        op.deps = list(deps.values())
        op.ddeps = ddeps
        for d in op.deps:
            d.signal = True
        for k in rk:
            if k[0] in self.psum_names and dma is None:
                self.bank_readers[k[0]] = op
        if dma is not None:
            pw = self.dsem_waited.get(dma, 0)
            if pw > ddeps.get(dma, 0):
                ddeps[dma] = pw
            c = self.dsem_count.get(dma, 0) + 16
            self.dsem_count[dma] = c
            pq_ = self.last_dma.get(eng)
            if pq_ is not None:
                alld[id(pq_)] = pq_
            self.last_dma[eng] = op
            self.last_dma_sem[dma] = op
        op.alldeps = list(alld.values())
        for k in rk:
            if k not in wk:
                self._st(k)[1].append(op)
        for k in wk:
            s = self._st(k)
            s[0] = op
            s[1] = []
        op.idx = len(self.ops)
        self.ops.append(op)
        return op

    def op(self, eng, name, *, out=None, accum_out=None, extra_reads=(), extra_writes=(), cost_=None, **kw):
        reads = list(extra_reads)
        writes = list(extra_writes)
        args = {}
        for k, v in kw.items():
            if isinstance(v, V):
                reads.append(v)
                args[k] = v.ap
            else:
                args[k] = v
        if out is not None:
            writes.append(out)
            args["out"] = out.ap
        if accum_out is not None:
            writes.append(accum_out)
            args["accum_out"] = accum_out.ap

        def fn(e):
            return getattr(e, name)(**args)

        if cost_ is None:
            ref = out if out is not None else reads[0]
            n = int(np.prod(ref.ap.shape[1:]))
            if eng == "pe":
                cost_ = 0.3
            else:
                cost_ = 0.2 + n / 960.0
        return self._record(eng, fn, reads, writes, cost=cost_)

    def dma(self, q, out, in_, sem, **kw):
        def fn(e):
            return e.dma_start(out=out.ap, in_=in_.ap, **kw)

        nb = int(np.prod(out.ap.shape)) * 4
        return self._record(q, fn, [in_], [out], dma=sem + "_" + q, cost=nb / 150e3)

    def barrier(self):
        self.ops.append(("BARRIER", dict(self.dsem_count)))

    def mode(self, reorder):
        self.ops.append(("MODE", reorder))

    def schedule(self, reorder=True):
        import heapq
        LAT = 0.25
        streams = {e: [] for e in self.ENG}
        seg = []
        reorder0 = reorder

        def flush(seg):
            if not seg:
                return
            if not reorder:
                for o in seg:
                    streams[o.eng].append(o)
                return
            inseg = {id(o) for o in seg}
            indeg = {}
            succ = {}
            ready_t = {}
            for o in seg:
                n = 0
                for d in o.alldeps:
                    if id(d) in inseg:
                        n += 1
                        succ.setdefault(id(d), []).append(o)
                indeg[id(o)] = n
                ready_t[id(o)] = 0.0
            cp = {}
            for o in reversed(seg):
                m = 0.0
                for s_ in succ.get(id(o), ()):
                    v = cp[id(s_)]
                    if v > m:
                        m = v
                cp[id(o)] = m + o.cost + LAT
            future = {e: [] for e in self.ENG}
            avail = {e: [] for e in self.ENG}
            for o in seg:
                if indeg[id(o)] == 0:
                    heapq.heappush(avail[o.eng], (-cp[id(o)], o.idx, o))
            free = {e: 0.0 for e in self.ENG}
            left = len(seg)
            while left:
                best = None
                for e in self.ENG:
                    fu, av = future[e], avail[e]
                    while fu and fu[0][0] <= free[e]:
                        _, i_, o_ = heapq.heappop(fu)
                        heapq.heappush(av, (-cp[id(o_)], i_, o_))
                    if av:
                        cand = (free[e], av[0][0], e, 0)
                    elif fu:
                        cand = (fu[0][0], -cp[id(fu[0][2])], e, 1)
                    else:
                        continue
                    if best is None or cand[:2] < best[:2]:
                        best = cand
                st_, _, e, src = best
                if src == 0:
                    _, _, o = heapq.heappop(avail[e])
                else:
                    _, _, o = heapq.heappop(future[e])
                left -= 1
                streams[e].append(o)
                if o.dma is not None:
                    free[e] = st_ + 0.1
                    fin = st_ + 2.0 + o.cost
                else:
                    fin = st_ + o.cost
                    free[e] = fin
                for s_ in succ.get(id(o), ()):
                    k = id(s_)
                    if ready_t[k] < fin + LAT:
                        ready_t[k] = fin + LAT
                    indeg[k] -= 1
                    if indeg[k] == 0:
                        heapq.heappush(future[s_.eng], (ready_t[k], s_.idx, s_))

        for it in self.ops:
            if isinstance(it, tuple) and it[0] == "MODE":
                flush(seg)
                seg = []
                reorder = it[1] and reorder0
                continue
            if isinstance(it, tuple):
                flush(seg)
                seg = []
                lasts = []
                for e in self.ENG:
                    for o in reversed(streams[e]):
                        if o.dma is None and o.fn is not None:
                            lasts.append(o)
                            break
                for e in self.ENG:
                    b_ = Op()
                    b_.eng, b_.fn, b_.signal, b_.count, b_.dma = e, None, False, 0, None
                    b_.deps = [x for x in lasts if x.eng != e]
                    b_.ddeps = dict(it[1])
                    for d in b_.deps:
                        d.signal = True
                    streams[e].append(b_)
            else:
                seg.append(it)
        flush(seg)
        self.streams = streams

    def emit(self, final_wait=True, reorder=True):
        self.schedule(reorder)
        nc = self.nc
        es = self.es
        esem = {e: es.enter_context(nc.semaphore("sem_" + e)) for e in self.ENG}
        dsem = {n: es.enter_context(nc.semaphore("d_" + n)) for n in self.dsem_count}
        for e in self.ENG:
            c = 0
            for op in self.streams[e]:
                if op.signal and op.dma is None:
                    c += 1
                    op.count = c
        block = es.enter_context(nc.Block())
        streams = self.streams
        dcount = self.dsem_count

        def run(ename):
            def body(eng):
                waited = {}
                for op in streams[ename]:
                    w = {}
                    for d in op.deps:
                        s = esem[d.eng]
                        if w.get(s, (0,))[0] < d.count:
                            w[s] = (d.count, s)
                    for sn, v in op.ddeps.items():
                        s = dsem[sn]
                        if w.get(s, (0,))[0] < v:
                            w[s] = (v, s)
                    for s, (v, _) in w.items():
                        if waited.get(s, 0) < v:
                            eng.wait_ge(s, v)
                            waited[s] = v
                    if op.fn is None:
                        continue
                    ins = op.fn(eng)
                    if op.dma is not None:
                        ins.then_inc(dsem[op.dma], 16)
                    elif op.signal:
                        ins.then_inc(esem[ename], 1)
                if ename == "sp" and final_wait:
                    for sn, c in dcount.items():
                        eng.wait_ge(dsem[sn], c)
            return body

        block.tensor(run("pe"))
        block.scalar(run("act"))
        block.vector(run("dve"))
        block.gpsimd(run("pool"))
        block.sync(run("sp"))
        es.close()

    def mm(self, out, lhsT, rhs, start=True, stop=True, **kw):
        n = int(np.prod(rhs.ap.shape[1:]))
        c = 0.04 + n / 2400.0 * (4.0 if rhs.ap.dtype == F32 else 1.0)
        return self.op("pe", "matmul", out=out, lhsT=lhsT, rhs=rhs, start=start, stop=stop, cost_=c, **kw)

    def tr(self, out, in_, ident):
        return self.op("pe", "transpose", out=out, in_=in_, identity=ident)

    def act(self, out, in_, func, eng="act", **kw):
        return self.op(eng, "activation", out=out, in_=in_, func=func, **kw)

    def tt(self, eng, out, in0, in1, op):
        return self.op(eng, "tensor_tensor", out=out, in0=in0, in1=in1, op=op)

    def ts(self, eng, out, in0, s1, op0, s2=None, op1=None, **kw):
        if op1 is None:
            return self.op(eng, "tensor_scalar", out=out, in0=in0, scalar1=s1, scalar2=s2, op0=op0, **kw)
        return self.op(eng, "tensor_scalar", out=out, in0=in0, scalar1=s1, scalar2=s2, op0=op0, op1=op1, **kw)

    def stt(self, eng, out, in0, scalar, in1, op0, op1, **kw):
        return self.op(eng, "scalar_tensor_tensor", out=out, in0=in0, scalar=scalar, in1=in1, op0=op0, op1=op1, **kw)

    def copy(self, eng, out, in_):
        if eng == "act":
            return self.op("act", "copy", out=out, in_=in_)
        return self.op(eng, "tensor_copy", out=out, in_=in_)

    def memset(self, eng, out, val):
        def fn(e):
            return e.memset(out.ap, val)
        return self._record(eng, fn, [], [out])


D = 1024
NT = 1536
NG = 3
DFF = 2816
NF = 22
DEPTH = 4
ARENA = 72 * 1024
REORDER = True


class Ctx:
    pass


def build(layers=(0, 1, 2, 3), mixers=True):
    nc = bass.Bass("TRN2", target_bir_lowering=False)
    P = Prog(nc)
    g = Ctx()
    g.P = P
    g.nc = nc
    g.x = P.dram("x", [NT, D], F32, "ExternalInput")
    g.y = P.dram("y", [NT, D], F32, "ExternalOutput")
    g.cv = P.dram("cv", [128, 8, 2], F32, "ExternalInput")
    g.ada_w = P.dram("ada_w", [DEPTH, D, 6 * D], F32, "ExternalInput")
    g.adabT = P.dram("adabT", [128, DEPTH, 48], F32, "ExternalInput")
    g.nrmT = P.dram("nrmT", [128, 2, DEPTH, 8], F32, "ExternalInput")
    g.ffn_w_in = P.dram("ffn_w_in", [DEPTH, D, 2 * DFF], F32, "ExternalInput")
    g.ffn_w_out = P.dram("ffn_w_out", [DEPTH, DFF, D], F32, "ExternalInput")
    g.identd = P.dram("ident", [128, 128], F32, "ExternalInput")
    g.xT = P.sb("xT", [128, 8, NT], F32, split=(2, 512))
    g.hT = P.sb("hT", [128, 8, NT], BF16, split=(2, 512))
    g.ident = P.sb("identS", [128, 128], F32)
    g.identb = P.sb("identB", [128, 128], BF16)
    g.onesb = P.sb("onesb", [128, 128], BF16)
    g.sc = P.sb("sc", [128, 8, 2], BF16)
    g.cvs = P.sb("cvs", [128, 8, 2], F32)
    g.adab = P.sb("adab", [128, DEPTH, 48], F32)
    g.nrm = P.sb("nrm", [128, 2, DEPTH, 8], F32)
    g.mod = [P.sb("mod%d" % l, [128, 48, 2], F32) for l in range(DEPTH)]
    g.gs = [P.sb("gs%d" % l, [128, 2, 8, 2], F32) for l in range(DEPTH)]
    g.arena = P.sb("arena", [128, ARENA // 4], F32)
    g.psb = [P.ps("ps%d" % i, [128, 512], F32) for i in range(8)]
    g.pq = 0
    g.pi = 0
    g.psrot = list(range(8))
    g.nsq = g.nrc = g.nE = 0
    g.ones2 = P.sb("ones2", [128, 128], BF16)
    g.onesb1 = P.sb("onesb1", [128, 128], BF16)
    g.Ebuf = [P.sb("Ebuf%d" % i, [128, 512], BF16) for i in range(3)]
    if 1 in layers and mixers:
        g.na_w_qkv = P.dram("na_w_qkv", [D, 3 * D], F32, "ExternalInput")
        g.na_w_out = P.dram("na_w_out", [D, D], F32, "ExternalInput")
        g.na_gn_d = P.dram("na_gn", [128, 2], F32, "ExternalInput")
        g.na_cm_d = P.dram("na_cm", [128, 12, 512], F32, "ExternalInput")
        g.na_tz_d = P.dram("na_tz", [16, 128, 31 * 64], F32, "ExternalInput")
        g.na_kc_d = P.dram("na_kc", [16, 256, 64], F32, "ExternalInput")
        g.na_vc_d = P.dram("na_vc", [16, 256, 64], F32, "ExternalInput")
        g.na_ko = P.dram("na_ko", [2, 16, 256, 64], F32, "ExternalOutput")
        g.na_vo = P.dram("na_vo", [2, 16, 256, 64], F32, "ExternalOutput")
    if 0 in layers and mixers:
        g.dn_w_in = P.dram("dn_w_in", [D, 4 * D], F32, "ExternalInput")
        g.dn_wg_d = P.dram("dn_wg", [D, 32], F32, "ExternalInput")
        g.dn_vec_d = P.dram("dn_vec", [128, 32], F32, "ExternalInput")
        g.dn_cw_d = P.dram("dn_cw", [128, 72], F32, "ExternalInput")
        g.dn_gout_d = P.dram("dn_gout", [128, 1], F32, "ExternalInput")
        g.dn_mk_d = P.dram("dn_mk", [128, 11, 128], F32, "ExternalInput")
        g.dn_w_out = P.dram("dn_w_out", [D, D], F32, "ExternalInput")
        g.dn_s0_d = P.dram("dn_s0", [2, 8, 128, 128], F32, "ExternalInput")
        g.dn_st_out = P.dram("dn_st_out", [2, 2, 8, 128, 128], F32, "ExternalOutput")
    if 3 in layers and mixers:
        if not hasattr(g, "dn_mk_d"):
            g.dn_mk_d = P.dram("dn_mk", [128, 11, 128], F32, "ExternalInput")
        g.rw_pv_d = P.dram("rw_pv", [128, 120], F32, "ExternalInput")
        g.rw_wrkv_d = P.dram("rw_wrkv", [3, D, D], F32, "ExternalInput")
        g.rw_w1_d = P.dram("rw_w1", [D, 128], F32, "ExternalInput")
        g.rw_a1_d = P.dram("rw_a1", [D, 128], F32, "ExternalInput")
        g.rw_g1_d = P.dram("rw_g1", [D, 128], F32, "ExternalInput")
        g.rw_l2_d = P.dram("rw_l2", [128, 3, D], F32, "ExternalInput")
        g.rw_wout_d = P.dram("rw_wout", [D, D], F32, "ExternalInput")
        g.rw_z0_d = P.dram("rw_z0", [2, 8, 128, 64], F32, "ExternalInput")
        g.rw_st_out = P.dram("rw_st_out", [2, 2, 16, 64, 64], F32, "ExternalOutput")
        g.rw_oT = P.sb("rw_oT", [128, NT], BF16, split=(1, 512))
        g.rwh = P.sb("rw_wh", [128, 8, 128], BF16)
        g.onesblk = P.sb("onesblk", [128, 128], BF16)
        g.eps_ln = P.sb("eps_ln", [128, 2], F32)
        P.memset("dve", g.eps_ln.all(), 64e-5)
        P.memset("dve", g.onesblk.all(), 0.0)
        P.memset("dve", g.onesblk[0:64, 0:64], 1.0)
        P.memset("dve", g.onesblk[64:128, 64:128], 1.0)
    if 2 in layers and mixers:
        g.ml_wa = P.dram("ml_wa", [D, 736], F32, "ExternalInput")
        g.ml_wqb = P.dram("ml_wqb", [384, 1536], F32, "ExternalInput")
        g.ml_wk = P.dram("ml_wk", [256, 1024], F32, "ExternalInput")
        g.ml_wv = P.dram("ml_wv", [256, 1024], F32, "ExternalInput")
        g.ml_wout = P.dram("ml_wout", [D, D], F32, "ExternalInput")
        g.ml_gn_d = P.dram("ml_gn", [128, 8], F32, "ExternalInput")
        g.ml_cos_d = P.dram("ml_cos", [96, 1024], F32, "ExternalInput")
        g.ml_sin_d = P.dram("ml_sin", [96, 1024], F32, "ExternalInput")
        g.ml_R_d = P.dram("ml_R", [96, 96], F32, "ExternalInput")
        g.ml_ckvc = P.dram("ml_ckvc", [256, 256], F32, "ExternalInput")
        g.ml_kpec = P.dram("ml_kpec", [256, 96], F32, "ExternalInput")
        g.ckv_out = P.dram("ckv_out", [2, 256, 256], F32, "ExternalOutput")
        g.kpe_out = P.dram("kpe_out", [2, 256, 32], F32, "ExternalOutput")

    P.dma("sp", g.ident.all(), g.identd.all(), "c0")
    P.dma("sp", g.cvs.all(), g.cv.all(), "c0")
    P.dma("sp", g.adab.all(), g.adabT.all(), "c0")
    P.dma("sp", g.nrm.all(), g.nrmT.all(), "c0")
    P.copy("dve", g.identb.all(), g.ident.all())
    P.memset("dve", g.onesb.all(), 1.0 / D)
    P.memset("dve", g.onesb1.all(), 1.0)
    P.memset("dve", g.ones2.all(), 0.0)
    P.memset("dve", g.ones2[0:64, 0:64], 1.0 / 64)
    P.memset("dve", g.ones2[64:128, 64:128], 1.0 / 64)
    g.sq = [P.sb("sq%d" % i, [128, 512], BF16) for i in range(2)]
    g.rstd = P.sb("rstd", [128, 512], F32)
    g.ntmp = [P.sb("ntmp%d" % i, [128, 512], F32) for i in range(2)]
    wbufs(g)
    g.one1 = P.sb("one1", [128, 2], F32)
    P.memset("dve", g.one1.all(), 1.0)
    g.eps6 = P.sb("eps6", [128, 2], F32)
    P.memset("dve", g.eps6.all(), 1e-6)
    P.act(g.sc.all(), g.cvs.all(), AF.Silu)

    load_x(g)
    first = True
    for l in layers:
        adaln(g, l)
        norm_mod(g, l, 0)
        if mixers:
            [dn_layer, na_layer, mla_layer, rw_layer][l % 4](g, l)
        norm_mod(g, l, 1)
        ffn(g, l)
    store_y(g)
    P.emit(reorder=REORDER)
    nc.in_names = list(P.in_names)
    global LASTP
    LASTP = P
    return nc


def carve(g, name, off, shape, dtype, split=None):
    n = int(np.prod(shape[1:]))
    esz = 4 if dtype == F32 else 2
    assert off % 4 == 0 and off + n * esz <= ARENA
    ap = g.arena.h[:, off // 4:(off + n * esz) // 4]
    if dtype != F32:
        ap = ap.bitcast(dtype)
    if len(shape) == 3:
        ap = ap.rearrange("p (a b) -> p a b", a=shape[1])
    elif len(shape) == 4:
        ap = ap.rearrange("p (a b c) -> p a b c", a=shape[1], b=shape[2])
    return T(name, ap, shape, split)


class PsCtx:
    def __init__(self, rot):
        self.psrot = rot
        self.pos = 0

    def take(self, g, k):
        total = 4 * len(self.psrot)
        if self.pos % k:
            self.pos += k - self.pos % k
        s_ = self.pos % total
        self.pos += k
        if k == 4:
            return g.psb[self.psrot[s_ // 4]], 0
        nb = len(self.psrot)
        p_ = s_ // 2
        return g.psb[self.psrot[p_ % nb]], ((p_ // nb) % 2) * 256 + (s_ % 2) * 128


def nextq(g, c=None):
    if c is not None:
        b, o = c.take(g, 1)
        return b[:, o:o + 128]
    r = g.psrot
    n = g.pq % (4 * len(r))
    g.pq += 1
    b = g.psb[r[n % len(r)]]
    q = n // len(r)
    return b[:, q * 128:(q + 1) * 128]


def nexth(g, c):
    b, o = c.take(g, 2)
    return b[:, o:o + 256]


class Pool:
    def __init__(self, tiles):
        self.t = tiles
        self.i = 0

    def get(self):
        x = self.t[self.i % len(self.t)]
        self.i += 1
        return x


def apool(g, name, off, n, shape, dtype, split=None):
    sz = int(np.prod(shape[1:])) * (4 if dtype == F32 else 2)
    return Pool([carve(g, "%s%d" % (name, i), off + i * sz, shape, dtype, split) for i in range(n)]), off + n * sz


def nextps(g, c=None):
    if c is not None:
        return c.take(g, 4)[0]
    r = g.psrot
    p = g.psb[r[g.pi % len(r)]]
    g.pi += 1
    return p


def load_x(g):
    P = g.P
    xin = [carve(g, "xin%d" % i, i * 4096, [128, D], F32) for i in range(2)]
    for t in range(NT // 128):
        b = xin[t % 2]
        P.dma("sp", b.all(), g.x[t * 128:(t + 1) * 128, :], "xin%d" % (t % 2))
        for hf in range(2):
            ps = nextps(g)
            for j in range(4):
                c = hf * 4 + j
                P.tr(ps[:, j * 128:(j + 1) * 128], b[:, c * 128:(c + 1) * 128], g.ident.all())
            P.copy("dve" if hf == 0 else "act", g.xT[:, hf * 4:(hf + 1) * 4, t * 128:(t + 1) * 128],
                   ps.all().re("p (j t) -> p j t", j=4))
    g.xin = xin
    P.barrier()


def store_y(g):
    P = g.P
    P.barrier()
    for t in range(NT // 128):
        b = g.xin[t % 2]
        for hf in range(2):
            ps = nextps(g)
            for j in range(4):
                c = hf * 4 + j
                P.tr(ps[:, j * 128:(j + 1) * 128], g.xT[:, c, t * 128:(t + 1) * 128], g.ident.all())
            P.copy("dve" if hf == 0 else "act", b[:, hf * 512:(hf + 1) * 512], ps.all())
        P.dma("sp", g.y[t * 128:(t + 1) * 128, :], b.all(), "xin%d" % (t % 2))


def adaln(g, l):
    P = g.P
    wbufs(g)
    P.barrier()
    g.awb = g.wi
    g.rowb = carve(g, "rowb", 0, [128, 6 * D], F32)
    src = g.ada_w.h[l].rearrange("(k p) n -> p k n", p=128)
    for n in range(12):
        wb = g.awb[n % 2]
        P.dma("pool", wb.all(), V(src[:, :, n * 512:(n + 1) * 512], g.ada_w.allkeys), "wi%d" % (n % 2))
        ps = nextps(g)
        for k in range(8):
            P.mm(ps[0:2, :], g.sc[:, k, :], wb[:, k, :], start=(k == 0), stop=(k == 7))
        P.copy("act", g.rowb[0:2, n * 512:(n + 1) * 512], ps[0:2, :])
    ps = nextps(g)
    for j in range(48):
        P.tr(ps[:, j * 2:(j + 1) * 2], g.rowb[0:2, j * 128:(j + 1) * 128], g.ident[0:2, 0:2])
    m = g.mod[l]
    P.tt("dve", m.all(), ps[:, 0:96].re("p (j s) -> p j s", s=2),
         g.adab[:, l, :].re("p (j o) -> p j o", o=1).bc([128, 48, 2]), ALU.add)
    for w in range(2):
        sl = m[:, (3 * w + 1) * 8:(3 * w + 2) * 8, :]
        P.stt("dve", g.gs[l][:, w, :, :], sl, 1.0,
              g.nrm[:, w, l, :].re("p (c o) -> p c o", o=1).bc([128, 8, 2]), ALU.add, ALU.mult)
    P.barrier()


def norm_mod(g, l, w):
    P = g.P
    m = g.mod[l]
    for tg in range(NG):
        cd = 0 if tg == 0 else 1
        ts_ = slice(tg * 512, (tg + 1) * 512)
        ps = nextps(g)
        for c in range(8):
            sq = g.sq[c % 2]
            P.act(sq.all(), g.xT[:, c, ts_], AF.Square)
            P.mm(ps.all(), g.onesb.all(), sq.all(), start=(c == 0), stop=(c == 7))
        P.act(g.rstd.all(), ps.all(), AF.Ln, bias=g.eps6[:, 0:1], scale=1.0)
        P.act(g.rstd.all(), g.rstd.all(), AF.Exp, scale=-0.5)
        for c in range(8):
            tmp = g.ntmp[c % 2]
            P.tt("dve", tmp.all(), g.xT[:, c, ts_], g.rstd.all(), ALU.mult)
            P.act(g.hT[:, c, ts_], tmp.all(), AF.Identity,
                  scale=g.gs[l][:, w, c, cd:cd + 1], bias=m[:, (3 * w) * 8 + c, cd:cd + 1])


def wbufs(g):
    P = g.P
    if not hasattr(g, "wi"):
        g.wi = [P.sb("wi%d" % i, [128, 8, 512], BF16) for i in range(3)]
        g.wo = [P.sb("wo%d" % i, [128, NF, 128], BF16) for i in range(2)]
        g.sg = [P.sb("sg%d" % i, [128, 512], F32) for i in range(2)]


def chunk_masks():
    i = np.arange(128)
    same = (i[:, None] // 64) == (i[None, :] // 64)
    le = i[:, None] <= i[None, :]
    ge = i[:, None] >= i[None, :]
    f = np.float32
    McumF = (same & le).astype(f)
    McumB = (same & ge).astype(f)
    validF = same & ge
    validB = same & le
    NEGF = np.where(validF, 0.0, -1e30).astype(f)
    NEGB = np.where(validB, 0.0, -1e30).astype(f)
    SMF = (validF & (i[:, None] != i[None, :])).astype(f)
    SMB = (validB & (i[:, None] != i[None, :])).astype(f)
    Mblk = same.astype(f)
    Mch0 = np.repeat((i < 64).astype(f)[:, None], 128, 1)
    Mch1 = np.repeat((i >= 64).astype(f)[:, None], 128, 1)
    return np.ascontiguousarray(np.stack([McumF, McumB, NEGF, NEGB, SMF, SMB, Mblk, Mch0, Mch1,
                                          validF.astype(f), validB.astype(f)], 1))


def neumann_solve(g, N, Nt, tp, steps=5):
    P = g.P
    Tt = tp.get()
    P.tt("dve", Tt.all(), Nt.all(), g.ident.all(), ALU.add)
    Pm, Pt = N, Nt
    for k in range(steps):
        q1 = nextq(g)
        P.mm(q1, Pt.all(), Pm.all())
        Pn = tp.get()
        P.copy("act", Pn.all(), q1)
        if k < steps - 1:
            q2 = nextq(g)
            P.mm(q2, Pm.all(), Pt.all())
            Ptn = tp.get()
            P.copy("act", Ptn.all(), q2)
        q3 = nextq(g)
        P.mm(q3, Pn.all(), Tt.all())
        Ttn = tp.get()
        P.tt("dve", Ttn.all(), q3, Tt.all(), ALU.add)
        Tt = Ttn
        Pm = Pn
        if k < steps - 1:
            Pt = Ptn
    return Tt


def interleave(gens):
    gens = list(gens)
    while gens:
        for gn in list(gens):
            try:
                next(gn)
            except StopIteration:
                gens.remove(gn)


def interleave_gen(gens):
    gens = list(gens)
    while gens:
        for gn in list(gens):
            try:
                next(gn)
            except StopIteration:
                gens.remove(gn)
        yield


def neumann_solve_gen(g, N, Nt, tp, steps=5, pc=None):
    P = g.P
    Tt = tp.get()
    P.tt("dve", Tt.all(), Nt.all(), g.ident.all(), ALU.add)
    Pm, Pt = N, Nt
    for k in range(steps):
        q1 = nextq(g, pc)
        P.mm(q1, Pt.all(), Pm.all())
        if k < steps - 1:
            q2 = nextq(g, pc)
            P.mm(q2, Pm.all(), Pt.all())
        yield
        Pn = tp.get()
        P.copy("act", Pn.all(), q1)
        if k < steps - 1:
            Ptn = tp.get()
            P.copy("dve", Ptn.all(), q2)
        yield
        q3 = nextq(g, pc)
        P.mm(q3, Pn.all(), Tt.all())
        yield
        Ttn = tp.get()
        P.tt("dve", Ttn.all(), q3, Tt.all(), ALU.add)
        Tt = Ttn
        Pm = Pn
        if k < steps - 1:
            Pt = Ptn
    return Tt


SEQS = ((0, 2), (2, 2), (4, 8))


def dn_layer(g, l):
    P = g.P
    P.barrier()
    P.mode(True)
    off = 0
    MK = carve(g, "dn_MK", off, [128, 11, 128], F32); off += 11 * 512
    ones = carve(g, "dn_ones", off, [128, 128], F32); off += 512
    raw = carve(g, "dn_raw", off, [128, NT], F32, split=(1, 512)); off += 6144
    qT = carve(g, "dn_qT", off, [128, NT], F32, split=(1, 128)); off += 6144
    kT = carve(g, "dn_kT", off, [128, NT], F32, split=(1, 128)); off += 6144
    vT = carve(g, "dn_vT", off, [128, NT], F32, split=(1, 128)); off += 6144
    zs = raw
    oTh = carve(g, "dn_oTh", off, [128, NT], BF16, split=(1, 512)); off += 3072
    qtok = carve(g, "dn_qtok", off, [128, 12, 128], F32, split=(1, 1)); off += 6144
    ktok = carve(g, "dn_ktok", off, [128, 12, 128], F32, split=(1, 1)); off += 6144
    vtok = carve(g, "dn_vtok", off, [128, 12, 128], F32, split=(1, 1)); off += 6144
    oacc = T("dn_vT", vT.h.rearrange("p (t c) -> p t c", t=12), [128, 12, 128], split=(1, 1))
    tp0, off = apool(g, "dn_tp", off, 9, [128, 128], F32)
    tq0, off = apool(g, "dn_tq", off, 6, [128, 128], F32)
    lp0, off = apool(g, "dn_lp", off, 3, [128, 6, 128], F32)
    uw0, off = apool(g, "dn_uw", off, 2, [128, 256], F32)
    Sb, off = apool(g, "dn_S", off, 4, [128, 128], F32)
    assert off <= ARENA, off
    X1 = T("dnx1", g.wi[1].h[:, :, :].rearrange("p a b -> p (a b)").bitcast(F32), [128, 2048], split=(1, 128))
    X2 = T("dnx2", g.wo[1].h[:, :, :].rearrange("p a b -> p (a b)").bitcast(F32), [128, 1408], split=(1, 128))
    tp1 = Pool([X1[:, i * 128:(i + 1) * 128] for i in range(9)])
    tq1 = Pool([X1[:, i * 128:(i + 1) * 128] for i in range(9, 15)])
    uw1 = Pool([X2[:, 0:256], X2[:, 256:512]])
    lp1 = Pool([X2[:, 512:1280].re("p (s c) -> p s c", s=6), lp0.t[2]])
    lp0 = Pool(lp0.t[0:2])
    tpd, tqd, uwd, lpd = [tp0, tp1], [tq0, tq1], [uw0, uw1], [lp0, lp1]
    pcs = [PsCtx([0, 1, 2, 3]), PsCtx([4, 5, 6, 7])]
    tp = tp0
    tb = g.wi[2].all().re("p a b -> p (a b)").bitcast(F32)
    gt = tb[:, 0:384].re("p (t c) -> p t c", t=12)
    gg = tb[:, 384:576].re("p (t c) -> p t c", t=12)
    gc = tb[:, 576:768].re("p (t c) -> p t c", t=12)
    egc = tb[:, 768:960].re("p (t c) -> p t c", t=12)
    egk = tb[:, 960:1152].re("p (t c) -> p t c", t=12)
    egl = tb[:, 1152:1536].re("p (t k c) -> p t k c", t=12, k=2)
    bex = tb[:, 1536:1728].re("p (t c) -> p t c", t=12)
    nbe = tb[:, 1728:1920].re("p (t c) -> p t c", t=12)
    vec = tb[:, 1920:1952]
    cw = tb[:, 1952:2024].re("p (h s k) -> p h s k", h=8, s=3)
    gout = tb[:, 2024:2025]
    P.dma("sp", MK.all(), g.dn_mk_d.all(), "c3")
    P.dma("sp", vec, g.dn_vec_d.all(), "c3")
    P.dma("sp", cw, g.dn_cw_d.all(), "c3")
    P.dma("sp", gout, g.dn_gout_d.all(), "c3")
    P.memset("dve", ones.all(), 1.0)
    McumD = [MK[:, 0, :], MK[:, 1, :]]
    NEGD = [MK[:, 2, :], MK[:, 3, :]]
    SMD = [MK[:, 4, :], MK[:, 5, :]]
    Mblk = MK[:, 6, :]
    Mch = [MK[:, 7, :], MK[:, 8, :]]
    g.psrot = [0, 1, 2, 3, 4, 5, 6, 7]
    wg = g.wo[0]
    P.dma("pool", wg[:, 0:8, 0:32], V(g.dn_wg_d.h.rearrange("(k p) n -> p k n", p=128), g.dn_wg_d.allkeys), "wo0")
    pg = g.psb[6]
    for t in range(12):
        for k in range(8):
            P.mm(pg[:, t * 32:(t + 1) * 32], g.hT[:, k, t * 128:(t + 1) * 128], wg[:, k, 0:32], start=(k == 0), stop=(k == 7))
    P.copy("dve", gt, pg[:, 0:384].re("p (t c) -> p t c", t=12))
    P.tt("dve", gg, gt[:, :, 0:16], vec[:, 16:32].re("p (o c) -> p o c", o=1).bc([128, 12, 16]), ALU.add)
    P.act(gg, gg, AF.Exp)
    P.act(gg, gg, AF.Ln, bias=g.one1[:, 0:1], scale=1.0)
    P.act(vec[:, 0:16], vec[:, 0:16], AF.Exp)
    P.stt("dve", gg, gg, -1.0, vec[:, 0:16].re("p (o c) -> p o c", o=1).bc([128, 12, 16]), ALU.mult, ALU.mult)
    P.act(gt[:, :, 16:32], gt[:, :, 16:32], AF.Sigmoid)
    beta = gt[:, :, 16:32]
    pc = g.psb[7]
    for t in range(12):
        for d_ in range(2):
            P.mm(pc[:, t * 16 + d_ * 8:t * 16 + (d_ + 1) * 8], McumD[d_], gg[:, t, d_ * 8:(d_ + 1) * 8])
    P.copy("dve", gc, pc[:, 0:192].re("p (t c) -> p t c", t=12))
    P.act(egc, gc, AF.Exp)
    pc2 = g.psb[6]
    for t in range(12):
        P.mm(pc2[:, t * 16:(t + 1) * 16], Mblk, gg[:, t, :])
    P.tt("dve", egk, pc2[:, 0:192].re("p (t c) -> p t c", t=12), gc, ALU.subtract)
    P.act(egk, egk, AF.Exp)
    pc3 = g.psb[7]
    for t in range(12):
        for c in range(2):
            P.mm(pc3[:, (t * 2 + c) * 16:(t * 2 + c + 1) * 16], Mch[c], gg[:, t, :])
    P.act(egl, pc3[:, 0:384].re("p (t k c) -> p t k c", t=12, k=2), AF.Exp)
    P.tt("dve", bex, beta, egc, ALU.mult)
    P.ts("dve", nbe, beta, -1.0, ALU.mult)
    P.ts("dve", gg, gg, -1.0, ALU.mult)
    win = g.dn_w_in.h.rearrange("(k p) n -> p k n", p=128)
    wout = g.dn_w_out.h
    m = g.mod[l]
    for h in range(8):
        wb = g.wi[0]
        for i_ in range(4):
            P.dma("pool", wb[:, :, i_ * 128:(i_ + 1) * 128], V(win[:, :, i_ * 1024 + h * 128:i_ * 1024 + (h + 1) * 128], g.dn_w_in.allkeys), "wi0")
        for i_, dst in enumerate((qT, kT, vT, None)):
            for tg in range(NG):
                ts_ = slice(tg * 512, (tg + 1) * 512)
                ps = nextps(g)
                for k in range(8):
                    P.mm(ps.all(), wb[:, k, i_ * 128:(i_ + 1) * 128], g.hT[:, k, ts_], start=(k == 0), stop=(k == 7))
                if dst is None:
                    P.act(zs[:, ts_], ps.all(), AF.Silu)
                else:
                    P.copy("act", raw[:, ts_], ps.all())
            if dst is None:
                continue
            P.ts("dve", dst.all(), raw.all(), cw[:, h, i_, 1:2], ALU.mult)
            for (s0, e0) in ((0, 256), (256, 512), (512, NT)):
                P.stt("dve", dst[:, s0 + 1:e0], raw[:, s0:e0 - 1], cw[:, h, i_, 0:1], dst[:, s0 + 1:e0], ALU.mult, ALU.add)
                P.stt("dve", dst[:, s0:e0 - 1], raw[:, s0 + 1:e0], cw[:, h, i_, 2:3], dst[:, s0:e0 - 1], ALU.mult, ALU.add)
            for tg in range(NG):
                ts_ = slice(tg * 512, (tg + 1) * 512)
                P.act(dst[:, ts_], dst[:, ts_], AF.Silu)
                if i_ < 2:
                    sq = g.sq[g.nsq % 2]
                    g.nsq += 1
                    P.act(sq.all(), dst[:, ts_], AF.Square)
                    pm = nextps(g)
                    P.mm(pm.all(), g.onesb1.all(), sq.all())
                    P.act(g.rstd.all(), pm.all(), AF.Ln, bias=g.eps6[:, 0:1], scale=1.0)
                    P.act(g.rstd.all(), g.rstd.all(), AF.Exp, scale=-0.5)
                    P.stt("dve", dst[:, ts_], dst[:, ts_], (128 ** -0.5) if i_ == 0 else 1.0, g.rstd.all(), ALU.mult, ALU.mult)
        for src, dst in ((qT, qtok), (kT, ktok), (vT, vtok)):
            for t4 in range(3):
                ps = nextps(g)
                for j in range(4):
                    t = t4 * 4 + j
                    P.tr(ps[:, j * 128:(j + 1) * 128], src[:, t * 128:(t + 1) * 128], g.ident.all())
                P.copy("act", dst[:, t4 * 4:(t4 + 1) * 4, :], ps.all().re("p (t c) -> p t c", t=4))
        done = set()
        for si, (t0, nt) in enumerate(SEQS):
            Sc = [None, None]
            for d_ in range(2):
                Sc[d_] = Sb.get()
                if si < 2:
                    P.memset("dve", Sc[d_].all(), 0.0)
                else:
                    P.dma("sp", Sc[d_].all(), g.dn_s0_d[d_, h, :, :], "dn_s0")
            def dn_body(d_, step):
                t = t0 + step if d_ == 0 else t0 + nt - 1 - step
                col = d_ * 8 + h
                tk = slice(t * 128, (t + 1) * 128)
                pG = nextq(g, pcs[d_])
                P.mm(pG, kT[:, tk], kT[:, tk])
                pQK = nextq(g, pcs[d_])
                P.mm(pQK, qT[:, tk], kT[:, tk])
                ngb = tpd[d_].get()
                P.ts("dve", ngb.all(), ones.all(), gg[:, t, col:col + 1], ALU.mult)
                pA = nextq(g, pcs[d_])
                P.mm(pA, ngb.all(), McumD[d_], start=True, stop=False)
                P.mm(pA, g.ident.all(), NEGD[d_], start=False, stop=True)
                yield
                Dm = tpd[d_].get()
                P.act(Dm.all(), pA, AF.Exp, bias=gc[:, t, col:col + 1], scale=1.0)
                Ds = tpd[d_].get()
                P.tt("dve", Ds.all(), Dm.all(), SMD[d_], ALU.mult)
                N = tpd[d_].get()
                P.stt("dve", N.all(), pG, nbe[:, t, col:col + 1], Ds.all(), ALU.mult, ALU.mult)
                attn = tpd[d_].get()
                P.tt("dve", attn.all(), pQK, Dm.all(), ALU.mult)
                yield
                pT1 = nextq(g, pcs[d_])
                P.tr(pT1, N.all(), g.ident.all())
                yield
                Nt = tpd[d_].get()
                P.copy("act", Nt.all(), pT1)
                pT2 = nextq(g, pcs[d_])
                P.tr(pT2, attn.all(), g.ident.all())
                attnT = tpd[d_].get()
                P.copy("act", attnT.all(), pT2)
                Tt = yield from neumann_solve_gen(g, N, Nt, tqd[d_], pc=pcs[d_])
                yield
                rhs = uwd[d_].get()
                P.ts("dve", rhs[:, 0:128], vtok[:, t, :], beta[:, t, col:col + 1], ALU.mult)
                P.ts("dve", rhs[:, 128:256], ktok[:, t, :], bex[:, t, col:col + 1], ALU.mult)
                pU = nexth(g, pcs[d_])
                P.mm(pU[:, 0:256], Tt.all(), rhs.all())
                yield
                UW = uwd[d_].get()
                P.copy("act", UW.all(), pU[:, 0:256])
                L = lpd[d_].get()
                pq_ = nextq(g, pcs[d_])
                P.mm(pq_, attnT.all(), UW[:, 128:256])
                yield
                Qh = tpd[d_].get()
                P.stt("dve", Qh.all(), qtok[:, t, :], egc[:, t, col:col + 1], pq_, ALU.mult, ALU.subtract)
                yield
                pq2 = nextq(g, pcs[d_])
                P.tr(pq2, Qh.all(), g.ident.all())
                P.copy("act", L[:, 0, :], pq2)
                pq3 = nextq(g, pcs[d_])
                P.mm(pq3, attnT.all(), UW[:, 0:128])
                P.copy("act", L[:, 1, :], pq3)
                yield
                kg = tpd[d_].get()
                P.ts("dve", kg.all(), ktok[:, t, :], egk[:, t, col:col + 1], ALU.mult)
                for c in range(2):
                    cr = slice(c * 64, (c + 1) * 64)
                    pp = nextq(g, pcs[d_])
                    P.mm(pp, UW[cr, 128:256], kg[cr, :])
                    P.stt("dve", L[:, 2 + c, :], g.ident.all(), egl[:, t, c, col:col + 1], pp, ALU.mult, ALU.subtract)
                    ps_ = nextq(g, pcs[d_])
                    P.mm(ps_, kg[cr, :], UW[cr, 0:128])
                    P.copy("act", L[:, 4 + c, :], ps_)
                yield
                for c in ((0, 1) if d_ == 0 else (1, 0)):
                    cr = slice(c * 64, (c + 1) * 64)
                    po_ = nextq(g, pcs[d_])
                    P.mm(po_[cr, :], L[:, 0, cr], Sc[d_].all())
                    if (t, c) in done:
                        P.tt("dve", oacc[cr, t, :], po_[cr, :], oacc[cr, t, :], ALU.add)
                        P.tt("dve", oacc[cr, t, :], oacc[cr, t, :], L[cr, 1, :], ALU.add)
                    else:
                        P.tt("dve", oacc[cr, t, :], po_[cr, :], L[cr, 1, :], ALU.add)
                        done.add((t, c))
                    pn = nextq(g, pcs[d_])
                    P.mm(pn, L[:, 2 + c, :], Sc[d_].all())
                    Sn = Sb.get()
                    P.tt("dve", Sn.all(), pn, L[:, 4 + c, :], ALU.add)
                    Sc[d_] = Sn
            for step in range(nt):
                interleave([dn_body(0, step), dn_body(1, step)])
            if si < 2:
                for d_ in range(2):
                    P.dma("sp", g.dn_st_out[si, d_, h, :, :], Sc[d_].all(), "dn_so")
        for t4 in range(3):
            ssq = g.rstd[:, t4 * 4:(t4 + 1) * 4]
            for j in range(4):
                t = t4 * 4 + j
                junk = tp.get()
                P.act(junk.all(), oacc[:, t, :], AF.Square, accum_out=g.rstd[:, t:t + 1])
            P.act(g.rstd[:, 16 + t4 * 4:16 + (t4 + 1) * 4], ssq, AF.Ln, bias=g.eps6[:, 0:1], scale=1.0 / 128)
            P.act(g.rstd[:, 16 + t4 * 4:16 + (t4 + 1) * 4], g.rstd[:, 16 + t4 * 4:16 + (t4 + 1) * 4], AF.Exp, scale=-0.5)
            ps = nextps(g)
            for j in range(4):
                t = t4 * 4 + j
                on = tp.get()
                P.ts("dve", on.all(), oacc[:, t, :], g.rstd[:, 16 + t:17 + t], ALU.mult)
                P.tr(ps[:, j * 128:(j + 1) * 128], on.all(), g.ident.all())
            P.stt("dve", oTh[:, t4 * 512:(t4 + 1) * 512], ps.all(), gout, zs[:, t4 * 512:(t4 + 1) * 512], ALU.mult, ALU.mult)
        wo = g.wo[0]
        wov = wo.all().re("p a b -> p (a b)")[:, 0:1024]
        P.dma("pool", wov, V(wout[h * 128:(h + 1) * 128, :], g.dn_w_out.allkeys), "wo0")
        for d in range(8):
            for tg in range(NG):
                cd = 0 if tg == 0 else 1
                ts_ = slice(tg * 512, (tg + 1) * 512)
                ps = nextps(g)
                P.mm(ps.all(), wov[:, d * 128:(d + 1) * 128], oTh[:, ts_])
                P.stt("dve", g.xT[:, d, ts_], ps.all(), m[:, 2 * 8 + d, cd:cd + 1], g.xT[:, d, ts_], ALU.mult, ALU.add)
    g.psrot = list(range(8))
    P.barrier()


def mla_consts():
    t = np.arange(1024)
    cos = np.ones((96, 1024), np.float32)
    sin = np.zeros((96, 1024), np.float32)
    R = np.zeros((96, 96), np.float32)
    inv = 10000.0 ** (-np.arange(8, dtype=np.float32) / 8)
    for j in range(32):
        grp, jj = j // 16, j % 16
        pos = (t // 64) if grp == 0 else (t % 64)
        ang = pos.astype(np.float32) * inv[jj % 8]
        cos[64 + j] = np.cos(ang)
        sin[64 + j] = np.sin(ang)
        if jj < 8:
            R[64 + j + 8, 64 + j] = -1.0
        else:
            R[64 + j - 8, 64 + j] = 1.0
    return cos, sin, R


def mla_layer(g, l):
    P = g.P
    P.barrier()
    craw = carve(g, "ml_craw", 0, [128, 5, 512], F32, split=(1, 1))
    cqn = carve(g, "ml_cqn", 10240, [128, 3, NT], BF16, split=(2, 512))
    ckvn = carve(g, "ml_ckvn", 19456, [128, 2, 1792], BF16, split=(2, 256))
    kpe = carve(g, "ml_kpe", 26624, [128, 1792], BF16, split=(1, 256))
    Vp = [carve(g, "ml_Vp%d" % i, 30208 + i * 3584, [128, 14, 128], BF16, split=(1, 1)) for i in range(2)]
    qb = [carve(g, "ml_q%d" % i, 37376 + i * 6656, [128, NT], BF16, split=(1, 512)) for i in range(2)]
    kb = [carve(g, "ml_k%d" % i, 37376 + i * 6656 + 3072, [128, 1792], BF16, split=(1, 256)) for i in range(2)]
    cos = carve(g, "ml_cos", 50688, [128, 1024], F32)
    sin = carve(g, "ml_sin", 54784, [128, 1024], F32)
    Rm = carve(g, "ml_R", 58880, [128, 96], BF16)
    gn = carve(g, "ml_gn", 59136, [128, 8], F32)
    oT = g.hT
    P.dma("sp", cos[0:96, :], g.ml_cos_d.all(), "c2")
    P.dma("sp", sin[0:96, :], g.ml_sin_d.all(), "c2")
    P.dma("sp", gn.all(), g.ml_gn_d.all(), "c2")
    P.dma("pool", Rm[0:96, :], g.ml_R_d.all(), "c2")
    wa_src = g.ml_wa.h.rearrange("(k p) n -> p k n", p=128)
    P.dma("pool", g.wi[0].all(), V(wa_src[:, :, 0:512], g.ml_wa.allkeys), "wi0")
    P.dma("pool", g.wi[1][:, :, 0:224], V(wa_src[:, :, 512:736], g.ml_wa.allkeys), "wi1")
    P.ts("dve", gn[:, 5:6], gn[:, 5:6], 96 ** -0.5, ALU.mult)
    g.psrot = [0, 1, 2, 3]
    ckv_out = g.ckv_out
    kpe_out = g.kpe_out
    for tg in range(NG):
        ts_ = slice(tg * 512, (tg + 1) * 512)
        for c in range(6):
            ps = nextps(g)
            wsl = g.wi[0][:, :, c * 128:(c + 1) * 128] if c < 4 else \
                (g.wi[1][:, :, 0:128] if c == 4 else g.wi[1][:, :, 128:224])
            mrows = 128 if c < 5 else 96
            for k in range(8):
                P.mm(ps[0:mrows, :], wsl[:, k, :], g.hT[:, k, ts_], start=(k == 0), stop=(k == 7))
            if c < 5:
                P.copy("act", craw[:, c, :], ps.all())
            else:
                P.copy("act", kpe[64:96, ts_], ps[64:96, :])
                if tg == 0:
                    kf = g.ntmp[0]
                    P.copy("dve", kf[64:96, :], ps[64:96, :])
                    pt = nextps(g)
                    for t in range(4):
                        P.tr(pt[:, t * 32:(t + 1) * 32], kf[64:96, t * 128:(t + 1) * 128], g.ident[64:96, 64:96])
                    st = g.ntmp[1]
                    P.copy("act", st[:, 0:128], pt[:, 0:128])
                    for t in range(4):
                        P.dma("sp", kpe_out[t // 2, (t % 2) * 128:(t % 2 + 1) * 128, :], st[:, t * 32:(t + 1) * 32], "ntmp1")
        for (c0, nch, dst, gc0) in ((0, 3, cqn, 0), (3, 2, ckvn, 3)):
            pm = nextps(g)
            for c in range(nch):
                sq = g.sq[g.nsq % 2]
                g.nsq += 1
                P.act(sq.all(), craw[:, c0 + c, :], AF.Square)
                P.mm(pm.all(), g.onesb1.all(), sq.all(), start=(c == 0), stop=(c == nch - 1))
            P.act(g.rstd.all(), pm.all(), AF.Ln, bias=g.eps6[:, 0:1], scale=1.0 / (128 * nch))
            P.act(g.rstd.all(), g.rstd.all(), AF.Exp, scale=-0.5)
            for c in range(nch):
                if dst is ckvn and tg == 0:
                    cf = g.sg[c % 2]
                    P.stt("dve", cf.all(), craw[:, c0 + c, :], gn[:, gc0 + c:gc0 + c + 1], g.rstd.all(), ALU.mult, ALU.mult)
                    P.copy("act", dst[:, c, ts_], cf.all())
                    pt = nextps(g)
                    for t in range(4):
                        P.tr(pt[:, t * 128:(t + 1) * 128], cf[:, t * 128:(t + 1) * 128], g.ident.all())
                    st = g.ntmp[c % 2]
                    P.copy("act", st.all(), pt.all())
                    for t in range(4):
                        P.dma("sp", ckv_out[t // 2, (t % 2) * 128:(t % 2 + 1) * 128, c * 128:(c + 1) * 128],
                              st[:, t * 128:(t + 1) * 128], "ntmp%d" % (c % 2))
                else:
                    P.stt("dve", dst[:, c, ts_], craw[:, c0 + c, :], gn[:, gc0 + c:gc0 + c + 1], g.rstd.all(), ALU.mult, ALU.mult)
    for t in range(2):
        st = g.sg[t]
        P.dma("sp", st[:, 0:256], g.ml_ckvc[t * 128:(t + 1) * 128, :], "sg%d" % t)
        P.dma("sp", st[:, 256:352], g.ml_kpec[t * 128:(t + 1) * 128, :], "sg%d" % t)
        ps = nextps(g)
        for c in range(2):
            P.tr(ps[:, c * 128:(c + 1) * 128], st[:, c * 128:(c + 1) * 128], g.ident.all())
        P.tr(ps[0:96, 256:384], st[:, 256:352], g.ident.all())
        P.copy("act", ckvn[:, :, NT + t * 128:NT + (t + 1) * 128], ps[:, 0:256].re("p (c t) -> p c t", c=2))
        P.copy("act", kpe[64:96, NT + t * 128:NT + (t + 1) * 128], ps[64:96, 256:384])
    wq_src = g.ml_wqb.h.rearrange("(k p) n -> p k n", p=128)
    wq0 = g.wi[0].all().re("p a b -> p (a b)")[:, 0:2304].re("p (k n) -> p k n", k=3)
    wq1 = g.wi[1].all().re("p a b -> p (a b)")[:, 0:2304].re("p (k n) -> p k n", k=3)
    P.dma("pool", wq0, V(wq_src[:, :, 0:768], g.ml_wqb.allkeys), "wi0")
    P.dma("pool", wq1, V(wq_src[:, :, 768:1536], g.ml_wqb.allkeys), "wi1")
    wkv = g.wi[2].all().re("p a b -> p (a b)").re("p (w k n) -> p w k n", w=2, k=2)
    P.dma("pool", wkv[:, 0, :, :], V(g.ml_wk.h.rearrange("(k p) n -> p k n", p=128), g.ml_wk.allkeys), "wi2")
    P.dma("pool", wkv[:, 1, :, :], V(g.ml_wv.h.rearrange("(k p) n -> p k n", p=128), g.ml_wv.allkeys), "wi2")
    nb = 0
    cols_groups = [(0, 512, False), (512, 512, True), (1024, 512, True), (1536, 256, False)]
    for a in range(8):
        Vt = Vp[a % 2]
        for t4 in range(4):
            ps = nextps(g)
            nt_ = 4 if t4 < 3 else 2
            for tt_ in range(nt_):
                t = t4 * 4 + tt_
                for k in range(2):
                    P.mm(ps[:, tt_ * 128:(tt_ + 1) * 128], ckvn[:, k, t * 128:(t + 1) * 128], wkv[:, 1, k, a * 128:(a + 1) * 128],
                         start=(k == 0), stop=(k == 1))
            P.copy("act", Vt[:, t4 * 4:t4 * 4 + nt_, :], ps[:, 0:nt_ * 128].re("p (t c) -> p t c", t=nt_))
        for hh in range(2):
            h = 2 * a + hh
            pb = hh * 64
            qT = qb[h % 2]
            kT = kb[h % 2]
            wq = wq0 if h < 8 else wq1
            hq = h % 8
            for tg in range(NG):
                ts_ = slice(tg * 512, (tg + 1) * 512)
                ps = nextps(g)
                for k in range(3):
                    P.mm(ps[0:96, :], wq[:, k, hq * 96:(hq + 1) * 96], cqn[:, k, ts_], start=(k == 0), stop=(k == 2))
                mla_norm96(g, ps, gn[0:96, 5:6], qT[0:96, ts_])
                if tg > 0:
                    mla_rope(g, qT, ts_, cos, sin, Rm, (tg - 1) * 512)
            for gi, (c0, n, roped) in enumerate(cols_groups):
                cs = slice(c0, c0 + n)
                ps = nextps(g)
                for k in range(2):
                    P.mm(ps[0:64, 0:n], wkv[:, 0, k, h * 64:(h + 1) * 64], ckvn[:, k, cs], start=(k == 0), stop=(k == 1))
                kr = g.ntmp[gi % 2]
                P.copy("act", kr[0:64, 0:n], ps[0:64, 0:n])
                P.copy("dve", kr[64:96, 0:n], kpe[64:96, cs])
                mla_norm96(g, kr, gn[0:96, 6:7], kT[0:96, cs], n=n)
                if roped:
                    mla_rope(g, kT, cs, cos, sin, Rm, c0 - 512)
            for s_ in range(2):
                po = g.psb[4 + 2 * (nb % 2)]
                pd = g.psb[5 + 2 * (nb % 2)]
                nb += 1
                qs = slice(s_ * 256, (s_ + 1) * 256)
                for kc in range(2):
                    ps = nextps(g)
                    tile = s_ * 2 + kc
                    P.mm(ps[:, 0:256], kT[0:96, tile * 128:(tile + 1) * 128], qT[0:96, qs])
                    E = g.Ebuf[g.nE % 3]
                    g.nE += 1
                    P.act(E[:, 0:256], ps[:, 0:256], AF.Exp)
                    P.mm(po[pb:pb + 64, 0:256], Vt[:, tile, pb:pb + 64], E[:, 0:256], start=(kc == 0), stop=(kc == 1))
                    P.mm(pd[pb:pb + 64, 0:256], g.onesb1[:, 0:64], E[:, 0:256], start=(kc == 0), stop=(kc == 1))
                attn_finish(g, po, pd, pb, 256, oT[pb:pb + 64, a, qs])
            for G in range(2):
                po = g.psb[4 + 2 * (nb % 2)]
                pd = g.psb[5 + 2 * (nb % 2)]
                nb += 1
                qs = slice(512 + G * 512, 512 + (G + 1) * 512)
                for ci in range(10):
                    ps = nextps(g)
                    tile = (12 + ci) if ci < 2 else (4 + ci - 2)
                    P.mm(ps.all(), kT[0:96, tile * 128:(tile + 1) * 128], qT[0:96, qs])
                    E = g.Ebuf[g.nE % 3]
                    g.nE += 1
                    P.act(E.all(), ps.all(), AF.Exp)
                    P.mm(po[pb:pb + 64, :], Vt[:, tile, pb:pb + 64], E.all(), start=(ci == 0), stop=(ci == 9))
                    P.mm(pd[pb:pb + 64, :], g.onesb1[:, 0:64], E.all(), start=(ci == 0), stop=(ci == 9))
                attn_finish(g, po, pd, pb, 512, oT[pb:pb + 64, a, qs])
    g.psrot = list(range(8))
    out_proj(g, l, g.ml_wout.h.rearrange("(c p) d -> p c d", p=128), g.ml_wout.allkeys, oT)
    P.barrier()


def mla_norm96(g, src, gain, out_bf, n=512):
    P = g.P
    sq = g.sq[g.nsq % 2]
    g.nsq += 1
    P.act(sq[0:96, 0:n], src[0:96, 0:n], AF.Square)
    pm = nextps(g)
    P.mm(pm[0:96, 0:n], g.onesb1[0:96, 0:96], sq[0:96, 0:n])
    P.act(g.rstd[0:96, 0:n], pm[0:96, 0:n], AF.Ln, bias=g.eps6[0:96, 0:1], scale=1.0 / 96)
    P.act(g.rstd[0:96, 0:n], g.rstd[0:96, 0:n], AF.Exp, scale=-0.5)
    P.stt("dve", out_bf, src[0:96, 0:n], gain, g.rstd[0:96, 0:n], ALU.mult, ALU.mult)


def mla_rope(g, xT, cs, cos, sin, Rm, p0):
    P = g.P
    n = cs.stop - cs.start
    pr = nextps(g)
    P.mm(pr[0:96, 0:n], Rm[0:96, 0:96], xT[0:96, cs])
    t1 = g.ntmp[0]
    t2 = g.ntmp[1]
    P.tt("dve", t1[64:96, 0:n], xT[64:96, cs], cos[64:96, p0:p0 + n], ALU.mult)
    P.tt("dve", t2[64:96, 0:n], pr[64:96, 0:n], sin[64:96, p0:p0 + n], ALU.mult)
    P.tt("dve", xT[64:96, cs], t1[64:96, 0:n], t2[64:96, 0:n], ALU.add)


def rw_layer(g, l):
    P = g.P
    P.barrier()
    off = 0
    MK = carve(g, "rw_MK", off, [128, 7, 128], F32); off += 7 * 512
    rT = carve(g, "rw_r", off, [128, NT], BF16, split=(1, 128)); off += 3072
    kT = carve(g, "rw_k", off, [128, NT], BF16, split=(1, 128)); off += 3072
    kkT = carve(g, "rw_kk", off, [128, NT], BF16, split=(1, 128)); off += 3072
    gT = carve(g, "rw_g", off, [128, NT], BF16, split=(1, 128)); off += 3072
    aT = [carve(g, "rw_a%d" % i, off + i * 3072, [128, NT], BF16, split=(1, 128)) for i in range(2)]; off += 6144
    vT = carve(g, "rw_v", off, [128, NT], F32, split=(1, 128)); off += 6144
    ldT = [carve(g, "rw_ld%d" % i, off + i * 6144, [128, NT], F32, split=(1, 128)) for i in range(2)]; off += 12288
    yacc = carve(g, "rw_yacc", off, [128, 12, 128], F32, split=(1, 1)); off += 6144
    tpA, off = apool(g, "rw_tp", off, 26, [128, 128], F32)
    wp, off = apool(g, "rw_wp", off, 4, [128, 256], F32)
    lp, off = apool(g, "rw_lp", off, 2, [128, 4, 128], F32)
    Zb, off = apool(g, "rw_Z", off, 6, [128, 64], F32)
    vtk, off = apool(g, "rw_vtk", off, 2, [128, 128], F32)
    assert off <= ARENA, off

    def alias_tiles(name, h, ncol):
        t_ = T(name, h, [128, ncol], split=(1, 128))
        return [t_[:, i * 128:(i + 1) * 128] for i in range(ncol // 128)]
    A_ = list(tpA.t)
    B_ = [T("Ebuf%d" % (i // 2), g.Ebuf[i // 2].h[:, :].bitcast(F32)[:, (i % 2) * 128:(i % 2 + 1) * 128], [128, 128]) for i in range(6)]
    C_ = alias_tiles("rwx0", g.wi[0].h[:, :, :].rearrange("p a b -> p (a b)").bitcast(F32), 2048) + \
        alias_tiles("rwx1", g.wi[1].h[:, :, :].rearrange("p a b -> p (a b)").bitcast(F32), 2048)
    D_ = alias_tiles("rwx2", g.sg[0].h[:, :], 512) + alias_tiles("rwx3", g.sg[1].h[:, :], 512) + \
        alias_tiles("rwx4", g.ntmp[0].h[:, :], 512) + alias_tiles("rwx5", g.ntmp[1].h[:, :], 512)
    E_ = alias_tiles("rwx6", g.rstd.h[:, :], 512) + alias_tiles("rwx7", g.sq[0].h[:, :].bitcast(F32), 256) + \
        alias_tiles("rwx8", g.sq[1].h[:, :].bitcast(F32), 256)
    tpd = [Pool(A_[0:16]), Pool(C_[10:26])]
    thd = [[Pool(A_[16:23]), Pool(A_[23:26] + C_[0:4])], [Pool(C_[26:32] + D_[0:1]), Pool(D_[7:14])]]
    tqd = [[Pool(B_), Pool(C_[4:10])], [Pool(D_[1:7]), Pool(D_[14:16] + E_[0:4])]]
    wpd = [Pool(wp.t[0:2]), Pool(wp.t[2:4])]
    lpd = [Pool(lp.t[0:1]), Pool(lp.t[1:2])]
    pcs = [[PsCtx([0, 1]), PsCtx([2, 3])], [PsCtx([4, 5]), PsCtx([6, 7])]]
    pv = g.wi[2].all().re("p a b -> p (a b)").bitcast(F32)
    muT = pv[:, 0:48].re("p (s k) -> p s k", s=6)
    w0v = pv[:, 48:64].re("p (d c) -> p d c", d=2)
    a0v = pv[:, 64:80].re("p (d c) -> p d c", d=2)
    kkv = pv[:, 80:88]
    kav = pv[:, 88:96]
    rkv = pv[:, 96:104]
    lng = pv[:, 104:112]
    lnb = pv[:, 112:120]
    omk = pv[:, 120:128]
    nw0 = pv[:, 128:144].re("p (d c) -> p d c", d=2)
    mh = pv[:, 144:192].re("p (s k) -> p s k", s=6)
    mm1 = pv[:, 192:240].re("p (s k) -> p s k", s=6)
    nhalf = pv[:, 240:241]
    mids = [g.wo[0].all().re("p a b -> p (a b)")[:, 0:NT], g.wo[1].all().re("p a b -> p (a b)")[:, 0:NT],
            g.wi[2].all().re("p a b -> p (a b)")[:, 1024:1024 + NT]]
    wsl = g.wi[2].all().re("p a b -> p (a b)")[:, 2560:3584]
    P.dma("sp", MK.all(), g.dn_mk_d[:, 0:7, :], "c4")
    P.dma("sp", pv[:, 0:120], g.rw_pv_d.all(), "c4")
    P.ts("dve", omk, kav, -1.0, ALU.mult, 1.0, ALU.add)
    P.ts("dve", nw0.re("p d c -> p (d c)"), w0v.re("p d c -> p (d c)"), -1.0, ALU.mult)
    P.ts("dve", mh.re("p s k -> p (s k)"), muT.re("p s k -> p (s k)"), 0.5, ALU.mult)
    P.ts("dve", mm1.re("p s k -> p (s k)"), muT.re("p s k -> p (s k)"), -1.0, ALU.mult, 1.0, ALU.add)
    P.memset("dve", nhalf, -0.5)
    McumD = [MK[:, 0, :], MK[:, 1, :]]
    SMD = [MK[:, 4, :], MK[:, 5, :]]
    SMT = [MK[:, 5, :], MK[:, 4, :]]
    INCT = [MK[:, 0, :], MK[:, 1, :]]
    Mblk = MK[:, 6, :]
    g.psrot = [0, 1, 2, 3, 4, 5, 6, 7]
    BOUNDS = (0, 256, 512, NT)

    def shifted_proj(ps, wd, wh, tg, mrows=128):
        s0 = tg * 512
        for k in range(8):
            P.mm(ps[0:mrows, :], wd[:, k, :], g.hT[:, k, s0:s0 + 512], start=(k == 0), stop=False)
        segs = [(0, 256), (256, 512)] if tg == 0 else [(0, 512)]
        n = 0
        tot = 16 * len(segs)
        for (a_, b_) in segs:
            lo = a_ + (1 if (s0 + a_) in BOUNDS else 0)
            hi = b_ - (1 if (s0 + b_) in BOUNDS else 0)
            for k in range(8):
                n += 1
                P.mm(ps[0:mrows, lo:b_], wh[:, k, :], g.hT[:, k, s0 + lo - 1:s0 + b_ - 1], start=False, stop=False)
            for k in range(8):
                n += 1
                P.mm(ps[0:mrows, a_:hi], wh[:, k, :], g.hT[:, k, s0 + a_ + 1:s0 + hi + 1], start=False, stop=(n == tot))

    def scaled_w(dst_d, dst_h, src, si):
        P.tt("dve", dst_d, src, mm1[:, si, :].re("p (k o) -> p k o", o=1).bc([128, 8, 128]), ALU.mult)
        P.tt("dve", dst_h, src, mh[:, si, :].re("p (k o) -> p k o", o=1).bc([128, 8, 128]), ALU.mult)

    wl = g.wi[0]
    for i_, (src, si, fn) in enumerate(((g.rw_w1_d, 3, AF.Tanh), (g.rw_a1_d, 4, AF.Identity), (g.rw_g1_d, 5, AF.Sigmoid))):
        P.dma("pool", wl[:, :, 0:128], V(src.h.rearrange("(k p) n -> p k n", p=128), src.allkeys), "wi0")
        scaled_w(wl[:, :, 128:256], wl[:, :, 256:384], wl[:, :, 0:128], si)
        for tg in range(NG):
            ps = nextps(g)
            shifted_proj(ps, wl[:, :, 128:256], wl[:, :, 256:384], tg)
            P.act(mids[i_][:, tg * 512:(tg + 1) * 512], ps.all(), fn)
    wr_src = g.rw_wrkv_d.h
    for c in range(8):
        wb = g.wi[0]
        wsc = g.wi[1]
        for i_ in range(3):
            P.dma("pool", wb[:, :, i_ * 128:(i_ + 1) * 128],
                  V(wr_src[i_].rearrange("(k p) n -> p k n", p=128)[:, :, c * 128:(c + 1) * 128], g.rw_wrkv_d.allkeys), "wi0")
        P.dma("pool", wsl[:, 0:384].re("p (i n) -> p i n", i=3), V(g.rw_l2_d.h[:, :, c * 128:(c + 1) * 128], g.rw_l2_d.allkeys), "rw_l2")
        whs = [wsc[:, :, 384:512], wb[:, :, 384:512], g.rwh.all()]
        for i_ in range(3):
            scaled_w(wsc[:, :, i_ * 128:(i_ + 1) * 128], whs[i_], wb[:, :, i_ * 128:(i_ + 1) * 128], i_)
        for tg in range(NG):
            ts_ = slice(tg * 512, (tg + 1) * 512)
            for i_, dst in enumerate((rT, kT, vT)):
                ps = nextps(g)
                shifted_proj(ps, wsc[:, :, i_ * 128:(i_ + 1) * 128], whs[i_], tg)
                P.copy("act", dst[:, ts_], ps.all())
            for d_ in range(2):
                dr = slice(d_ * 64, (d_ + 1) * 64)
                ps = nextps(g)
                P.mm(ps.all(), wsl[dr, 0:128], mids[0][dr, ts_])
                t1 = g.ntmp[0]
                P.act(t1.all(), ps.all(), AF.Exp, bias=nw0[:, d_, c:c + 1], scale=-1.0)
                P.act(t1.all(), t1.all(), AF.Ln, bias=g.one1[:, 0:1], scale=1.0)
                P.act(ldT[d_][:, ts_], t1.all(), AF.Exp, bias=nhalf, scale=-1.0)
                ps = nextps(g)
                P.mm(ps.all(), wsl[dr, 128:256], mids[1][dr, ts_])
                P.act(aT[d_][:, ts_], ps.all(), AF.Sigmoid, bias=a0v[:, d_, c:c + 1], scale=1.0)
            ps = nextps(g)
            P.mm(ps.all(), wsl[:, 256:384], mids[2][:, ts_])
            P.copy("act", gT[:, ts_], ps.all())
            t2 = g.ntmp[1]
            P.ts("dve", t2.all(), kT[:, ts_], kkv[:, c:c + 1], ALU.mult)
            sq = g.sq[g.nsq % 2]
            g.nsq += 1
            P.act(sq.all(), t2.all(), AF.Square)
            pm = nextps(g)
            P.mm(pm.all(), g.ones2.all(), sq.all())
            P.act(g.rstd.all(), pm.all(), AF.Ln, bias=g.eps6[:, 0:1], scale=64.0)
            P.act(g.rstd.all(), g.rstd.all(), AF.Exp, scale=-0.5)
            P.tt("dve", kkT[:, ts_], t2.all(), g.rstd.all(), ALU.mult)
        P.barrier()
        done = set()
        for si, (t0, nt) in enumerate(SEQS):
            Zc = [None, None]
            for d_ in range(2):
                Zc[d_] = Zb.get()
                if si < 2:
                    P.memset("dve", Zc[d_].all(), 0.0)
                else:
                    P.dma("sp", Zc[d_].all(), g.rw_z0_d[d_, c, :, :], "rw_z0")
            def hh_body(d_, t, hh, AR, KB, Vt, eC, atok, khtok, bhtok, L):
                pc = pcs[d_][hh]
                th = thd[d_][hh]
                hr = slice(hh * 64, (hh + 1) * 64)
                pN = nextq(g, pc)
                P.mm(pN, AR[hr, 0:128], KB[hr, 128:256])
                pB = nexth(g, pc)
                P.mm(pB[:, 0:256], KB[hr, 128:256], AR[hr, 0:256])
                yield
                N = th.get()
                P.tt("dve", N.all(), pN, SMD[d_], ALU.mult)
                Nt = th.get()
                P.tt("dve", Nt.all(), pB[:, 0:128], SMT[d_], ALU.mult)
                MrbT = th.get()
                P.tt("dve", MrbT.all(), pB[:, 128:256], INCT[d_], ALU.mult)
                pK = nexth(g, pc)
                P.mm(pK[:, 0:256], KB[hr, 0:128], AR[hr, 0:256])
                yield
                LakT = th.get()
                P.tt("dve", LakT.all(), pK[:, 0:128], SMT[d_], ALU.mult)
                MrkT = th.get()
                P.tt("dve", MrkT.all(), pK[:, 128:256], INCT[d_], ALU.mult)
                Tt = yield from neumann_solve_gen(g, N, Nt, tqd[d_][hh], pc=pc)
                X = th.get()
                pLV = nextq(g, pc)
                P.mm(pLV[:, 0:64], LakT.all(), Vt[:, hr])
                yield
                P.copy("act", X[:, 0:64], pLV[:, 0:64])
                P.copy("dve", X[:, 64:128], atok[:, hr])
                yield
                pUA = nextq(g, pc)
                P.mm(pUA, Tt.all(), X.all())
                yield
                UA = th.get()
                P.copy("act", UA.all(), pUA)
                yield
                pR = nextq(g, pc)
                P.mm(pR[hr, :], UA[:, 64:128], MrbT.all())
                pY = nextq(g, pc)
                P.mm(pY[:, 0:64], MrbT.all(), UA[:, 0:64], start=True, stop=False)
                P.mm(pY[:, 0:64], MrkT.all(), Vt[:, hr], start=False, stop=True)
                yield
                P.tt("dve", L[hr, 0, :], pR[hr, :], AR[hr, 128:256], ALU.add)
                P.copy("act", L[:, 1, hr], pY[:, 0:64])
                for c2 in range(2):
                    cr = slice(c2 * 64, (c2 + 1) * 64)
                    pP = nextq(g, pc)
                    P.mm(pP[hr, 0:64], UA[cr, 64:128], bhtok[cr, hr])
                    pZ = nextq(g, pc)
                    P.mm(pZ[hr, 0:64], bhtok[cr, hr], UA[cr, 0:64], start=True, stop=False)
                    P.mm(pZ[hr, 0:64], khtok[cr, hr], Vt[cr, hr], start=False, stop=True)
                    yield
                    P.stt("dve", L[hr, 2, cr], g.ident[hr, hr], eC[hr, c2 * 64:c2 * 64 + 1], pP[hr, 0:64], ALU.mult, ALU.add)
                    P.copy("act", L[hr, 3, cr], pZ[hr, 0:64])

            def rw_body(d_, step):
                tp = tpd[d_]
                pc = pcs[d_][0]
                t = t0 + step if d_ == 0 else t0 + nt - 1 - step
                tk = slice(t * 128, (t + 1) * 128)
                Vt = vtk.get()
                pv_ = nextq(g, pc)
                P.tr(pv_, vT[:, tk], g.ident.all())
                pl = nextq(g, pc)
                P.tr(pl, ldT[d_][:, tk], g.ident.all())
                yield
                P.copy("act", Vt.all(), pv_)
                ldtok = tp.get()
                P.copy("act", ldtok.all(), pl)
                yield
                pnl = nextq(g, pc)
                P.mm(pnl, ldtok.all(), McumD[d_])
                pnc = nextq(g, pc)
                P.mm(pnc, ldtok.all(), Mblk)
                yield
                nlc = tp.get()
                P.copy("act", nlc.all(), pnc)
                eC = tp.get()
                P.act(eC.all(), pnc, AF.Exp, scale=-1.0)
                epos = tp.get()
                P.act(epos.all(), pnl, AF.Exp, scale=-1.0)
                eneg = tp.get()
                P.act(eneg.all(), pnl, AF.Exp)
                tA = tp.get()
                P.tt("dve", tA.all(), pnl, ldT[d_][:, tk], ALU.subtract)
                yield
                eprev = tp.get()
                P.act(eprev.all(), tA.all(), AF.Exp, scale=-1.0)
                tB = tp.get()
                P.tt("dve", tB.all(), pnl, nlc.all(), ALU.subtract)
                kd = tp.get()
                P.ts("dve", kd.all(), aT[d_][:, tk], kav[:, c:c + 1], ALU.mult, omk[:, c:c + 1], ALU.add)
                P.tt("dve", kd.all(), kd.all(), kT[:, tk], ALU.mult)
                bv = tp.get()
                P.tt("dve", bv.all(), kkT[:, tk], aT[d_][:, tk], ALU.mult)
                yield
                ehat = tp.get()
                P.act(ehat.all(), tB.all(), AF.Exp)
                AR = wpd[d_].get()
                P.stt("dve", AR[:, 0:128], kkT[:, tk], -1.0, eprev.all(), ALU.mult, ALU.mult)
                P.tt("dve", AR[:, 128:256], rT[:, tk], epos.all(), ALU.mult)
                KB = wpd[d_].get()
                P.tt("dve", KB[:, 0:128], kd.all(), eneg.all(), ALU.mult)
                P.tt("dve", KB[:, 128:256], bv.all(), eneg.all(), ALU.mult)
                yield
                khat = tp.get()
                P.tt("dve", khat.all(), kd.all(), ehat.all(), ALU.mult)
                bhat = tp.get()
                P.tt("dve", bhat.all(), bv.all(), ehat.all(), ALU.mult)
                pts = []
                for src in (AR[:, 0:128], khat.all(), bhat.all()):
                    pt_ = nextq(g, pc)
                    P.tr(pt_, src, g.ident.all())
                    pts.append(pt_)
                yield
                toks = []
                for pt_ in pts:
                    tk_ = tp.get()
                    P.copy("act", tk_.all(), pt_)
                    toks.append(tk_)
                atok, khtok, bhtok = toks
                L = lpd[d_].get()
                yield from interleave_gen([hh_body(d_, t, hh, AR, KB, Vt, eC, atok, khtok, bhtok, L) for hh in range(2)])
                yield
                for c2 in ((0, 1) if d_ == 0 else (1, 0)):
                    cr = slice(c2 * 64, (c2 + 1) * 64)
                    Zn = Zb.get()
                    for hh in range(2):
                        hr = slice(hh * 64, (hh + 1) * 64)
                        py = nextq(g, pc)
                        P.mm(py[cr, 0:64], L[hr, 0, cr], Zc[d_][hr, :])
                        if (t, c2, hh) in done:
                            P.tt("dve", yacc[cr, t, hr], py[cr, 0:64], yacc[cr, t, hr], ALU.add)
                            P.tt("dve", yacc[cr, t, hr], yacc[cr, t, hr], L[cr, 1, hr], ALU.add)
                        else:
                            P.tt("dve", yacc[cr, t, hr], py[cr, 0:64], L[cr, 1, hr], ALU.add)
                            done.add((t, c2, hh))
                        pz = nextq(g, pc)
                        P.mm(pz[hr, 0:64], L[hr, 2, cr], Zc[d_][hr, :])
                        P.tt("dve", Zn[hr, :], pz[hr, 0:64], L[hr, 3, cr], ALU.add)
                    Zc[d_] = Zn
            for step in range(nt):
                interleave([rw_body(0, step), rw_body(1, step)])
            if si < 2:
                for d_ in range(2):
                    pzt = nextq(g)
                    P.tr(pzt[0:64, :], Zc[d_].all(), g.ident.all())
                    zo = tpd[0].get()
                    P.copy("act", zo[0:64, :], pzt[0:64, :])
                    P.dma("sp", V(g.rw_st_out.h[si, d_, 2 * c:2 * c + 2].rearrange("h v k -> v h k"), g.rw_st_out.allkeys),
                          zo[0:64, :].re("p (h k) -> p h k", h=2), "rw_so")
        P.barrier()
        P.dma("pool", wsl, V(g.rw_wout_d.h[c * 128:(c + 1) * 128, :], g.rw_wout_d.allkeys), "rw_wo")
        oTc = g.rw_oT
        for tg in range(NG):
            ts_ = slice(tg * 512, (tg + 1) * 512)
            ps = nextps(g)
            for j in range(4):
                t = tg * 4 + j
                P.tr(ps[:, j * 128:(j + 1) * 128], yacc[:, t, :], g.ident.all())
            yT_ = g.ntmp[0]
            P.copy("act", yT_.all(), ps.all())
            sqb = g.sq[g.nsq % 2]
            g.nsq += 1
            P.copy("dve", sqb.all(), yT_.all())
            pm = nextps(g)
            P.mm(pm.all(), g.ones2.all(), sqb.all())
            cen = g.ntmp[1]
            P.tt("dve", cen.all(), yT_.all(), pm.all(), ALU.subtract)
            sq2 = g.sq[g.nsq % 2]
            g.nsq += 1
            P.act(sq2.all(), cen.all(), AF.Square)
            pv2 = nextps(g)
            P.mm(pv2.all(), g.ones2.all(), sq2.all())
            P.act(g.rstd.all(), pv2.all(), AF.Ln, bias=g.eps_ln[:, 0:1], scale=1.0)
            P.act(g.rstd.all(), g.rstd.all(), AF.Exp, scale=-0.5)
            P.tt("dve", cen.all(), cen.all(), g.rstd.all(), ALU.mult)
            P.ts("dve", cen.all(), cen.all(), lng[:, c:c + 1], ALU.mult, lnb[:, c:c + 1], ALU.add)
            rk2 = g.sg[0]
            P.ts("dve", rk2.all(), rT[:, ts_], rkv[:, c:c + 1], ALU.mult)
            pb_ = nextps(g)
            for d_ in range(2):
                kd2 = g.sg[1]
                P.ts("dve", kd2.all(), aT[d_][:, ts_], kav[:, c:c + 1], ALU.mult, omk[:, c:c + 1], ALU.add)
                P.tt("dve", kd2.all(), kd2.all(), kT[:, ts_], ALU.mult)
                pr_ = g.Ebuf[d_]
                P.tt("dve", pr_.all(), kd2.all(), rk2.all(), ALU.mult)
                P.mm(pb_.all(), g.onesblk.all(), pr_.all(), start=(d_ == 0), stop=(d_ == 1))
            bon = g.sg[0]
            P.tt("dve", bon.all(), pb_.all(), vT[:, ts_], ALU.mult)
            P.tt("dve", cen.all(), cen.all(), bon.all(), ALU.add)
            P.tt("dve", oTc[:, ts_], cen.all(), gT[:, ts_], ALU.mult)
        m = g.mod[l]
        for d in range(8):
            for tg in range(NG):
                cd = 0 if tg == 0 else 1
                ts_ = slice(tg * 512, (tg + 1) * 512)
                ps = nextps(g)
                P.mm(ps.all(), wsl[:, d * 128:(d + 1) * 128], oTc[:, ts_])
                P.stt("dve", g.xT[:, d, ts_], ps.all(), m[:, 2 * 8 + d, cd:cd + 1], g.xT[:, d, ts_], ALU.mult, ALU.add)
    g.psrot = list(range(8))
    P.barrier()


def ffn(g, l):
    P = g.P
    wbufs(g)
    P.barrier()
    actT = carve(g, "actT", 0, [128, NF, NT], BF16, split=(1, 1))
    win = g.ffn_w_in.h[l].rearrange("(k p) n -> p k n", p=128)
    m = g.mod[l]
    n_sg = 0
    for j in range(NF // 2):
        wb = g.wi[j % 3]
        P.dma("pool", wb[:, :, 0:256], V(win[:, :, j * 256:(j + 1) * 256], g.ffn_w_in.allkeys), "wi%d" % (j % 3))
        P.dma("pool", wb[:, :, 256:512], V(win[:, :, DFF + j * 256:DFF + (j + 1) * 256], g.ffn_w_in.allkeys), "wi%d" % (j % 3))
        for tg in range(NG):
            ts_ = slice(tg * 512, (tg + 1) * 512)
            for hf in range(2):
                f = j * 2 + hf
                pg = nextps(g)
                pu = nextps(g)
                for k in range(8):
                    P.mm(pg.all(), wb[:, k, hf * 128:(hf + 1) * 128], g.hT[:, k, ts_], start=(k == 0), stop=(k == 7))
                for k in range(8):
                    P.mm(pu.all(), wb[:, k, 256 + hf * 128:256 + (hf + 1) * 128], g.hT[:, k, ts_], start=(k == 0), stop=(k == 7))
                sg = g.sg[n_sg % 2]
                n_sg += 1
                P.act(sg.all(), pg.all(), AF.Silu)
                P.tt("dve", actT[:, f, ts_], sg.all(), pu.all(), ALU.mult)
    wout = g.ffn_w_out.h[l].rearrange("(f p) d -> p f d", p=128)
    for d in range(8):
        wo = g.wo[d % 2]
        P.dma("pool", wo.all(), V(wout[:, :, d * 128:(d + 1) * 128], g.ffn_w_out.allkeys), "wo%d" % (d % 2))
        for tg in range(NG):
            cd = 0 if tg == 0 else 1
            ts_ = slice(tg * 512, (tg + 1) * 512)
            ps = nextps(g)
            for f in range(NF):
                P.mm(ps.all(), wo[:, f, :], actT[:, f, ts_], start=(f == 0), stop=(f == NF - 1))
            P.stt("dve", g.xT[:, d, ts_], ps.all(), m[:, 5 * 8 + d, cd:cd + 1], g.xT[:, d, ts_], ALU.mult, ALU.add)
    P.barrier()


def na_tables(rel_bias):
    H = rel_bias.shape[0]
    kcol = np.arange(64)[:, None]
    qcol = np.arange(64)[None, :]
    cidx = np.clip(kcol - qcol + 15, 0, 30)
    tz = np.zeros((H, 2, 64, 31, 64), np.float32)
    for krl in range(2):
        for m in range(31):
            ridx = 22 - m + krl
            if 0 <= ridx <= 14:
                tz[:, krl, :, m, :] = rel_bias[:, ridx][:, cidx]
    return tz.reshape(H, 128, 31 * 64)


def na_cmask():
    qc = np.arange(64)
    ws = np.clip(qc - 8, 0, 48)
    kc = np.arange(64)
    colok = (kc[:, None] >= ws[None, :]) & (kc[:, None] < ws[None, :] + 16)
    cm = np.full((2, 6, 2, 64, 8, 64), -1e30, np.float32)
    for G in range(2):
        for ci in range(6):
            for krl in range(2):
                kr = 2 * (2 * G + ci) + krl
                for rq in range(8):
                    r = 8 * G + rq
                    rs = min(max(r - 4, 0), 8)
                    if rs <= kr < rs + 8:
                        cm[G, ci, krl, :, rq, :] = np.where(colok, 0.0, -1e30)
    return cm.reshape(12, 128, 512).transpose(1, 0, 2).copy()


def attn_norm_pair(g, ps, gain, out_bf, out_f32=None):
    P = g.P
    sq = g.sq[g.nsq % 2]
    g.nsq += 1
    P.act(sq.all(), ps.all(), AF.Square)
    pm = nextps(g)
    P.mm(pm.all(), g.ones2.all(), sq.all())
    P.act(g.rstd.all(), pm.all(), AF.Ln, bias=g.eps6[:, 0:1], scale=1.0)
    P.act(g.rstd.all(), g.rstd.all(), AF.Exp, scale=-0.5)
    if out_f32 is not None:
        P.stt("dve", out_f32, ps.all(), gain, g.rstd.all(), ALU.mult, ALU.mult)
        P.copy("act", out_bf, out_f32)
    else:
        P.stt("dve", out_bf, ps.all(), gain, g.rstd.all(), ALU.mult, ALU.mult)


def attn_finish(g, po, pd, pb, n, out):
    P = g.P
    rc = g.ntmp[g.nrc % 2]
    g.nrc += 1
    P.op("dve", "reciprocal", out=rc[pb:pb + 64, 0:n], in_=pd[pb:pb + 64, 0:n])
    P.tt("dve", out, po[pb:pb + 64, 0:n], rc[pb:pb + 64, 0:n], ALU.mult)


def na_layer(g, l):
    P = g.P
    P.barrier()
    Vc = carve(g, "na_Vc", 0, [128, 2, 1024], BF16)
    oT = carve(g, "na_oT", 4096, [128, 8, NT], BF16, split=(1, 1))
    kcT = carve(g, "na_kcT", 28672, [128, 8, 256], BF16, split=(1, 1))
    CM = carve(g, "na_CM", 32768, [128, 12, 512], BF16)
    TZ = carve(g, "na_TZ", 45056, [128, 31, 64], BF16)
    qk = [[carve(g, "na_q%d" % i, 49152 + i * 6144, [128, NT], BF16, split=(1, 512)),
           carve(g, "na_k%d" % i, 49152 + i * 6144 + 3072, [128, NT], BF16, split=(1, 512))] for i in range(2)]
    Vp = [carve(g, "na_Vp%d" % i, 61440 + i * 3072, [128, 12, 128], BF16, split=(1, 1)) for i in range(2)]
    gn = carve(g, "na_gn", 67584, [128, 2], F32)
    P.dma("sp", gn.all(), g.na_gn_d.all(), "c1")
    P.ts("dve", gn[:, 0:1], gn[:, 0:1], 0.125, ALU.mult)
    P.dma("pool", CM.all(), g.na_cm_d.all(), "c1")
    g.psrot = [0, 1, 2, 3]
    kc_src = g.na_kc_d.h.rearrange("h t d -> t h d")
    vc_src = g.na_vc_d.h.rearrange("h t d -> t h d")
    for t in range(2):
        P.dma("pool", Vc[:, t, :].re("p (h d) -> p h d", h=16),
              V(vc_src[t * 128:(t + 1) * 128], g.na_vc_d.allkeys), "na_vc")
        for hf in range(2):
            st = g.ntmp[hf]
            P.dma("sp", st.all().re("p (h d) -> p h d", h=8),
                  V(kc_src[t * 128:(t + 1) * 128, hf * 8:(hf + 1) * 8, :], g.na_kc_d.allkeys), "na_kc%d" % hf)
            ps = nextps(g)
            for a in range(4):
                P.tr(ps[:, a * 128:(a + 1) * 128], st[:, a * 128:(a + 1) * 128], g.ident.all())
            P.copy("act", kcT[:, hf * 4:(hf + 1) * 4, t * 128:(t + 1) * 128], ps.all().re("p (a t) -> p a t", a=4))
    STOP = 99.0
    if STOP <= 1:
        return
    wsrc = g.na_w_qkv.h.rearrange("(k p) n -> p k n", p=128)
    wk_ = g.na_w_qkv.allkeys
    vout = g.na_vo.h.rearrange("s h t d -> s t h d")
    kout = g.na_ko.h.rearrange("s h t d -> s t h d")
    nb = 0
    for a in range(8):
        wb = g.wi[a % 3]
        for i_ in range(3):
            P.dma("pool", wb[:, :, i_ * 128:(i_ + 1) * 128], V(wsrc[:, :, i_ * 1024 + a * 128:i_ * 1024 + (a + 1) * 128], wk_), "wi%d" % (a % 3))
        qT, kT = qk[a % 2]
        Vt = Vp[a % 2]
        for t4 in range(3):
            ps = nextps(g)
            for tt_ in range(4):
                t = t4 * 4 + tt_
                for k in range(8):
                    P.mm(ps[:, tt_ * 128:(tt_ + 1) * 128], g.hT[:, k, t * 128:(t + 1) * 128], wb[:, k, 256:384], start=(k == 0), stop=(k == 7))
            P.copy("act", Vt[:, t4 * 4:(t4 + 1) * 4, :], ps.all().re("p (t c) -> p t c", t=4))
            if t4 == 0 and STOP > 1.2:
                st = g.sg[0]
                P.copy("dve", st.all(), ps.all())
                VAR = "0"
                for t in range(4):
                    if VAR == "0":
                        P.dma("sp", V(vout[t // 2, (t % 2) * 128:(t % 2 + 1) * 128, 2 * a:2 * a + 2, :], g.na_vo.allkeys),
                              st[:, t * 128:(t + 1) * 128].re("p (h d) -> p h d", h=2), "sg0")
                    elif VAR == "1":
                        pass
                    elif VAR == "2":
                        for hh_ in range(2):
                            P.dma("sp", V(g.na_vo.h[t // 2, 2 * a + hh_, (t % 2) * 128:(t % 2 + 1) * 128, :], g.na_vo.allkeys),
                                  st[:, t * 128 + hh_ * 64:t * 128 + (hh_ + 1) * 64], "sg0")
                    elif VAR == "3":
                        P.dma("pool", V(vout[t // 2, (t % 2) * 128:(t % 2 + 1) * 128, 2 * a:2 * a + 2, :], g.na_vo.allkeys),
                              st[:, t * 128:(t + 1) * 128].re("p (h d) -> p h d", h=2), "sg0")
        if STOP <= 1.4:
            return
        for tg in range(NG):
            ts_ = slice(tg * 512, (tg + 1) * 512)
            ps = nextps(g)
            for k in range(8):
                P.mm(ps.all(), wb[:, k, 0:128], g.hT[:, k, ts_], start=(k == 0), stop=(k == 7))
            attn_norm_pair(g, ps, gn[:, 0:1], qT[:, ts_])
            ps = nextps(g)
            for k in range(8):
                P.mm(ps.all(), wb[:, k, 128:256], g.hT[:, k, ts_], start=(k == 0), stop=(k == 7))
            if tg == 0 and STOP > 1.6:
                kf = g.sg[1]
                attn_norm_pair(g, ps, gn[:, 1:2], kT[:, ts_], out_f32=kf.all())
                pt = nextps(g)
                for t in range(4):
                    P.tr(pt[:, t * 128:(t + 1) * 128], kf[:, t * 128:(t + 1) * 128], g.ident.all())
                st = g.ntmp[0]
                P.copy("act", st.all(), pt.all())
                for t in range(4):
                    P.dma("sp", V(kout[t // 2, (t % 2) * 128:(t % 2 + 1) * 128, 2 * a:2 * a + 2, :], g.na_ko.allkeys),
                          st[:, t * 128:(t + 1) * 128].re("p (h d) -> p h d", h=2), "ntmp0")
            else:
                attn_norm_pair(g, ps, gn[:, 1:2], kT[:, ts_])
        if STOP <= 2:
            return
        for hh in range(2):
            h = 2 * a + hh
            pb = hh * 64
            P.dma("pool", TZ.all().re("p m q -> p (m q)"), g.na_tz_d[h, :, :], "na_tz")
            for s_ in range(2):
                po = g.psb[4 + 2 * (nb % 2)]
                pd = g.psb[5 + 2 * (nb % 2)]
                nb += 1
                qs = slice(s_ * 256, (s_ + 1) * 256)
                for kc in range(2):
                    ps = nextps(g)
                    tile = s_ * 2 + kc
                    P.mm(ps[:, 0:256], kT[pb:pb + 64, tile * 128:(tile + 1) * 128], qT[pb:pb + 64, qs])
                    E = g.Ebuf[g.nE % 3]
                    g.nE += 1
                    P.act(E[:, 0:256], ps[:, 0:256], AF.Exp)
                    P.mm(po[pb:pb + 64, 0:256], Vt[:, tile, pb:pb + 64], E[:, 0:256], start=(kc == 0), stop=(kc == 1))
                    P.mm(pd[pb:pb + 64, 0:256], g.onesb1[:, 0:64], E[:, 0:256], start=(kc == 0), stop=(kc == 1))
                attn_finish(g, po, pd, pb, 256, oT[pb:pb + 64, a, qs])
            if STOP <= 3:
                return
            for G in range(2):
                po = g.psb[4 + 2 * (nb % 2)]
                pd = g.psb[5 + 2 * (nb % 2)]
                nb += 1
                qs = slice(512 + G * 512, 512 + (G + 1) * 512)
                for ci in range(8):
                    ps = nextps(g)
                    if ci < 6:
                        tile = 4 + 2 * G + ci
                        kr0 = 2 * (2 * G + ci)
                        m0 = 15 - kr0 + 8 * G
                        P.mm(ps.all(), kT[pb:pb + 64, tile * 128:(tile + 1) * 128], qT[pb:pb + 64, qs], start=True, stop=False)
                        P.mm(ps.all(), g.identb.all(), CM[:, G * 6 + ci, :], start=False, stop=False)
                        P.mm(ps.all(), g.identb.all(), TZ[:, m0:m0 + 8, :].re("p m q -> p (m q)"), start=False, stop=True)
                        vv = Vt[:, tile, pb:pb + 64]
                    else:
                        P.mm(ps.all(), kcT[pb:pb + 64, a, (ci - 6) * 128:(ci - 5) * 128], qT[pb:pb + 64, qs])
                        vv = Vc[:, ci - 6, h * 64:(h + 1) * 64]
                    E = g.Ebuf[g.nE % 3]
                    g.nE += 1
                    P.act(E.all(), ps.all(), AF.Exp)
                    P.mm(po[pb:pb + 64, :], vv, E.all(), start=(ci == 0), stop=(ci == 7))
                    P.mm(pd[pb:pb + 64, :], g.onesb1[:, 0:64], E.all(), start=(ci == 0), stop=(ci == 7))
                attn_finish(g, po, pd, pb, 512, oT[pb:pb + 64, a, qs])
            if STOP <= 4:
                return
    g.psrot = list(range(8))
    out_proj(g, l, g.na_w_out.h.rearrange("(c p) d -> p c d", p=128), g.na_w_out.allkeys, oT)
    P.barrier()


def out_proj(g, l, wsrc, wkeys, oT):
    P = g.P
    m = g.mod[l]
    for d in range(8):
        wo = g.wo[d % 2]
        P.dma("pool", wo[:, 0:8, :], V(wsrc[:, :, d * 128:(d + 1) * 128], wkeys), "wo%d" % (d % 2))
        for tg in range(NG):
            cd = 0 if tg == 0 else 1
            ts_ = slice(tg * 512, (tg + 1) * 512)
            ps = nextps(g)
            for c in range(8):
                P.mm(ps.all(), wo[:, c, :], oT[:, c, ts_], start=(c == 0), stop=(c == 7))
            P.stt("dve", g.xT[:, d, ts_], ps.all(), m[:, 2 * 8 + d, cd:cd + 1], g.xT[:, d, ts_], ALU.mult, ALU.add)


def _lay_vec(v):
    return np.ascontiguousarray(v.reshape(8, 128).T)


def make_inputs(core, inp):
    f = np.float32
    x = np.concatenate([inp["x_prompt"][2 * core], inp["x_prompt"][2 * core + 1], inp["x_sample"][core]], 0)
    cv = np.stack([_lay_vec(inp["c_ctx"]), _lay_vec(inp["c"][core])], -1)
    adabT = np.ascontiguousarray(inp["ada_b"].reshape(DEPTH, 48, 128).transpose(2, 0, 1))
    nrmT = np.stack([inp["norm_mix"].reshape(DEPTH, 8, 128).transpose(2, 0, 1),
                     inp["norm_ffn"].reshape(DEPTH, 8, 128).transpose(2, 0, 1)], 1)
    return {
        "x": np.ascontiguousarray(x, f), "cv": np.ascontiguousarray(cv, f),
        "ada_w": inp["ada_w"], "adabT": adabT.astype(f), "nrmT": np.ascontiguousarray(nrmT, f),
        "ffn_w_in": inp["ffn_w_in"], "ffn_w_out": inp["ffn_w_out"],
        "ident": np.eye(128, dtype=f),
        "na_w_qkv": inp["na_w_qkv"][0], "na_w_out": inp["na_w_out"][0],
        "na_gn": np.ascontiguousarray(np.stack([np.tile(inp["na_q_norm"][0], 2), np.tile(inp["na_k_norm"][0], 2)], -1), f),
        "na_cm": na_cmask(), "na_tz": na_tables(inp["na_rel_bias"][0]),
        "na_kc": np.ascontiguousarray(inp["cache_na_k"][core, 0]), "na_vc": np.ascontiguousarray(inp["cache_na_v"][core, 0]),
        **mla_inputs(core, inp), **dn_inputs(core, inp), **rw_inputs(core, inp),
    }


def rw_inputs(core, inp):
    f = np.float32
    pv = np.zeros((128, 120), f)
    pv[:, 0:48] = inp["rw_mu"][0].reshape(6, 8, 128).transpose(2, 0, 1).reshape(128, 48)
    pv[:, 48:64] = inp["rw_w0"][0].reshape(2, 8, 128).transpose(2, 0, 1).reshape(128, 16)
    pv[:, 64:80] = inp["rw_a0"][0].reshape(2, 8, 128).transpose(2, 0, 1).reshape(128, 16)
    pv[:, 80:88] = inp["rw_k_k"][0].reshape(8, 128).T
    pv[:, 88:96] = inp["rw_k_a"][0].reshape(8, 128).T
    pv[:, 96:104] = inp["rw_r_k"][0].reshape(8, 128).T
    pv[:, 104:112] = inp["rw_ln_g"][0].reshape(8, 128).T
    pv[:, 112:120] = inp["rw_ln_b"][0].reshape(8, 128).T
    cat = lambda w: np.ascontiguousarray(np.concatenate([w[0], w[1]], 1), f)
    l2 = np.stack([inp["rw_w2"][0].reshape(128, D), inp["rw_a2"][0].reshape(128, D), inp["rw_g2"][0]], 1)
    z0 = inp["state_rwkv"][core, 0].transpose(0, 1, 3, 2).reshape(2, 8, 128, 64)
    return {
        "rw_pv": pv, "rw_wrkv": inp["rw_w_rkv"][0], "rw_w1": cat(inp["rw_w1"][0]), "rw_a1": cat(inp["rw_a1"][0]),
        "rw_g1": inp["rw_g1"][0], "rw_l2": np.ascontiguousarray(l2, f), "rw_wout": inp["rw_w_out"][0],
        "rw_z0": np.ascontiguousarray(z0, f), "dn_mk": chunk_masks(),
    }


def dn_inputs(core, inp):
    f = np.float32
    wg = inp["dn_w_gate"][0].reshape(D, 2, 2, 8).transpose(0, 2, 1, 3).reshape(D, 32)
    vec = np.concatenate([inp["dn_a_log"][0].reshape(16), inp["dn_dt_bias"][0].reshape(16)])
    cw = inp["dn_conv"][0].reshape(3, 3, 8, 128).transpose(3, 2, 1, 0).reshape(128, 72)
    return {
        "dn_w_in": inp["dn_w_in"][0], "dn_wg": np.ascontiguousarray(wg, f),
        "dn_vec": np.ascontiguousarray(np.tile(vec[None], (128, 1)), f), "dn_cw": np.ascontiguousarray(cw, f),
        "dn_gout": np.ascontiguousarray(inp["dn_out_norm"][0].reshape(128, 1), f), "dn_mk": chunk_masks(),
        "dn_w_out": inp["dn_w_out"][0], "dn_s0": np.ascontiguousarray(inp["state_dn"][core, 0]),
    }


def mla_inputs(core, inp):
    f = np.float32
    wa = inp["mla_w_a"][0]
    wa_pad = np.concatenate([wa[:, :640], np.zeros((D, 64), f), wa[:, 640:672]], 1)
    wkv = inp["mla_w_kv_b"][0].reshape(256, 16, 128)
    gn = np.zeros((128, 8), f)
    gn[:, 0:3] = inp["mla_q_a_norm"][0].reshape(3, 128).T
    gn[:, 3:5] = inp["mla_kv_a_norm"][0].reshape(2, 128).T
    gn[:96, 5] = inp["mla_q_norm"][0]
    gn[:96, 6] = inp["mla_k_norm"][0]
    cos, sin, R = mla_consts()
    kpec = np.concatenate([np.zeros((256, 64), f), inp["cache_mla_kpe"][core, 0]], 1)
    return {
        "ml_wa": np.ascontiguousarray(wa_pad, f), "ml_wqb": inp["mla_w_q_b"][0],
        "ml_wk": np.ascontiguousarray(wkv[:, :, :64].reshape(256, 1024)), "ml_wv": np.ascontiguousarray(wkv[:, :, 64:].reshape(256, 1024)),
        "ml_wout": inp["mla_w_out"][0], "ml_gn": gn, "ml_cos": cos, "ml_sin": sin, "ml_R": R,
        "ml_ckvc": np.ascontiguousarray(inp["cache_mla_ckv"][core, 0]), "ml_kpec": np.ascontiguousarray(kpec, f),
    }


def kernel(**inp):
    inp = {k: np.asarray(v) for k, v in inp.items()}
    nc = build()
    in_maps = []
    for i in range(8):
        im = make_inputs(i, inp)
        in_maps.append({k: v for k, v in im.items() if k in nc.in_names})
    res = run_bass_kernel_spmd(nc, in_maps, core_ids=list(range(8)))
    r = res.results
    f = np.float32
    yp = np.stack([r[i // 2]["y"][(i % 2) * 256:(i % 2 + 1) * 256] for i in range(16)], 0).astype(f)
    ys = np.stack([r[i]["y"][512:] for i in range(8)], 0).astype(f)
    cat = lambda k: np.concatenate([np.asarray(r[i][k]) for i in range(8)], 0).astype(f)
    st_dn = cat("dn_st_out")[:, None]
    na_k = cat("na_ko")[:, None]
    na_v = cat("na_vo")[:, None]
    ckv = cat("ckv_out")[:, None]
    kpe = cat("kpe_out")[:, None]
    st_rw = cat("rw_st_out")[:, None]
    return yp, ys, st_dn, na_k, na_v, ckv, kpe, st_rw
```
